# Optimizing a Trainium2 kernel written in Bass

```python
import math
import jax, jax.numpy as jnp
from jax import lax
import numpy as np

D_MODEL = 1024
BATCH = 8
SEQ = 2048
DEPTH = 1
DEC_BATCH = 128
DEC_SEQ = 8
PAST_LEN = 16384
PAGE_SIZE = 128

N_META = 16
CHUNK = 64
CONV_W = 4
RMS_EPS = 1e-6

GDN_HEADS = 4
GDN_DK = 128
GDN_DV = 128
GDN_WIDTH = GDN_HEADS * GDN_DV
GDN_CONV_DIM = GDN_HEADS * (2 * GDN_DK + GDN_DV)

MLSTM_HEADS = 4
MLSTM_DK = 128
MLSTM_DV = 128
MLSTM_WIDTH = MLSTM_HEADS * MLSTM_DV
MLSTM_CONV_DIM = 2 * MLSTM_HEADS * MLSTM_DK

IN_SPLITS = (GDN_HEADS * GDN_DK, GDN_HEADS * GDN_DK, GDN_HEADS * GDN_DV, GDN_HEADS, GDN_HEADS, GDN_HEADS * GDN_DV,
             MLSTM_HEADS * MLSTM_DK, MLSTM_HEADS * MLSTM_DK, MLSTM_HEADS * MLSTM_DV, MLSTM_HEADS, MLSTM_HEADS, MLSTM_HEADS * MLSTM_DV,
             D_MODEL, D_MODEL)
N_IN = 2 * (GDN_HEADS * (2 * GDN_DK + 2 * GDN_DV + 2)) + 2 * D_MODEL

PEER_HEADS = 8
N_KEYS = 128
N_EXPERTS = N_KEYS * N_KEYS
PEER_TOPK = 16
PEER_DQ = 256
KEY_HALF = PEER_DQ // 2
PEER_BLOCK = 128

kernel_name = 'hybrid_gdn_mlstm_peer_step'


def rms_norm(x, gain):
    xf = x.astype(jnp.float32)
    y = xf * lax.rsqrt(jnp.mean(xf * xf, axis=-1, keepdims=True) + RMS_EPS)
    return (y * gain.astype(jnp.float32)).astype(x.dtype)


def l2norm(x):
    return x * lax.rsqrt(jnp.sum(x * x, axis=-1, keepdims=True) + RMS_EPS)


def split_cols(t, sizes):
    offs = np.cumsum(np.array(sizes))[:-1].tolist()
    return jnp.split(t, offs, axis=-1)


def heads(t, n):
    b, l, _ = t.shape
    return t.reshape(b, l, n, -1).transpose(0, 2, 1, 3).astype(jnp.float32)


def causal_conv(ext, w):
    L = ext.shape[1] - (CONV_W - 1)
    y = ext[:, 0:L] * w[0]
    for j in range(1, CONV_W):
        y = y + ext[:, j:j + L] * w[j]
    return y


def gdn_chunked(q, k, v, g, beta, S0, chunk):
    B, H, L, _ = q.shape
    nc = L // chunk

    def blocks(t):
        return t.reshape(B, H, nc, chunk, *t.shape[3:])

    q, k, v, g, beta = blocks(q), blocks(k), blocks(v), blocks(g), blocks(beta)
    G = jnp.cumsum(g, axis=-1)
    incl = jnp.tril(jnp.ones((chunk, chunk), dtype=bool))
    strict = jnp.tril(jnp.ones((chunk, chunk), dtype=bool), k=-1)
    decay = jnp.exp(jnp.where(incl, G[..., :, None] - G[..., None, :], -jnp.inf))
    kk = jnp.einsum('bhnck,bhnsk->bhncs', k, k)
    lhs = jnp.where(strict, beta[..., :, None] * decay * kk, 0.0) + jnp.eye(chunk, dtype=jnp.float32)

    def solve(rhs):
        return lax.linalg.triangular_solve(lhs, rhs, left_side=True, lower=True, unit_diagonal=True)

    u = solve(beta[..., None] * v)
    w = solve((beta * jnp.exp(G))[..., None] * k)
    qk = jnp.einsum('bhnck,bhnsk->bhncs', q, k) * decay
    q_dec = q * jnp.exp(G)[..., None]
    k_end = k * jnp.exp(G[..., -1:] - G)[..., None]
    g_end = jnp.exp(G[..., -1])
    xs = tuple(jnp.moveaxis(t, 2, 0) for t in (u, w, qk, q_dec, k_end, g_end))

    def step(S, inp):
        u_c, w_c, qk_c, qd_c, ke_c, ge_c = inp
        delta = u_c - jnp.einsum('bhck,bhvk->bhcv', w_c, S)
        o = jnp.einsum('bhck,bhvk->bhcv', qd_c, S) + jnp.einsum('bhcs,bhsv->bhcv', qk_c, delta)
        S = ge_c[..., None, None] * S + jnp.einsum('bhsv,bhsk->bhvk', delta, ke_c)
        return S, o

    S, o = lax.scan(step, S0, xs)
    return jnp.moveaxis(o, 0, 2).reshape(B, H, L, -1), S


def mlstm_chunked(q, k, v, logi, logf, C0, n0, m0, chunk):
    B, H, L, _ = q.shape
    nc = L // chunk

    def blocks(t):
        return jnp.moveaxis(t.reshape(B, H, nc, chunk, *t.shape[3:]), 2, 0)

    q, k, v, logi, logf = blocks(q), blocks(k), blocks(v), blocks(logi), blocks(logf)
    F = jnp.cumsum(logf, axis=-1)
    incl = jnp.tril(jnp.ones((chunk, chunk), dtype=bool))
    Dmat = jnp.where(incl, F[..., :, None] - F[..., None, :] + logi[..., None, :], -jnp.inf)
    Dmax = jnp.max(Dmat, axis=-1)
    qk = jnp.einsum('nbhck,nbhsk->nbhcs', q, k)
    F_end = F[..., -1]
    E = F_end[..., None] - F + logi
    E_max = jnp.max(E, axis=-1)

    def step(carry, inp):
        Cs, ns, ms = carry
        q_c, k_c, v_c, F_c, D_c, Dmx, qk_c, Fe, E_c, Emx = inp
        m_inter = F_c + ms[..., None]
        m_t = jnp.maximum(m_inter, Dmx)
        w_intra = jnp.exp(D_c - m_t[..., None]) * qk_c
        w_inter = jnp.exp(m_inter - m_t)
        num = w_inter[..., None] * jnp.einsum('bhck,bhvk->bhcv', q_c, Cs) + jnp.einsum('bhcs,bhsv->bhcv', w_intra, v_c)
        den = w_inter * jnp.einsum('bhck,bhk->bhc', q_c, ns) + jnp.sum(w_intra, axis=-1)
        h = num / jnp.maximum(jnp.abs(den), jnp.exp(-m_t))[..., None]
        m_new = jnp.maximum(Fe + ms, Emx)
        dec = jnp.exp(Fe + ms - m_new)
        wk = jnp.exp(E_c - m_new[..., None])
        Cs = dec[..., None, None] * Cs + jnp.einsum('bhs,bhsv,bhsk->bhvk', wk, v_c, k_c)
        ns = dec[..., None] * ns + jnp.einsum('bhs,bhsk->bhk', wk, k_c)
        return (Cs, ns, m_new), h

    (C, n, m), h = lax.scan(step, (C0, n0, m0), (q, k, v, F, Dmat, Dmax, qk, F_end, E, E_max))
    return jnp.moveaxis(h, 0, 2).reshape(B, H, L, -1), C, n, m


def token_mixers(xn, S0, buf_gdn, C0, n0, m0, buf_mlstm, w_in, conv_gdn, gdn_a_log, gdn_dt_bias, gdn_out_norm,
                 conv_mlstm, mlstm_i_bias, mlstm_f_bias, mlstm_out_norm, w_branch, w_out, n_pad, chunk):
    B, L, _ = xn.shape
    dt = xn.dtype
    f32 = jnp.float32
    proj = jnp.einsum('bld,de->ble', xn, w_in)
    qa, ka, va, a, b, za, qb, kb, vb, ib, fb, ob, ga, gb = split_cols(proj, IN_SPLITS)

    def pad_front(t, value=0.0):
        return jnp.pad(t, [(0, 0), (0, 0), (n_pad, 0)] + [(0, 0)] * (t.ndim - 3), constant_values=value)

    ext_a = jnp.concatenate([buf_gdn.astype(dt), jnp.concatenate([qa, ka, va], axis=-1)], axis=1)
    new_buf_gdn = ext_a[:, L:]
    qkv_a = jax.nn.silu(causal_conv(ext_a, conv_gdn))
    qa, ka, va = jnp.split(qkv_a, [GDN_HEADS * GDN_DK, 2 * GDN_HEADS * GDN_DK], axis=-1)
    qa = l2norm(heads(qa, GDN_HEADS)) * (GDN_DK ** -0.5)
    ka = l2norm(heads(ka, GDN_HEADS))
    va = heads(va, GDN_HEADS)
    g = -jnp.exp(gdn_a_log.astype(f32)) * jax.nn.softplus(a.astype(f32) + gdn_dt_bias.astype(f32))
    beta = jax.nn.sigmoid(b.astype(f32))
    o_a, S_new = gdn_chunked(pad_front(qa), pad_front(ka), pad_front(va), pad_front(g.transpose(0, 2, 1)),
                             pad_front(beta.transpose(0, 2, 1)), S0.astype(f32), chunk)
    o_a = o_a[:, :, n_pad:].transpose(0, 2, 1, 3)
    o_a = rms_norm(o_a, gdn_out_norm) * jax.nn.silu(za.astype(f32)).reshape(B, L, GDN_HEADS, GDN_DV)
    o_a = o_a.reshape(B, L, GDN_WIDTH).astype(dt)

    ext_b = jnp.concatenate([buf_mlstm.astype(dt), jnp.concatenate([qb, kb], axis=-1)], axis=1)
    new_buf_mlstm = ext_b[:, L:]
    qk_b = jax.nn.silu(causal_conv(ext_b, conv_mlstm))
    qb, kb = jnp.split(qk_b, [MLSTM_HEADS * MLSTM_DK], axis=-1)
    qb = heads(qb, MLSTM_HEADS)
    kb = heads(kb, MLSTM_HEADS) * (MLSTM_DK ** -0.5)
    vb = heads(vb, MLSTM_HEADS)
    logi = (ib.astype(f32) + mlstm_i_bias.astype(f32)).transpose(0, 2, 1)
    logf = jax.nn.log_sigmoid(fb.astype(f32) + mlstm_f_bias.astype(f32)).transpose(0, 2, 1)
    o_b, C_new, n_new, m_new = mlstm_chunked(pad_front(qb), pad_front(kb), pad_front(vb), pad_front(logi, -jnp.inf),
                                             pad_front(logf), C0.astype(f32), n0.astype(f32), m0.astype(f32), chunk)
    o_b = o_b[:, :, n_pad:].transpose(0, 2, 1, 3)
    o_b = rms_norm(o_b, mlstm_out_norm).reshape(B, L, MLSTM_WIDTH) * jax.nn.sigmoid(ob.astype(f32))
    o_b = o_b.astype(dt)

    y_a = jnp.einsum('blc,cd->bld', o_a, w_branch[:GDN_WIDTH])
    y_b = jnp.einsum('blc,cd->bld', o_b, w_branch[GDN_WIDTH:])
    merged = jax.nn.sigmoid(ga) * y_a + jax.nn.sigmoid(gb) * y_b
    out = jnp.einsum('bld,de->ble', merged, w_out)
    states = (S_new.astype(S0.dtype), new_buf_gdn.astype(buf_gdn.dtype), C_new.astype(C0.dtype),
              n_new.astype(n0.dtype), m_new.astype(m0.dtype), new_buf_mlstm.astype(buf_mlstm.dtype))
    return out, states


def peer(xn, w_query, sub_keys, expert_u, expert_v):
    B, L, D = xn.shape
    T = B * L
    n_pad = (-T) % PEER_BLOCK
    xt = jnp.pad(xn.reshape(T, D), ((0, n_pad), (0, 0))).reshape(-1, PEER_BLOCK, D)

    def block(xb):
        qh = jnp.einsum('td,de->te', xb, w_query).reshape(PEER_BLOCK, PEER_HEADS, 2, KEY_HALF)
        s = jnp.einsum('thpd,hpnd->thpn', qh, sub_keys).astype(jnp.float32)
        sv, si = lax.top_k(s, PEER_TOPK)
        cand = (sv[:, :, 0, :, None] + sv[:, :, 1, None, :]).reshape(PEER_BLOCK, PEER_HEADS, PEER_TOPK * PEER_TOPK)
        cidx = (si[:, :, 0, :, None] * N_KEYS + si[:, :, 1, None, :]).reshape(PEER_BLOCK, PEER_HEADS, PEER_TOPK * PEER_TOPK)
        top_s, pos = lax.top_k(cand, PEER_TOPK)
        eidx = jnp.take_along_axis(cidx, pos, axis=-1)
        gate = jax.nn.softmax(top_s, axis=-1).astype(xb.dtype)
        act = jax.nn.gelu(jnp.einsum('thkd,td->thk', expert_u[eidx], xb), approximate=False)
        return jnp.einsum('thk,thkd->td', gate * act, expert_v[eidx])

    y = lax.map(block, xt).reshape(-1, D)[:T]
    return y.reshape(B, L, D)


def run_trunk(h, st_gdn_S, st_gdn_conv, st_C, st_n, st_m, st_mconv, norm_mix, w_in, conv_gdn, gdn_a_log, gdn_dt_bias,
              gdn_out_norm, conv_mlstm, mlstm_i_bias, mlstm_f_bias, mlstm_out_norm, w_branch, w_out, norm_ffn,
              peer_w_query, peer_sub_keys, peer_u, peer_v, norm_final, n_pad, chunk, n_drop):
    new = [[] for _ in range(6)]
    for l in range(DEPTH):
        xn = rms_norm(h, norm_mix[l])
        out, states = token_mixers(xn, st_gdn_S[l], st_gdn_conv[l], st_C[l], st_n[l], st_m[l], st_mconv[l],
                                   w_in[l], conv_gdn[l], gdn_a_log[l], gdn_dt_bias[l], gdn_out_norm[l],
                                   conv_mlstm[l], mlstm_i_bias[l], mlstm_f_bias[l], mlstm_out_norm[l],
                                   w_branch[l], w_out[l], n_pad, chunk)
        for lst, s in zip(new, states):
            lst.append(s)
        h = h + out
        if l == DEPTH - 1:
            h = h[:, n_drop:]
        h = h + peer(rms_norm(h, norm_ffn[l]), peer_w_query[l], peer_sub_keys[l], peer_u[l], peer_v[l])
    y = rms_norm(h, norm_final)
    return y, [jnp.stack(lst) for lst in new]


def setup_inputs(seed: int = 0) -> dict:
    key = jax.random.key(seed)
    ks = jax.random.split(key, 32)
    f32 = jnp.float32

    def nrm(k, shape, s):
        return jax.random.normal(k, shape, f32) * s

    dt_init = jnp.exp(jax.random.uniform(ks[13], (DEPTH, GDN_HEADS), f32, math.log(1e-3), math.log(1e-1)))
    return {
        'x_prompt': nrm(ks[0], (BATCH, SEQ, D_MODEL), 1.0),
        'x_sample': nrm(ks[1], (DEC_BATCH, DEC_SEQ, D_MODEL), 1.0),
        'state_gdn_S': nrm(ks[2], (DEPTH, DEC_BATCH, GDN_HEADS, GDN_DV, GDN_DK), 0.1),
        'state_gdn_conv': nrm(ks[3], (DEPTH, DEC_BATCH, CONV_W - 1, GDN_CONV_DIM), 1.0),
        'state_mlstm_C': nrm(ks[4], (DEPTH, DEC_BATCH, MLSTM_HEADS, MLSTM_DV, MLSTM_DK), 0.1),
        'state_mlstm_n': nrm(ks[5], (DEPTH, DEC_BATCH, MLSTM_HEADS, MLSTM_DK), 0.3),
        'state_mlstm_m': nrm(ks[6], (DEPTH, DEC_BATCH, MLSTM_HEADS), 1.0),
        'state_mlstm_conv': nrm(ks[7], (DEPTH, DEC_BATCH, CONV_W - 1, MLSTM_CONV_DIM), 1.0),
        'meta_tokens': nrm(ks[8], (N_META, D_MODEL), 1.0),
        'norm_mix': 1.0 + nrm(ks[9], (DEPTH, D_MODEL), 0.02),
        'w_in': nrm(ks[10], (DEPTH, D_MODEL, N_IN), D_MODEL ** -0.5),
        'conv_gdn': nrm(ks[11], (DEPTH, CONV_W, GDN_CONV_DIM), CONV_W ** -0.5),
        'gdn_a_log': jnp.log(jax.random.uniform(ks[12], (DEPTH, GDN_HEADS), f32, 1.0, 16.0)),
        'gdn_dt_bias': dt_init + jnp.log(-jnp.expm1(-dt_init)),
        'gdn_out_norm': 1.0 + nrm(ks[14], (DEPTH, GDN_DV), 0.02),
        'conv_mlstm': nrm(ks[15], (DEPTH, CONV_W, MLSTM_CONV_DIM), CONV_W ** -0.5),
        'mlstm_i_bias': nrm(ks[16], (DEPTH, MLSTM_HEADS), 0.1),
        'mlstm_f_bias': jnp.linspace(3.0, 6.0, MLSTM_HEADS, dtype=f32)[None] + nrm(ks[17], (DEPTH, MLSTM_HEADS), 0.1),
        'mlstm_out_norm': 1.0 + nrm(ks[18], (DEPTH, MLSTM_HEADS, MLSTM_DV), 0.02),
        'w_branch': nrm(ks[19], (DEPTH, GDN_WIDTH + MLSTM_WIDTH, D_MODEL), GDN_WIDTH ** -0.5),
        'w_out': nrm(ks[20], (DEPTH, D_MODEL, D_MODEL), D_MODEL ** -0.5),
        'norm_ffn': 1.0 + nrm(ks[21], (DEPTH, D_MODEL), 0.02),
        'peer_w_query': nrm(ks[22], (DEPTH, D_MODEL, PEER_HEADS * PEER_DQ), D_MODEL ** -0.5),
        'peer_sub_keys': nrm(ks[23], (DEPTH, PEER_HEADS, 2, N_KEYS, KEY_HALF), KEY_HALF ** -0.5),
        'peer_u': nrm(ks[24], (DEPTH, N_EXPERTS, D_MODEL), D_MODEL ** -0.5),
        'peer_v': nrm(ks[25], (DEPTH, N_EXPERTS, D_MODEL), 0.1),
        'norm_final': 1.0 + nrm(ks[26], (D_MODEL,), 0.02),
    }


def reference(x_prompt, x_sample, state_gdn_S, state_gdn_conv, state_mlstm_C, state_mlstm_n, state_mlstm_m,
              state_mlstm_conv, meta_tokens, norm_mix, w_in, conv_gdn, gdn_a_log, gdn_dt_bias, gdn_out_norm,
              conv_mlstm, mlstm_i_bias, mlstm_f_bias, mlstm_out_norm, w_branch, w_out, norm_ffn, peer_w_query,
              peer_sub_keys, peer_u, peer_v, norm_final):
    weights = (norm_mix, w_in, conv_gdn, gdn_a_log, gdn_dt_bias, gdn_out_norm, conv_mlstm, mlstm_i_bias,
               mlstm_f_bias, mlstm_out_norm, w_branch, w_out, norm_ffn, peer_w_query, peer_sub_keys, peer_u,
               peer_v, norm_final)
    f32 = jnp.float32
    B = x_prompt.shape[0]
    dt = x_prompt.dtype
    sdt = state_gdn_S.dtype
    h_p = jnp.concatenate([jnp.broadcast_to(meta_tokens.astype(dt)[None], (B, N_META, D_MODEL)), x_prompt], axis=1)
    y_prompt, p_states = run_trunk(
        h_p,
        jnp.zeros((DEPTH, B, GDN_HEADS, GDN_DV, GDN_DK), sdt),
        jnp.zeros((DEPTH, B, CONV_W - 1, GDN_CONV_DIM), state_gdn_conv.dtype),
        jnp.zeros((DEPTH, B, MLSTM_HEADS, MLSTM_DV, MLSTM_DK), state_mlstm_C.dtype),
        jnp.zeros((DEPTH, B, MLSTM_HEADS, MLSTM_DK), state_mlstm_n.dtype),
        jnp.zeros((DEPTH, B, MLSTM_HEADS), state_mlstm_m.dtype),
        jnp.zeros((DEPTH, B, CONV_W - 1, MLSTM_CONV_DIM), state_mlstm_conv.dtype),
        *weights, n_pad=CHUNK - N_META, chunk=CHUNK, n_drop=N_META)
    y_sample, s_states = run_trunk(
        x_sample, state_gdn_S, state_gdn_conv, state_mlstm_C, state_mlstm_n, state_mlstm_m, state_mlstm_conv,
        *weights, n_pad=0, chunk=x_sample.shape[1], n_drop=0)
    p_S, p_conv, p_C, p_n, p_m, p_mconv = p_states
    s_S, s_conv, s_C, s_n, s_m, s_mconv = s_states
    return (y_prompt, y_sample, p_S, p_conv, p_C, p_n, p_m, p_mconv, s_S, s_conv, s_C, s_n, s_m, s_mconv)
```

```python
import os
import numpy as np
from contextlib import ExitStack
import concourse.bass as bass
import concourse.mybir as mybir
from concourse.bass_utils import run_bass_kernel_spmd

F32 = mybir.dt.float32
BF16 = mybir.dt.bfloat16
I32 = mybir.dt.int32
U32 = mybir.dt.uint32
AF = mybir.ActivationFunctionType
ALU = mybir.AluOpType
AX = mybir.AxisListType


class Res:
    __slots__ = ("name", "writer", "readers", "dsem", "dcnt", "dim")

    def __init__(self, name):
        self.name = name
        self.writer = None
        self.readers = {}
        self.dsem = None
        self.dcnt = 0
        self.dim = None


class Buf:
    def __init__(self, t, name):
        self.t = t
        self.r = Res(name)

    def __getitem__(self, k):
        return self.t[k]


class Sched:
    ENGS = ("pe", "act", "dve", "pool", "sp")

    def __init__(self, nc, es):
        self.nc = nc
        self.es = es
        self.sems = {}
        for e in self.ENGS[:4]:
            self.sems[e] = es.enter_context(nc.semaphore("s_" + e))
        self.cnt = {e: 0 for e in self.ENGS}
        self.seen = {e: {} for e in self.ENGS}
        self.vc = {}
        self.prog = {e: [] for e in self.ENGS}
        self.nres = 0
        self.out_deps = []
        self.dma_counts = {}

    def _need(self, eng, dep, waits):
        if dep is None:
            return
        dim, c = dep
        se = self.seen[eng]
        if se.get(dim, 0) >= c:
            return
        if dim == eng and eng == "pe":
            return
        if waits.get(dim, 0) < c:
            waits[dim] = c
        for k, v in self.vc[(dim, c)].items():
            if se.get(k, 0) < v:
                se[k] = v

    def _deps(self, eng, reads, writes):
        waits = {}
        for b in reads:
            self._need(eng, b.r.writer, waits)
        for b in writes:
            self._need(eng, b.r.writer, waits)
            for d, c in list(b.r.readers.items()):
                self._need(eng, (d, c), waits)
        return waits

    def op(self, eng, fn, reads=(), writes=()):
        waits = self._deps(eng, reads, writes)
        n = self.cnt[eng] + 1
        self.cnt[eng] = n
        snap = dict(self.seen[eng])
        snap[eng] = n
        self.vc[(eng, n)] = snap
        for b in reads:
            if b.r.readers.get(eng, 0) < n:
                b.r.readers[eng] = n
        for b in writes:
            b.r.writer = (eng, n)
            b.r.readers = {}
        self.prog[eng].append((list(waits.items()), fn, eng, 1))

    def dma(self, q, fn, reads=(), writes=(), semres=None, is_output=False):
        waits = self._deps(q, reads, writes)
        R = semres.r
        if R.dsem is None:
            self.nres += 1
            R.dim = "D%d_%s" % (self.nres, R.name)
            R.dsem = self.es.enter_context(self.nc.semaphore(R.dim))
            self.sems[R.dim] = R.dsem
        R.dcnt += 16
        c = R.dcnt
        dim = R.dim
        snap = dict(self.seen[q])
        snap[dim] = c
        self.vc[(dim, c)] = snap
        self.dma_counts[dim] = c
        for b in reads:
            if b.r.readers.get(dim, 0) < c:
                b.r.readers[dim] = c
        for b in writes:
            b.r.writer = (dim, c)
            b.r.readers = {}
        self.prog[q].append((list(waits.items()), fn, dim, 16))
        if is_output:
            self.out_deps.append((dim, c))
        return (dim, c)

    def barrier(self):
        targets = [(e, self.cnt[e]) for e in self.ENGS[:4] if self.cnt[e] > 0]
        targets += list(self.dma_counts.items())
        for eng in self.ENGS:
            waits = {}
            for dep in targets:
                self._need(eng, dep, waits)
            self.prog[eng].append((list(waits.items()), None, None, 0))

    def emit(self):
        nc = self.nc
        fw = {}
        for d, c in self.out_deps:
            if fw.get(d, 0) < c:
                fw[d] = c
        self.prog["sp"].append((list(fw.items()), None, None, 0))
        sems = self.sems
        prog = self.prog

        def mk(name):
            def body(e):
                for waits, fn, dim, inc in prog[name]:
                    for d, c in waits:
                        e.wait_ge(sems[d], c)
                    if fn is not None:
                        fn(e).then_inc(sems[dim], inc)
            return body

        with nc.Block() as block:
            block.tensor(mk("pe"))
            block.scalar(mk("act"))
            block.vector(mk("dve"))
            block.gpsimd(mk("pool"))
            block.sync(mk("sp"))


class Builder:
    def __init__(self, nc, es):
        self.nc = nc
        self.es = es
        self.S = Sched(nc, es)
        self.aes = es

    def sb(self, name, shape, dt=F32):
        t = self.aes.enter_context(self.nc.sbuf_tensor("sb_" + name, list(shape), dt))
        return Buf(t, name)

    def ps(self, name, shape=(128, 512), dt=F32):
        t = self.es.enter_context(self.nc.psum_tensor("ps_" + name, list(shape), dt))
        return Buf(t, name)

    def dram_in(self, name, shape, dt=F32):
        return self.nc.dram_tensor(name, list(shape), dt, kind="ExternalInput")

    def dram_out(self, name, shape, dt=F32):
        return self.nc.dram_tensor(name, list(shape), dt, kind="ExternalOutput")

    def mm(self, out, lhsT, rhs, start=True, stop=True, reads=(), writes=()):
        self.S.op("pe", lambda e: e.matmul(out, lhsT, rhs, start=start, stop=stop), reads, writes)

    def tr(self, out, in_, ident, reads=(), writes=()):
        self.S.op("pe", lambda e: e.transpose(out, in_, ident), reads, writes)

    def act(self, out, in_, func, reads=(), writes=(), **kw):
        self.S.op("act", lambda e: e.activation(out, in_, func, **kw), reads, writes)

    def tt(self, eng, out, in0, in1, op, reads=(), writes=()):
        self.S.op(eng, lambda e: e.tensor_tensor(out, in0, in1, op), reads, writes)

    def ts(self, eng, out, in0, s1, s2, op0, op1=None, reads=(), writes=(), accum_out=None):
        if op1 is None:
            self.S.op(eng, lambda e: e.tensor_scalar(out, in0, s1, None, op0), reads, writes)
        elif accum_out is None:
            self.S.op(eng, lambda e: e.tensor_scalar(out, in0, s1, s2, op0, op1), reads, writes)
        else:
            self.S.op(eng, lambda e: e.tensor_scalar(out, in0, s1, s2, op0, op1, accum_out=accum_out), reads, writes)

    def stt(self, out, in0, scalar, in1, op0, op1, reads=(), writes=(), accum_out=None):
        if accum_out is None:
            self.S.op("dve", lambda e: e.scalar_tensor_tensor(out, in0, scalar, in1, op0, op1), reads, writes)
        else:
            self.S.op("dve", lambda e: e.scalar_tensor_tensor(out, in0, scalar, in1, op0, op1, accum_out=accum_out), reads, writes)

    def cp(self, eng, out, in_, reads=(), writes=()):
        if eng == "act":
            self.S.op("act", lambda e: e.copy(out, in_), reads, writes)
        else:
            self.S.op(eng, lambda e: e.tensor_copy(out, in_), reads, writes)

    def memset(self, eng, ap, val, writes=()):
        self.S.op(eng, lambda e: e.memset(ap, val), (), writes)

    def load(self, out, in_, buf, q="sp", reads=()):
        self.S.dma(q, lambda e: e.dma_start(out=out, in_=in_), reads=reads, writes=(buf,), semres=buf)

    def store(self, out, in_, buf, q="sp", is_output=True):
        self.S.dma(q, lambda e: e.dma_start(out=out, in_=in_), reads=(buf,), writes=(), semres=buf, is_output=is_output)


def sap(t, p0, pn, f0, dims):
    base = t[:]
    F = base.ap[0][0]
    return bass.AP(t, p0 * F + f0, [[F, pn]] + [list(d) for d in dims])

T_P = 2064
T_S = 128
T = T_P + T_S
NB = [(0, 512), (512, 512), (1024, 512), (1536, 512), (2048, 144)]
EPS = 1e-6
NEG = -1.0e30


def _maskset(nseg, seglen):
    r = np.arange(128)
    seg = r // seglen
    same = seg[:, None] == seg[None, :]
    le = r[:, None] <= r[None, :]
    lt = r[:, None] < r[None, :]
    MinclT = (same & le).astype(np.float32)
    SegB = same.astype(np.float32)
    MstrictNeg = -((same & lt.T).astype(np.float32))
    MstrictTNeg = -((same & lt).astype(np.float32))
    MaddIncl = np.where(same & le.T, 0.0, NEG).astype(np.float32)
    MaddSeg = np.where(same, 0.0, NEG).astype(np.float32)
    segrows = (seg[:, None] == np.arange(nseg)[None, :]).astype(np.float32)
    segfirst = segrows * ((r % seglen) == 0)[:, None].astype(np.float32)
    segT = np.broadcast_to(segrows.T[None], (128, nseg, 128)).reshape(128, nseg * 128)
    return np.ascontiguousarray(np.concatenate(
        [MinclT, SegB, MstrictNeg, MstrictTNeg, MaddIncl, MaddSeg, segrows, segfirst, segT], axis=1).astype(np.float32))


class MaskSet:
    def __init__(self, buf, nseg):
        self.buf = buf
        self.nseg = nseg
        t = buf.t
        self.MinclT = t[:, 0:128]
        self.SegB = t[:, 128:256]
        self.MstrictNeg = t[:, 256:384]
        self.MstrictTNeg = t[:, 384:512]
        self.MaddIncl = t[:, 512:640]
        self.MaddSeg = t[:, 640:768]
        self.segrows = t[:, 768:768 + nseg]
        self.segfirst = t[:, 768 + nseg:768 + 2 * nseg]
        o = 768 + 2 * nseg
        self.segT = sap(t, 0, 128, o, [[128, nseg], [1, 128]])
        self.segrows_b = sap(t, 0, 128, 768, [[1, nseg], [0, 128]])


KCUT = int(os.environ.get("KCUT", "99"))


class _Stop(Exception):
    pass


def build_program(stage=9, ntiles=33):
    nc = bass.Bass("TRN2", target_bir_lowering=False)
    es = ExitStack()
    with es:
        B = Builder(nc, es)
        S = B.S
        d = {}

        def din(name, shape, dt=F32):
            d[name] = B.dram_in(name, shape, dt).ap()
            return d[name]

        def dout(name, shape, dt=F32):
            d[name] = B.dram_out(name, shape, dt).ap()
            return d[name]

        din("xall", [T, 1024]); din("w_in", [1024, 6160]); din("w_branch", [1024, 1024]); din("w_out", [1024, 1024])
        din("nffn_bc", [128, 1024]); din("nfin_bc", [128, 1024]); din("wq", [1024, 2048]); din("subk", [128, 16, 128])
        din("peer_uv", [16384, 2048]); din("iota16", [128, 16])
        din("ident", [128, 128]); din("ones", [128, 128]); din("maskP", [128, 1028]); din("maskS", [128, 2848])
        din("rowmask0", [128, 2]); din("nmix", [128, 8]);
        din("convg", [128, 48]); din("convm", [128, 32]); din("rowp", [128, 24]); din("gn_g", [128, 128])
        din("gn_m", [128, 768]); din("sS", [16, 4, 128, 128]); din("sconv_g", [48, 1536]); din("sC", [16, 4, 128, 128])
        din("sn", [16, 512]); din("m0s", [128, 4]); din("sconv_m", [48, 1024])
        dout("y_p", [2048, 1024]); dout("y_s", [128, 1024]); dout("p_S", [4, 128, 128]); dout("p_conv", [3, 1536])
        dout("p_C", [4, 128, 128]); dout("p_n", [4, 128]); dout("p_m", [1, 4]); dout("p_mconv", [3, 1024])
        dout("s_S", [16, 4, 128, 128]); dout("s_conv", [48, 1536]); dout("s_C", [16, 4, 128, 128]); dout("s_n", [16, 512])
        dout("s_m", [16, 4]); dout("s_mconv", [48, 1024])
        oT_d = nc.dram_tensor("oT_d", [1024, T], BF16, kind="Internal").ap()
        oT_res = Buf(None, "oT_d")

        ident = B.sb("ident", [128, 128]); ones = B.sb("ones", [128, 128]); cst = B.sb("cst", [128, 8])
        ps = [B.ps("b%d" % i) for i in range(8)]
        xnT = B.sb("xnT", [128, 8, T], BF16)
        es_mix = ExitStack()
        B.aes = es_mix
        mP = B.sb("maskP", [128, 1028]); mS = B.sb("maskS", [128, 2848]); rm0 = B.sb("rm0", [128, 2])
        nmix = B.sb("nmix", [128, 8]); convg = B.sb("convg", [128, 48]); convm = B.sb("convm", [128, 32])
        rowp = B.sb("rowp", [128, 24]); gn_g = B.sb("gn_g", [128, 128]); gn_m = B.sb("gn_m", [128, 768])
        m0s = B.sb("m0s", [128, 4]); prm = B.sb("prm", [128, 24])
        for b_, nm in ((ident, "ident"), (ones, "ones"), (mP, "maskP"), (mS, "maskS"), (rm0, "rowmask0"), (nmix, "nmix"),
                       (convg, "convg"), (convm, "convm"), (rowp, "rowp"), (gn_g, "gn_g"), (gn_m, "gn_m"), (m0s, "m0s")):
            B.load(b_[:], d[nm], b_)
        MP = MaskSet(mP, 2)
        MS = MaskSet(mS, 16)
        B.memset("pool", cst[:, 0:1], EPS, writes=[cst])
        B.memset("pool", cst[:, 1:2], 1.0, writes=[cst])
        B.memset("pool", cst[:, 2:3], -0.5 * float(np.log(128.0)), writes=[cst])
        B.memset("pool", cst[:, 3:4], 0.0, writes=[cst])
        c_eps, c_one, c_lnq, c_zero = cst[:, 0:1], cst[:, 1:2], cst[:, 2:3], cst[:, 3:4]
        rp3 = rowp.t[:].rearrange("p (t f) -> p t f", f=4)
        pr3 = prm.t[:].rearrange("p (t f) -> p t f", f=4)
        B.act(pr3[:, :, 0:1], rp3[:, :, 0:1], AF.Exp, reads=[rowp], writes=[prm])
        B.ts("dve", pr3[:, :, 0:1], pr3[:, :, 0:1], -1.0, None, ALU.mult, reads=[prm], writes=[prm])
        B.cp("dve", pr3[:, :, 1:3], rp3[:, :, 1:3], reads=[rowp], writes=[prm])
        B.ts("dve", pr3[:, :, 3:4], rp3[:, :, 3:4], -1.0, None, ALU.mult, reads=[rowp], writes=[prm])

        uvb = nc.dram_tensor("uvb", [16384, 2048], BF16, kind="Internal").ap()
        uvb_res = Buf(None, "uvb")
        cvd = [B.sb("cvd%d" % i, [128, 1]) for i in range(4)]
        NCVD = 64
        rws = 16384 // NCVD
        for i in range(NCVD):
            S.dma("pool", (lambda i: (lambda e: e.dma_start(out=uvb[i * rws:(i + 1) * rws, :], in_=d["peer_uv"][i * rws:(i + 1) * rws, :])))(i),
                  reads=[], writes=[uvb_res], semres=cvd[i % 4])

        EX1 = B.sb("EX1", [128, 2048]); EX2 = B.sb("EX2", [128, 2048]); EX3 = B.sb("EX3", [128, 2048])
        st = [B.sb("st%d" % i, [128, 4]) for i in range(2)]

        class _V:
            def __init__(self, buf, c0):
                self.buf = buf; self.c0 = c0
            def ap(self, r0, r1, a, b):
                return self.buf.t[r0:r1, self.c0 + a:self.c0 + b]
        xtv = [_V(EX1, 0), _V(EX1, 1024)]
        xsv = [_V(EX2, 0), _V(EX2, 1024)]
        junkv = _V(EX3, 0)

        def rstd_col(stb, rows, width):
            B.act(stb[0:rows, 1:2], stb[0:rows, 0:1], AF.Ln, reads=[stb, cst], writes=[stb], scale=1.0 / width, bias=c_eps[0:rows])
            B.act(stb[0:rows, 2:3], stb[0:rows, 1:2], AF.Exp, reads=[stb], writes=[stb], scale=-0.5)

        ntile = (T + 127) // 128
        for i in range(ntile):
            t0 = i * 128
            rows = min(128, T - t0)
            sl = i % 2
            B.load(xtv[sl].ap(0, rows, 0, 1024), d["xall"][t0:t0 + rows, :], EX1)
            B.act(junkv.ap(0, rows, 0, 1024), xtv[sl].ap(0, rows, 0, 1024), AF.Square, reads=[EX1], writes=[EX3, st[sl]], accum_out=st[sl][0:rows, 0:1])
            rstd_col(st[sl], rows, 1024)
            B.act(xsv[sl].ap(0, rows, 0, 1024), xtv[sl].ap(0, rows, 0, 1024), AF.Copy, reads=[EX1, st[sl]], writes=[EX2], scale=st[sl][0:rows, 2:3])
            for c in range(8):
                pb = ps[2 * sl + c // 4]
                B.tr(pb[:, (c % 4) * 128:(c % 4) * 128 + rows], xsv[sl].ap(0, rows, c * 128, (c + 1) * 128), ident[0:rows, 0:rows],
                     reads=[EX2, ident], writes=[pb])
            for h in range(2):
                pb = ps[2 * sl + h]
                B.tt("dve", xnT[:, 4 * h:4 * h + 4, t0:t0 + rows], sap(pb.t, 0, 128, 0, [[128, 4], [1, rows]]),
                     sap(nmix.t, 0, 128, 4 * h, [[1, 4], [0, rows]]), ALU.mult, reads=[pb, nmix], writes=[xnT])

        if stage <= 1:
            S.emit()
            es_mix.close()
            return nc
        Wb = B.sb("Wb", [128, 8, 1032], BF16)
        Wst = [B.sb("Wst0", [128, 1032])]
        E = B.sb("E", [128, 2243])
        Y = B.sb("Y", [128, T])
        Rt = B.sb("Rt", [128, 512])
        CO = [B.sb("CO%d" % i, [48, 256]) for i in range(2)]
        qT = B.sb("qT", [128, 2, T]); kT = B.sb("kT", [128, 2, T]); vT = B.sb("vT", [128, 2, T])
        B.memset("pool", E[:, 0:3], 0.0, writes=[E])
        gc = B.sb("gc", [128, 48])
        D2 = B.sb("D2", [128, 256]); XX = B.sb("XX", [128, 256]); DD = B.sb("DD", [128, 256]); Xb = B.sb("Xb", [128, 128])
        t1 = B.sb("t1", [128, 128]); t2 = B.sb("t2", [128, 128]); t3 = B.sb("t3", [128, 128]); t4 = B.sb("t4", [128, 128])
        Mk = [B.sb("Mk%d" % i, [128, 128]) for i in range(2)]; Nk = [B.sb("Nk%d" % i, [128, 128]) for i in range(2)]
        IM = B.sb("IM", [128, 128]); TT = [B.sb("TT%d" % i, [128, 128]) for i in range(2)]
        qkmT = B.sb("qkmT", [128, 128]); QT = B.sb("QT", [128, 128]); KT = B.sb("KT", [128, 128]); VT = B.sb("VT", [128, 128])
        bv = B.sb("bv", [128, 128]); bgk = B.sb("bgk", [128, 128]); ke = B.sb("ke", [128, 128])
        dsb = B.sb("dsb", [128, 128]); o1 = B.sb("o1", [128, 128]); ob = B.sb("ob", [128, 128]); on = B.sb("on", [128, 128])
        ez = B.sb("ez", [128, 128]); og = B.sb("og", [128, 128]); jk2 = B.sb("jk2", [128, 128])
        obf = [B.sb("obf%d" % i, [128, 128], BF16) for i in range(2)]
        GEt = B.sb("GEt", [128, 16]); gebc = B.sb("gebc", [128, 16])
        ST = B.sb("ST", [128, 2048]); Sio = B.sb("Sio", [128, 2048])
        nT = B.sb("nT", [128, 16]); mrow = B.sb("mrow", [128, 1]); nio = B.sb("nio", [16, 512]); ntm = B.sb("ntm", [16, 128])
        msm = B.sb("msm", [128, 4])

        def G(i):
            return gc[:, i:i + 1]

        obf_ctr = [0]
        co_ctr = [0]
        wst_ctr = [0]

        def load_wpair(base, j):
            rngs = [(base + 2 * j * 128, 256, 0), (base + 512 + 2 * j * 128, 256, 256), (base + 1024 + 2 * j * 128, 256, 512),
                    (base + 1536, 8, 768), (base + 1544 + 2 * j * 128, 256, 776)]
            for c in range(8):
                ws = Wst[0]
                wst_ctr[0] += 1
                for (c0, n, o) in rngs:
                    B.load(ws[:, o:o + n], d["w_in"][c * 128:(c + 1) * 128, c0:c0 + n], ws)
                eng = ("pool", "dve")[c % 2]
                B.cp(eng, Wb[:, c, :], ws[:, :], reads=[ws], writes=[Wb])

        def project(e, evac):
            for bi, (t0, n) in enumerate(NB):
                pb = ps[bi % 4]
                for c in range(8):
                    B.mm(pb[:, 0:n], Wb[:, c, e * 128:(e + 1) * 128], xnT[:, c, t0:t0 + n], start=(c == 0), stop=(c == 7),
                         reads=[Wb, xnT], writes=[pb])
                evac(bi, pb, t0, n)

        Esamp = sap(E.t, 0, 128, 2067, [[11, 16], [1, 11]])

        def evac_to_E(bi, pb, t0, n):
            if bi < 4:
                B.cp("act", E[:, 3 + t0:3 + t0 + n], pb[:, 0:n], reads=[pb], writes=[E])
            else:
                B.cp("act", E[:, 3 + 2048:3 + 2064], pb[:, 0:16], reads=[pb], writes=[E])
                B.cp("act", sap(E.t, 0, 128, 2067 + 3, [[11, 16], [1, 8]]), sap(pb.t, 0, 128, 16, [[8, 16], [1, 8]]), reads=[pb], writes=[E])

        def conv_chunk(convw, cc, hist_cols, out_state_p, out_state_s, ch0):
            pb = ps[4]
            B.tr(pb[:, 0:48], Sio[0:48, hist_cols:hist_cols + 128], ident[0:48, 0:48], reads=[Sio, ident], writes=[pb])
            B.cp("act", sap(E.t, 0, 128, 2067, [[11, 16], [1, 3]]), sap(pb.t, 0, 128, 0, [[3, 16], [1, 3]]), reads=[pb], writes=[E])
            w = lambda jj: convw[:, cc * 4 + jj:cc * 4 + jj + 1]
            B.ts("dve", Y[:, 0:T_P], E[:, 0:T_P], w(0), None, ALU.mult, reads=[E, convw], writes=[Y])
            for jj in range(1, 4):
                B.stt(Y[:, 0:T_P], E[:, jj:jj + T_P], w(jj), Y[:, 0:T_P], ALU.mult, ALU.add, reads=[E, convw, Y], writes=[Y])
            Ys = sap(Y.t, 0, 128, T_P, [[8, 16], [1, 8]])
            B.ts("dve", Ys, sap(E.t, 0, 128, 2067, [[11, 16], [1, 8]]), w(0), None, ALU.mult, reads=[E, convw], writes=[Y])
            for jj in range(1, 4):
                B.stt(Ys, sap(E.t, 0, 128, 2067 + jj, [[11, 16], [1, 8]]), w(jj), Ys, ALU.mult, ALU.add, reads=[E, convw, Y], writes=[Y])
            co = CO[co_ctr[0] % 2]
            co_ctr[0] += 1
            pb2 = ps[5]
            B.tr(pb2[0:3, 0:128], E[:, 2064:2067], ident[:, :], reads=[E, ident], writes=[pb2])
            B.cp("pool", sap(Rt.t, 0, 128, 0, [[3, 16], [1, 3]]), sap(E.t, 0, 128, 2067 + 8, [[11, 16], [1, 3]]), reads=[E], writes=[Rt])
            B.tr(pb2[0:48, 128:256], Rt[:, 0:48], ident[:, :], reads=[Rt, ident], writes=[pb2])
            B.cp("act", co[0:3, 0:128], pb2[0:3, 0:128], reads=[pb2], writes=[co])
            B.cp("act", co[0:48, 128:256], pb2[0:48, 128:256], reads=[pb2], writes=[co])
            B.store(out_state_p[0:3, ch0:ch0 + 128], co[0:3, 0:128], co)
            B.store(out_state_s[0:48, ch0:ch0 + 128], co[0:48, 128:256], co)

        def l2norm_to(dst3, hh, lnbias):
            B.tt("pool", E[:, 0:T], Y[:, :], Y[:, :], ALU.mult, reads=[Y], writes=[E])
            for bi, (t0, n) in enumerate(NB):
                pb = ps[bi % 4]
                B.mm(pb[:, 0:n], ones[:, :], E[:, t0:t0 + n], reads=[ones, E], writes=[pb])
                B.act(Rt[:, 0:n], pb[:, 0:n], AF.Ln, reads=[pb, cst], writes=[Rt], bias=c_eps)
                B.act(Rt[:, 0:n], Rt[:, 0:n], AF.Exp, reads=[Rt, cst], writes=[Rt], scale=-0.5, bias=lnbias)
                B.tt("dve", dst3[:, hh, t0:t0 + n], Y[:, t0:t0 + n], Rt[:, 0:n], ALU.mult, reads=[Y, Rt], writes=[dst_buf[0]])
            B.memset("pool", E[:, 0:3], 0.0, writes=[E])

        dst_buf = [None]

        def run_tile(kind, M, groups, qa, ka, va, ty, hl_of, STv, first_chunk, m_levels, out_rows, mx, gain_ap):
            nseg = M.nseg
            PR = lambda f: prm[:, ty * 4 + f:ty * 4 + f + 1]
            pg = ps[0]
            for (r0, nr, tk, hg, hl) in groups:
                for c in range(8):
                    B.mm(pg[r0:r0 + nr, 0:2], xnT[:, c, tk:tk + nr], sap(Wb.t, 0, 128, c * 1032 + 768 + hg, [[4, 2]]), start=(c == 0), stop=(c == 7),
                         reads=[xnT, Wb], writes=[pg])
            if KCUT <= 1:
                raise _Stop()
            if kind == "gdn":
                B.act(G(0), pg[:, 0:1], AF.Exp, reads=[pg, prm], writes=[gc], bias=PR(1))
                B.act(G(1), G(0), AF.Ln, reads=[gc, cst], writes=[gc], bias=c_one)
                if first_chunk:
                    B.ts("dve", G(2), G(1), PR(0), rm0[:, 0:1], ALU.mult, ALU.mult, reads=[gc, prm, rm0], writes=[gc])
                else:
                    B.ts("dve", G(2), G(1), PR(0), None, ALU.mult, reads=[gc, prm], writes=[gc])
                B.act(G(3), pg[:, 1:2], AF.Exp, reads=[pg], writes=[gc], scale=-1.0)
                B.ts("dve", G(3), G(3), 1.0, None, ALU.add, reads=[gc], writes=[gc])
                S.op("dve", lambda e: e.reciprocal(G(4), G(3)), [gc], [gc])
                if first_chunk:
                    B.ts("dve", G(4), G(4), rm0[:, 0:1], None, ALU.mult, reads=[gc, rm0], writes=[gc])
                lg = G(2)
            else:
                B.act(G(5), pg[:, 0:1], AF.Identity, reads=[pg, prm], writes=[gc], bias=PR(2))
                if first_chunk:
                    B.ts("dve", G(5), G(5), rm0[:, 1:2], None, ALU.add, reads=[gc, rm0], writes=[gc])
                B.act(G(0), pg[:, 1:2], AF.Exp, reads=[pg, prm], writes=[gc], scale=-1.0, bias=PR(3))
                B.act(G(1), G(0), AF.Ln, reads=[gc, cst], writes=[gc], bias=c_one)
                if first_chunk:
                    B.ts("dve", G(2), G(1), -1.0, rm0[:, 0:1], ALU.mult, ALU.mult, reads=[gc, rm0], writes=[gc])
                else:
                    B.ts("dve", G(2), G(1), -1.0, None, ALU.mult, reads=[gc], writes=[gc])
                lg = G(2)
            if KCUT <= 2:
                raise _Stop()
            B.mm(pg[:, 2:3], M.MinclT, lg, reads=[M.buf, gc], writes=[pg])
            B.mm(pg[:, 3:4], M.SegB, lg, reads=[M.buf, gc], writes=[pg])
            B.cp("act", gc[:, 6:8], pg[:, 2:4], reads=[pg], writes=[gc])
            if KCUT <= 3:
                raise _Stop()
            pR = ps[6]
            B.cp("pool", KT[:, :], ka, reads=[kT], writes=[KT])
            B.cp("act", VT[:, :], va, reads=[vT], writes=[VT])
            B.cp("pool", QT[:, :], qa, reads=[qT], writes=[QT])
            B.tr(pR[:, 0:128], KT[:, :], ident[:, :], reads=[KT, ident], writes=[pR])
            B.tr(pR[:, 128:256], VT[:, :], ident[:, :], reads=[VT, ident], writes=[pR])
            B.tt("pool", sap(EX2.t, 0, 128, 0, [[128, nseg], [1, 128]]), sap(QT.t, 0, 128, 0, [[0, nseg], [1, 128]]), M.segT, ALU.mult,
                 reads=[QT, M.buf], writes=[EX2])
            if KCUT <= 4:
                raise _Stop()
            pB = ps[1]
            pKQ = ps[2]
            p5 = ps[5]
            if kind == "gdn":
                B.act(G(8), G(6), AF.Exp, reads=[gc], writes=[gc])
                B.act(G(9), G(7), AF.Exp, reads=[gc], writes=[gc])
                B.tt("dve", G(10), G(7), G(6), ALU.subtract, reads=[gc], writes=[gc])
                B.act(G(10), G(10), AF.Exp, reads=[gc], writes=[gc])
                B.tt("dve", G(11), G(4), G(8), ALU.mult, reads=[gc], writes=[gc])
                B.ts("pool", D2[:, 0:128], ident[:, :], G(6), None, ALU.mult, reads=[ident, gc], writes=[D2])
                B.ts("pool", D2[:, 128:256], ident[:, :], G(4), None, ALU.mult, reads=[ident, gc], writes=[D2])
                B.mm(pB[:, 0:256], ones[:, :], D2[:, :], reads=[ones, D2], writes=[pB])
                B.ts("dve", Xb[:, :], pB[:, 0:128], G(6), None, ALU.subtract, reads=[pB, gc], writes=[Xb])
                B.ts("dve", XX[:, 0:128], Xb[:, :], 0.0, -1.0, ALU.max, ALU.mult, reads=[Xb], writes=[XX])
                B.ts("pool", XX[:, 128:256], Xb[:, :], 0.0, None, ALU.min, reads=[Xb], writes=[XX])
                B.act(DD[:, :], XX[:, :], AF.Exp, reads=[XX], writes=[DD])
                if KCUT <= 5:
                    raise _Stop()
                B.mm(pKQ[:, 0:128], KT[:, :], KT[:, :], reads=[KT], writes=[pKQ])
                B.mm(pKQ[:, 128:256], KT[:, :], QT[:, :], reads=[KT, QT], writes=[pKQ])
                B.tt("dve", t1[:, :], pKQ[:, 0:128], DD[:, 0:128], ALU.mult, reads=[pKQ, DD], writes=[t1])
                B.stt(Mk[0][:, :], t1[:, :], G(4), M.MstrictNeg, ALU.mult, ALU.mult, reads=[t1, gc, M.buf], writes=[Mk[0]])
                B.tt("dve", t2[:, :], pKQ[:, 0:128], DD[:, 128:256], ALU.mult, reads=[pKQ, DD], writes=[t2])
                B.tt("dve", t3[:, :], t2[:, :], pB[:, 128:256], ALU.mult, reads=[t2, pB], writes=[t3])
                B.tt("pool", Nk[0][:, :], t3[:, :], M.MstrictTNeg, ALU.mult, reads=[t3, M.buf], writes=[Nk[0]])
                B.tt("dve", t4[:, :], pKQ[:, 128:256], DD[:, 128:256], ALU.mult, reads=[pKQ, DD], writes=[t4])
                B.tt("pool", qkmT[:, :], t4[:, :], M.MinclT, ALU.mult, reads=[t4, M.buf], writes=[qkmT])
                B.tt("pool", TT[0][:, :], Nk[0][:, :], ident[:, :], ALU.add, reads=[Nk[0], ident], writes=[TT[0]])
                if KCUT <= 6:
                    raise _Stop()
                pc = ps[3]
                pt = ps[4]
                cur = 0
                for k in range(1, m_levels + 1):
                    nxt = 1 - cur
                    B.mm(pc[:, 0:128], Nk[cur][:, :], Mk[cur][:, :], reads=[Nk[cur], Mk[cur]], writes=[pc])
                    if k < m_levels:
                        B.mm(pc[:, 128:256], Mk[cur][:, :], Nk[cur][:, :], reads=[Nk[cur], Mk[cur]], writes=[pc])
                    B.tt("dve", IM[:, :], pc[:, 0:128], ident[:, :], ALU.add, reads=[pc, ident], writes=[IM])
                    if k < m_levels:
                        B.cp("dve", Mk[nxt][:, :], pc[:, 0:128], reads=[pc], writes=[Mk[nxt]])
                        B.cp("dve", Nk[nxt][:, :], pc[:, 128:256], reads=[pc], writes=[Nk[nxt]])
                    B.mm(pt[:, 0:128], IM[:, :], TT[cur][:, :], reads=[IM, TT[cur]], writes=[pt])
                    B.cp("dve", TT[nxt][:, :], pt[:, 0:128], reads=[pt], writes=[TT[nxt]])
                    cur = nxt
                if KCUT <= 7:
                    raise _Stop()
                TTf = TT[cur]
                B.ts("dve", bv[:, :], pR[:, 128:256], G(4), None, ALU.mult, reads=[pR, gc], writes=[bv])
                B.ts("dve", bgk[:, :], pR[:, 0:128], G(11), None, ALU.mult, reads=[pR, gc], writes=[bgk])
                B.ts("dve", ke[:, :], pR[:, 0:128], G(10), None, ALU.mult, reads=[pR, gc], writes=[ke])
                B.mm(p5[:, 0:128], bgk[:, :], TTf[:, :], reads=[bgk, TTf], writes=[p5])
                B.stt(sap(EX1.t, 0, 128, 0, [[128, nseg], [1, 128]]), sap(p5.t, 0, 128, 0, [[0, nseg], [1, 128]]), -1.0, M.segT,
                      ALU.mult, ALU.mult, reads=[p5, M.buf], writes=[EX1])
                if KCUT <= 8:
                    raise _Stop()
                B.mm(p5[:, 128:256], TTf[:, :], bv[:, :], start=True, stop=False, reads=[TTf, bv], writes=[p5])
                for b in range(nseg):
                    B.mm(p5[:, 128:256], EX1[:, b * 128:(b + 1) * 128], STv[:, b * 128:(b + 1) * 128], start=False, stop=(b == nseg - 1),
                         reads=[EX1, ST], writes=[p5])
                for b in range(nseg):
                    B.mm(p5[:, 256:384], EX2[:, b * 128:(b + 1) * 128], STv[:, b * 128:(b + 1) * 128], start=(b == 0), stop=(b == nseg - 1),
                         reads=[EX2, ST], writes=[p5])
                B.cp("dve", dsb[:, :], p5[:, 128:256], reads=[p5], writes=[dsb])
                B.tt("dve", sap(EX3.t, 0, 128, 0, [[128, nseg], [1, 128]]), sap(p5.t, 0, 128, 128, [[0, nseg], [1, 128]]), M.segrows_b,
                     ALU.mult, reads=[p5, M.buf], writes=[EX3])
                B.mm(p5[:, 384:512], qkmT[:, :], dsb[:, :], reads=[qkmT, dsb], writes=[p5])
                B.ts("dve", o1[:, :], p5[:, 256:384], G(8), None, ALU.mult, reads=[p5, gc], writes=[o1])
                B.tt("dve", ob[:, :], o1[:, :], p5[:, 384:512], ALU.add, reads=[o1, p5], writes=[ob])
                lhs_state = ke
                dec_col = G(9)
            else:
                B.tt("dve", G(12), G(5), G(6), ALU.subtract, reads=[gc], writes=[gc])
                B.tt("dve", G(13), G(12), G(7), ALU.add, reads=[gc], writes=[gc])
                B.ts("pool", D2[:, 0:128], ident[:, :], G(12), None, ALU.mult, reads=[ident, gc], writes=[D2])
                B.ts("pool", D2[:, 128:256], ident[:, :], G(13), None, ALU.mult, reads=[ident, gc], writes=[D2])
                B.mm(pB[:, 0:256], ones[:, :], D2[:, :], reads=[ones, D2], writes=[pB])
                B.stt(Xb[:, :], pB[:, 0:128], G(6), M.MaddIncl, ALU.add, ALU.add, reads=[pB, gc, M.buf], writes=[Xb])
                S.op("dve", lambda e: e.tensor_reduce(G(14), Xb[:, :], AX.X, ALU.max), [Xb], [gc])
                B.tt("dve", t1[:, :], pB[:, 128:256], M.MaddSeg, ALU.add, reads=[pB, M.buf], writes=[t1])
                S.op("dve", lambda e: e.tensor_reduce(G(15), t1[:, :], AX.X, ALU.max), [t1], [gc])
                B.tt("dve", G(16), G(6), mrow[:, 0:1], ALU.add, reads=[gc, mrow], writes=[gc])
                B.tt("dve", G(17), G(16), G(14), ALU.max, reads=[gc], writes=[gc])
                B.ts("dve", G(18), G(17), -1.0, None, ALU.mult, reads=[gc], writes=[gc])
                B.tt("dve", G(19), G(16), G(17), ALU.subtract, reads=[gc], writes=[gc])
                B.act(G(19), G(19), AF.Exp, reads=[gc], writes=[gc])
                B.act(t2[:, :], Xb[:, :], AF.Exp, reads=[Xb, gc], writes=[t2], bias=G(18))
                B.mm(pKQ[:, 0:128], QT[:, :], KT[:, :], reads=[KT, QT], writes=[pKQ])
                B.stt(t3[:, :], t2[:, :], 1.0, pKQ[:, 0:128], ALU.mult, ALU.mult, reads=[t2, pKQ], writes=[t3, gc], accum_out=G(20))
                pc = ps[3]
                B.tr(pc[:, 0:128], t3[:, :], ident[:, :], reads=[t3, ident], writes=[pc])
                B.cp("dve", t4[:, :], pc[:, 0:128], reads=[pc], writes=[t4])
                B.cp("dve", bv[:, :], pR[:, 128:256], reads=[pR], writes=[bv])
                B.tt("dve", G(21), G(7), mrow[:, 0:1], ALU.add, reads=[gc, mrow], writes=[gc])
                B.tt("dve", G(22), G(21), G(15), ALU.max, reads=[gc], writes=[gc])
                B.tt("dve", G(23), G(21), G(22), ALU.subtract, reads=[gc], writes=[gc])
                B.act(G(23), G(23), AF.Exp, reads=[gc], writes=[gc])
                B.tt("dve", G(24), G(13), G(22), ALU.subtract, reads=[gc], writes=[gc])
                B.act(G(24), G(24), AF.Exp, reads=[gc], writes=[gc])
                B.ts("dve", ke[:, :], pR[:, 0:128], G(24), None, ALU.mult, reads=[pR, gc], writes=[ke])
                for b in range(nseg):
                    B.mm(p5[:, 256:384], EX2[:, b * 128:(b + 1) * 128], STv[:, b * 128:(b + 1) * 128], start=(b == 0), stop=(b == nseg - 1),
                         reads=[EX2, ST], writes=[p5])
                B.mm(p5[:, 384:512], t4[:, :], bv[:, :], reads=[t4, bv], writes=[p5])
                B.mm(pg[:, 32:32 + nseg], QT[:, :], nT[:, 0:nseg], reads=[QT, nT], writes=[pg])
                B.stt(jk2[:, 0:nseg], pg[:, 32:32 + nseg], 1.0, M.segrows, ALU.mult, ALU.mult, reads=[pg, M.buf], writes=[jk2, gc], accum_out=G(25))
                B.stt(G(26), G(25), G(19), G(20), ALU.mult, ALU.add, reads=[gc], writes=[gc])
                B.ts("dve", G(29), G(26), -1.0, None, ALU.mult, reads=[gc], writes=[gc])
                B.tt("dve", G(26), G(26), G(29), ALU.max, reads=[gc], writes=[gc])
                B.act(G(27), G(18), AF.Exp, reads=[gc], writes=[gc])
                B.tt("dve", G(26), G(26), G(27), ALU.max, reads=[gc], writes=[gc])
                S.op("dve", lambda e: e.reciprocal(G(28), G(26)), [gc], [gc])
                B.ts("dve", o1[:, :], p5[:, 256:384], G(19), None, ALU.mult, reads=[p5, gc], writes=[o1])
                B.tt("dve", ob[:, :], o1[:, :], p5[:, 384:512], ALU.add, reads=[o1, p5], writes=[ob])
                B.ts("dve", ob[:, :], ob[:, :], G(28), None, ALU.mult, reads=[ob, gc], writes=[ob])
                B.tt("dve", sap(EX3.t, 0, 128, 0, [[128, nseg], [1, 128]]), sap(bv.t, 0, 128, 0, [[0, nseg], [1, 128]]), M.segrows_b,
                     ALU.mult, reads=[bv, M.buf], writes=[EX3])
                lhs_state = ke
                dec_col = G(23)
            if KCUT <= 9:
                raise _Stop()
            B.act(jk2[:, :], ob[:, :], AF.Square, reads=[ob], writes=[jk2, gc], accum_out=G(30))
            B.act(G(31), G(30), AF.Ln, reads=[gc, cst], writes=[gc], scale=1.0 / 128, bias=c_eps)
            B.act(G(32), G(31), AF.Exp, reads=[gc], writes=[gc], scale=-0.5)
            B.stt(on[:, :], ob[:, :], G(32), gain_ap, ALU.mult, ALU.mult, reads=[ob, gc, gn_g, gn_m], writes=[on])
            if KCUT <= 10:
                raise _Stop()
            pZ = ps[7]
            for (r0, nr, tk, hg, hl) in groups:
                for c in range(8):
                    B.mm(pZ[r0:r0 + nr, 0:128], xnT[:, c, tk:tk + nr], Wb[:, c, 776 + hl * 128:776 + (hl + 1) * 128], start=(c == 0), stop=(c == 7),
                         reads=[xnT, Wb], writes=[pZ])
            B.cp("dve", t1[:, :], pZ[:, 0:128], reads=[pZ], writes=[t1])
            B.act(ez[:, :], t1[:, :], AF.Exp, reads=[t1], writes=[ez], scale=-1.0)
            B.ts("pool", ez[:, :], ez[:, :], 1.0, None, ALU.add, reads=[ez], writes=[ez])
            S.op("dve", lambda e: e.reciprocal(ez[:, :], ez[:, :]), [ez], [ez])
            if kind == "gdn":
                B.tt("dve", og[:, :], on[:, :], t1[:, :], ALU.mult, reads=[on, t1], writes=[og])
                B.tt("pool", og[:, :], og[:, :], ez[:, :], ALU.mult, reads=[og, ez], writes=[og])
            else:
                B.tt("pool", og[:, :], on[:, :], ez[:, :], ALU.mult, reads=[on, ez], writes=[og])
            B.tr(pZ[:, 128:256], og[:, :], ident[:, :], reads=[og, ident], writes=[pZ])
            of = obf[obf_ctr[0] % 2]
            obf_ctr[0] += 1
            B.cp("dve", of[:, :], pZ[:, 128:256], reads=[pZ], writes=[of])
            for (r0, nv, tk, hg) in (out_rows if os.environ.get("KSKIP", "") != "store" else []):
                S.dma("sp", (lambda o_, i_: (lambda e: e.dma_start(out=o_, in_=i_)))(oT_d[(mx * 4 + hg) * 128:(mx * 4 + hg + 1) * 128, tk:tk + nv], of[:, r0:r0 + nv]),
                      reads=[of], writes=[oT_res], semres=of)
            if KCUT <= 11:
                raise _Stop()
            ncols = nseg * 128
            banks = [ps[1]] if nseg == 2 else [ps[1], ps[2], ps[3], ps[4]]
            for bi in range((ncols + 511) // 512):
                w = min(512, ncols - bi * 512)
                B.mm(banks[bi][:, 0:w], lhs_state[:, :], EX3[:, bi * 512:bi * 512 + w], reads=[lhs_state, EX3], writes=[banks[bi]])
            B.ts("pool", GEt[:, 0:nseg], M.segfirst, dec_col, None, ALU.mult, reads=[M.buf, gc], writes=[GEt])
            B.mm(pg[:, 8:8 + nseg], ones[:, :], GEt[:, 0:nseg], reads=[ones, GEt], writes=[pg])
            B.cp("act", gebc[:, 0:nseg], pg[:, 8:8 + nseg], reads=[pg], writes=[gebc])
            B.tt("pool", sap(ST.t, 0, 128, 0, [[128, nseg], [1, 128]]), sap(ST.t, 0, 128, 0, [[128, nseg], [1, 128]]),
                 sap(gebc.t, 0, 128, 0, [[1, nseg], [0, 128]]), ALU.mult, reads=[ST, gebc], writes=[ST])
            for bi in range((ncols + 511) // 512):
                w = min(512, ncols - bi * 512)
                B.tt("dve", ST[:, bi * 512:bi * 512 + w], ST[:, bi * 512:bi * 512 + w], banks[bi][:, 0:w], ALU.add, reads=[ST, banks[bi]], writes=[ST])
            if kind == "mls":
                B.mm(pg[:, 64:64 + nseg], ke[:, :], M.segrows, reads=[ke, M.buf], writes=[pg])
                B.tt("dve", nT[:, 0:nseg], nT[:, 0:nseg], gebc[:, 0:nseg], ALU.mult, reads=[nT, gebc], writes=[nT])
                B.tt("dve", nT[:, 0:nseg], nT[:, 0:nseg], pg[:, 64:64 + nseg], ALU.add, reads=[nT, pg], writes=[nT])
                B.cp("dve", mrow[:, 0:1], G(22), reads=[gc], writes=[mrow])

        def store_states(nseg, dstS_fn):
            for b in range(nseg):
                pb = ps[1 + (b // 4) % 4]
                B.tr(pb[:, (b % 4) * 128:(b % 4 + 1) * 128], ST[:, b * 128:(b + 1) * 128], ident[:, :], reads=[ST, ident], writes=[pb])
                if b % 4 == 3 or b == nseg - 1:
                    g0 = (b // 4) * 4
                    w = (b - g0 + 1) * 128
                    B.cp("act", Sio[:, g0 * 128:g0 * 128 + w], pb[:, 0:w], reads=[pb], writes=[Sio])
            dstS_fn()

        try:
          for mx, kind in enumerate(("gdn", "mls")):
              base = 0 if kind == "gdn" else 2056
              convw = convg if kind == "gdn" else convm
              nconv = 1536 if kind == "gdn" else 1024
              out_p = d["p_conv"] if kind == "gdn" else d["p_mconv"]
              out_s = d["s_conv"] if kind == "gdn" else d["s_mconv"]
              for j in range(2):
                  B.load(Sio[0:48, 0:nconv], d["sconv_g" if kind == "gdn" else "sconv_m"], Sio)
                  load_wpair(base, j)
                  for e in range(6):
                      which = e // 2
                      hh = e % 2
                      dstb = (qT, kT, vT)[which]
                      dst_buf[0] = dstb
                      if kind == "mls" and which == 2:
                          def ev(bi, pb, t0, n, dstb=dstb, hh=hh):
                              B.cp("act", dstb[:, hh, t0:t0 + n], pb[:, 0:n], reads=[pb], writes=[dstb])
                          project(e, ev)
                          continue
                      project(e, evac_to_E)
                      cc = which * 4 + 2 * j + hh
                      conv_chunk(convw, cc, cc * 128, out_p, out_s, cc * 128)
                      if kind == "gdn" and which < 2:
                          B.act(Y[:, :], Y[:, :], AF.Silu, reads=[Y], writes=[Y])
                          l2norm_to(dstb, hh, c_lnq if which == 0 else c_zero)
                      elif kind == "mls" and which == 1:
                          B.act(Y[:, :], Y[:, :], AF.Silu, reads=[Y], writes=[Y])
                          B.ts("pool", dstb[:, hh, :], Y[:, :], float(128.0 ** -0.5), None, ALU.mult, reads=[Y], writes=[dstb])
                      else:
                          B.act(dstb[:, hh, :], Y[:, :], AF.Silu, reads=[Y], writes=[dstb])
                  if stage <= 2:
                      raise _Stop()
                  STp = ST
                  B.memset("pool", ST[:, 0:256], 0.0, writes=[ST])
                  if kind == "mls":
                      B.memset("pool", nT[:, 0:2], 0.0, writes=[nT])
                      B.memset("pool", mrow[:, :], 0.0, writes=[mrow])
                  for ci in range(ntiles):
                      if ci == 0:
                          tk, nv = 0, 16
                      else:
                          tk, nv = 16 + 64 * (ci - 1), 64
                      groups = [(0, 64, tk, 2 * j, 0), (64, 64, tk, 2 * j + 1, 1)]
                      qa = qT[:, 0:2, tk:tk + 64]; ka = kT[:, 0:2, tk:tk + 64]; va = vT[:, 0:2, tk:tk + 64]
                      outr = [(0, nv, tk, 2 * j), (64, nv, tk, 2 * j + 1)]
                      gain_ap = gn_g[:, :] if kind == "gdn" else gn_m[:, j * 128:(j + 1) * 128]
                      run_tile(kind, MP, groups, qa, ka, va, j, None, ST, ci == 0, 5, outr, mx, gain_ap)
                  if stage <= 3:
                      raise _Stop()
                  def dstP(j=j, kind=kind):
                      dS = d["p_S"] if kind == "gdn" else d["p_C"]
                      B.store(dS[2 * j:2 * j + 2].rearrange("h v k -> v h k"), sap(Sio.t, 0, 128, 0, [[128, 2], [1, 128]]), Sio)
                  store_states(2, dstP)
                  if kind == "mls":
                      pb = ps[5]
                      B.tr(pb[0:2, 0:128], nT[:, 0:2], ident[:, :], reads=[nT, ident], writes=[pb])
                      B.cp("act", ntm[0:2, :], pb[0:2, 0:128], reads=[pb], writes=[ntm])
                      B.store(d["p_n"][2 * j:2 * j + 2, :], ntm[0:2, :], ntm)
                      B.store(d["p_m"][0:1, 2 * j:2 * j + 1], mrow[0:1, 0:1], mrow)
                      B.store(d["p_m"][0:1, 2 * j + 1:2 * j + 2], mrow[64:65, 0:1], mrow)
                  if stage <= 4:
                      raise _Stop()
                  for hl in range(2):
                      hg = 2 * j + hl
                      dS_in = d["sS"] if kind == "gdn" else d["sC"]
                      B.load(sap(Sio.t, 0, 128, 0, [[128, 16], [1, 128]]), dS_in[:, hg].rearrange("b v k -> v b k"), Sio)
                      for b in range(16):
                          pb = ps[1 + (b // 4) % 4]
                          B.tr(pb[:, (b % 4) * 128:(b % 4 + 1) * 128], Sio[:, b * 128:(b + 1) * 128], ident[:, :], reads=[Sio, ident], writes=[pb])
                          if b % 4 == 3:
                              g0 = (b // 4) * 4
                              B.cp("act", ST[:, g0 * 128:g0 * 128 + 512], pb[:, 0:512], reads=[pb], writes=[ST])
                      if kind == "mls":
                          B.load(nio[:, :], d["sn"], nio)
                          pb = ps[5]
                          B.tr(pb[:, 0:16], nio[0:16, hg * 128:(hg + 1) * 128], ident[0:16, 0:16], reads=[nio, ident], writes=[pb])
                          B.cp("act", nT[:, 0:16], pb[:, 0:16], reads=[pb], writes=[nT])
                          B.cp("dve", mrow[:, 0:1], m0s[:, hg:hg + 1], reads=[m0s], writes=[mrow])
                      groups = [(0, 128, T_P, hg, hl)]
                      qa = qT[:, hl, T_P:T]; ka = kT[:, hl, T_P:T]; va = vT[:, hl, T_P:T]
                      outr = [(0, 128, T_P, hg)]
                      gain_ap = gn_g[:, :] if kind == "gdn" else gn_m[:, (2 + hg) * 128:(3 + hg) * 128]
                      run_tile(kind, MS, groups, qa, ka, va, 2 + hg, None, ST, False, 2, outr, mx, gain_ap)
                      def dstS(hg=hg, kind=kind):
                          dS = d["s_S"] if kind == "gdn" else d["s_C"]
                          B.store(dS[:, hg].rearrange("b v k -> v b k"), sap(Sio.t, 0, 128, 0, [[128, 16], [1, 128]]), Sio)
                      store_states(16, dstS)
                      if kind == "mls":
                          pb = ps[5]
                          B.tr(pb[0:16, 0:128], nT[:, 0:16], ident[:, :], reads=[nT, ident], writes=[pb])
                          B.cp("act", ntm[0:16, :], pb[0:16, 0:128], reads=[pb], writes=[ntm])
                          B.store(d["s_n"][:, hg * 128:(hg + 1) * 128], ntm[0:16, :], ntm)
                          B.cp("dve", msm[:, hg:hg + 1], mrow[:, 0:1], reads=[mrow], writes=[msm])
              if kind == "mls":
                  B.store(d["s_m"], bass.AP(msm.t, 0, [[32, 16], [1, 4]]), msm)

        except _Stop:
            pass
        if stage <= 5:
            S.emit()
            es_mix.close()
            return nc

        def dma(out, in_, reads, writes, semres, q="sp", is_output=False):
            S.dma(q, lambda e: e.dma_start(out=out, in_=in_), reads=reads, writes=writes, semres=semres, is_output=is_output)

        tiles = [(16 + 128 * i, 128 * i) for i in range(16)] + [(T_P, 2048)]
        if ntiles < 33:
            tiles = tiles[:2]

        S.barrier()
        es_mix.close()
        es_mg = ExitStack()
        B.aes = es_mg
        Hd = nc.dram_tensor("H_d", [2176, 1024], F32, kind="Internal").ap()
        Hd_res = Buf(None, "H_d")
        Wg = B.sb("Wg", [128, 8, 2048], BF16); wbr = B.sb("wbr", [128, 8, 1024], BF16); wout = B.sb("wout", [128, 8, 1024], BF16)
        wstg = [B.sb("wstg%d" % i, [128, 2048]) for i in range(2)]
        wc = 0
        for (dst, src, c0, ncol) in ((Wg, "w_in", 4112, 2048), (wbr, "w_branch", 0, 1024), (wout, "w_out", 0, 1024)):
            for k in range(8):
                ws = wstg[wc % 2]
                B.load(ws[:, 0:ncol], d[src][k * 128:(k + 1) * 128, c0:c0 + ncol], ws)
                B.cp(("pool", "dve")[wc % 2], dst[:, k, :], ws[:, 0:ncol], reads=[ws], writes=[dst])
                wc += 1
        OT = [B.sb("OT%d" % i, [128, 8, 128], BF16) for i in range(2)]
        XT = [B.sb("XT%d" % i, [128, 1024]) for i in range(2)]
        sgm = [B.sb("sgm%d" % i, [128, 256]) for i in range(2)]
        mm1 = B.sb("mm1", [128, 128]); mm2 = B.sb("mm2", [128, 128])
        mgT = [B.sb("mgT%d" % i, [128, 8, 128], BF16) for i in range(2)]
        Hs = [B.sb("Hs%d" % i, [128, 1024]) for i in range(2)]
        oT_v = oT_d.rearrange("(c p) t -> p c t", p=128)
        for ti, (tok0, hrow) in enumerate(tiles):
            sl = ti % 2
            dma(OT[sl][:, :, :], oT_v[:, :, tok0:tok0 + 128], [oT_res], [OT[sl]], OT[sl])
            B.load(XT[sl][:, :], d["xall"][tok0:tok0 + 128, :], XT[sl])
            for c in range(8):
                gA = ps[c % 2]
                yB = ps[2 + c % 2]
                for g in range(2):
                    for k in range(8):
                        B.mm(gA[:, g * 128:(g + 1) * 128], Wg[:, k, g * 1024 + c * 128:g * 1024 + (c + 1) * 128], xnT[:, k, tok0:tok0 + 128],
                             start=(k == 0), stop=(k == 7), reads=[Wg, xnT], writes=[gA])
                for g in range(2):
                    for h in range(4):
                        B.mm(yB[:, g * 128:(g + 1) * 128], wbr[:, g * 4 + h, c * 128:(c + 1) * 128], OT[sl][:, g * 4 + h, :],
                             start=(h == 0), stop=(h == 3), reads=[wbr, OT[sl]], writes=[yB])
                sgb = sgm[c % 2]
                B.act(sgb[:, :], gA[:, 0:256], AF.Sigmoid, reads=[gA], writes=[sgb])
                B.tt("dve", mm1[:, :], sgb[:, 0:128], yB[:, 0:128], ALU.mult, reads=[sgb, yB], writes=[mm1])
                B.tt("dve", mm2[:, :], sgb[:, 128:256], yB[:, 128:256], ALU.mult, reads=[sgb, yB], writes=[mm2])
                B.tt("pool", mgT[sl][:, c, :], mm1[:, :], mm2[:, :], ALU.add, reads=[mm1, mm2], writes=[mgT[sl]])
            for half in range(2):
                pb = ps[4 + half]
                for c in range(8):
                    B.mm(pb[:, 0:512], mgT[sl][:, c, :], wout[:, c, half * 512:(half + 1) * 512], start=(c == 0), stop=(c == 7),
                         reads=[mgT[sl], wout], writes=[pb])
                B.tt("dve", Hs[sl][:, half * 512:(half + 1) * 512], XT[sl][:, half * 512:(half + 1) * 512], pb[:, 0:512], ALU.add,
                     reads=[XT[sl], pb], writes=[Hs[sl]])
            dma(Hd[hrow:hrow + 128, :], Hs[sl][:, :], [Hs[sl]], [Hd_res], Hs[sl])
        if stage <= 6:
            S.emit()
            es_mg.close()
            return nc

        S.barrier()
        es_mg.close()
        es_pe = ExitStack()
        B.aes = es_pe
        wq = B.sb("wq", [128, 8, 2048], BF16); skT = B.sb("skT", [128, 16, 128], BF16)
        nffn = B.sb("nffn", [128, 1024]); nfin = B.sb("nfin", [128, 1024]); identb = B.sb("identb", [128, 128], BF16); iota16 = B.sb("iota16", [128, 16])
        es_tmp = ExitStack()
        B.aes = es_tmp
        skn = B.sb("skn", [128, 16, 128])
        wst2 = [B.sb("wst2_%d" % i, [128, 2048]) for i in range(2)]
        B.load(nffn[:, :], d["nffn_bc"], nffn)
        B.load(nfin[:, :], d["nfin_bc"], nfin)
        B.load(iota16[:, :], d["iota16"], iota16)
        B.cp("dve", identb[:, :], ident[:, :], reads=[ident], writes=[identb])
        for k in range(8):
            ws = wst2[k % 2]
            B.load(ws[:, :], d["wq"][k * 128:(k + 1) * 128, :], ws)
            B.cp(("pool", "dve")[k % 2], wq[:, k, :], ws[:, :], reads=[ws], writes=[wq])
        B.load(skn[:, :, :], d["subk"], skn)
        for hp in range(16):
            pb = ps[hp // 4]
            B.tr(pb[:, (hp % 4) * 128:(hp % 4 + 1) * 128], skn[:, hp, :], ident[:, :], reads=[skn, ident], writes=[pb])
            if hp % 4 == 3:
                g0 = hp - 3
                B.cp("dve", skT[:, g0:g0 + 4, :], sap(pb.t, 0, 128, 0, [[128, 4], [1, 128]]), reads=[pb], writes=[skT])
        S.barrier()
        es_tmp.close()
        B.aes = es_pe
        NBUF = 10
        Hs2 = [B.sb("Hs2_%d" % i, [128, 1024]) for i in range(2)]
        XN2 = [B.sb("XN2_%d" % i, [128, 1024]) for i in range(2)]
        XN2b = [B.sb("XN2b_%d" % i, [128, 1024], BF16) for i in range(2)]
        EIi = [B.sb("EIi%d" % i, [128, 128], I32) for i in range(2)]
        gate = [B.sb("gate%d" % i, [128, 128]) for i in range(2)]
        jkA = B.sb("jkA", [128, 256]); jkB = B.sb("jkB", [128, 1024]); jkC = B.sb("jkC", [128, 1024], BF16); stp = B.sb("stp", [128, 4]); stq = B.sb("stq", [128, 4])
        x2T = B.sb("x2T", [128, 8, 128], BF16); qTs = B.sb("qTs", [128, 16, 128], BF16)
        Ssb = B.sb("Ssb", [128, 2048]); SC2 = B.sb("SC2", [128, 2048]); cand = B.sb("cand", [128, 2048]); eq4 = B.sb("eq4", [128, 2048])
        Vv = B.sb("Vv", [128, 256]); Iu = B.sb("Iu", [128, 256], U32); If = B.sb("If", [128, 256])
        TS = B.sb("TS", [128, 128]); negm = B.sb("negm", [128, 8]); Zs = B.sb("Zs", [128, 8]); rZ = B.sb("rZ", [128, 8])
        EI = B.sb("EI", [128, 128]); egt = B.sb("egt", [128, 128])
        PU = B.sb("PU", [128, 128], U32); PA = B.sb("PA", [128, 128], U32); PB = B.sb("PB", [128, 128], U32)
        Af = B.sb("Af", [128, 128]); Bf = B.sb("Bf", [128, 128]); sel0 = B.sb("sel0", [128, 128]); sel1 = B.sb("sel1", [128, 128])
        AVs = [B.sb("AVs%d" % i, [128, 1]) for i in range(4)]
        GAs = [B.sb("GAs%d" % i, [128, 1]) for i in range(4)]
        Wts = [B.sb("Wts%d" % i, [128, 1]) for i in range(4)]
        dgb = [B.sb("dgb%d" % i, [128, 128], BF16) for i in range(4)]
        UG = [B.sb("UG%d" % i, [128, 2048], BF16) for i in range(NBUF)]
        YO = [B.sb("YO%d" % i, [128, 1024]) for i in range(2)]

        def subs(buf, n):
            return [Buf(buf.t, "%s_s%d" % (buf.r.name, i)) for i in range(n)]
        VvS = subs(Vv, 16); IuS = subs(Iu, 16); SC2S = subs(SC2, 16); TSS = subs(TS, 8); PUS = subs(PU, 8)

        def routing(ti):
            tok0, hrow = tiles[ti]
            sl = ti % 2
            H = Hs2[sl]
            X2 = XN2[sl]
            dma(H[:, :], Hd[hrow:hrow + 128, :], [Hd_res], [H], H)
            B.act(SC2[:, 0:1024], H[:, :], AF.Square, reads=[H], writes=SC2S[0:8] + [stp], accum_out=stp[:, 0:1])
            rstd_col(stp, 128, 1024)
            B.stt(X2[:, :], H[:, :], stp[:, 2:3], nffn[:, :], ALU.mult, ALU.mult, reads=[H, stp, nffn], writes=[X2])
            B.cp("pool", XN2b[sl][:, :], X2[:, :], reads=[X2], writes=[XN2b[sl]])
            yield
            for c in range(8):
                pb = ps[c // 4]
                B.tr(pb[:, (c % 4) * 128:(c % 4 + 1) * 128], X2[:, c * 128:(c + 1) * 128], ident[:, :], reads=[X2, ident], writes=[pb])
            for hh in range(2):
                B.cp("act", x2T[:, 4 * hh:4 * hh + 4, :], sap(ps[hh].t, 0, 128, 0, [[128, 4], [1, 128]]), reads=[ps[hh]], writes=[x2T])
                yield
            for g in range(4):
                pb = ps[2 + g % 2]
                for q4 in range(4):
                    hp = 4 * g + q4
                    for k in range(8):
                        B.mm(pb[:, q4 * 128:(q4 + 1) * 128], wq[:, k, hp * 128:(hp + 1) * 128], x2T[:, k, :], start=(k == 0), stop=(k == 7),
                             reads=[wq, x2T], writes=[pb])
                B.cp("act", qTs[:, 4 * g:4 * g + 4, :], sap(pb.t, 0, 128, 0, [[128, 4], [1, 128]]), reads=[pb], writes=[qTs])
                yield
            for hp in range(16):
                pb = ps[4 + (hp // 4) % 2]
                B.mm(pb[:, (hp % 4) * 128:(hp % 4 + 1) * 128], qTs[:, hp, :], skT[:, hp, :], reads=[qTs, skT], writes=[pb])
                if hp % 4 == 3:
                    g = hp // 4
                    B.cp("act", Ssb[:, g * 512:(g + 1) * 512], pb[:, 0:512], reads=[pb], writes=[Ssb])
                    yield
            yield "front_done"
            for g4 in range(4):
                hps = [4 * g4 + q for q in range(4)]
                seg = lambda hp: Ssb[:, hp * 128:(hp + 1) * 128]
                seg2 = lambda hp: SC2[:, hp * 128:(hp + 1) * 128]
                v0 = lambda hp: Vv[:, hp * 16:hp * 16 + 8]
                v1 = lambda hp: Vv[:, hp * 16 + 8:hp * 16 + 16]
                i0_ = lambda hp: Iu[:, hp * 16:hp * 16 + 8]
                i1_ = lambda hp: Iu[:, hp * 16 + 8:hp * 16 + 16]
                for hp in hps:
                    S.op("dve", (lambda a, b: (lambda e: e.max(a, b)))(v0(hp), seg(hp)), [Ssb], [VvS[hp]])
                yield
                for hp in hps:
                    S.op("dve", (lambda a, b, c_: (lambda e: e.max_index(a, b, c_)))(i0_(hp), v0(hp), seg(hp)), [Ssb, VvS[hp]], [IuS[hp]])
                yield
                for hp in hps:
                    S.op("dve", (lambda a, b, c_: (lambda e: e.match_replace(a, b, c_, NEG)))(seg2(hp), v0(hp), seg(hp)), [Ssb, VvS[hp]], [SC2S[hp]])
                yield
                for hp in hps:
                    S.op("dve", (lambda a, b: (lambda e: e.max(a, b)))(v1(hp), seg2(hp)), [SC2S[hp]], [VvS[hp]])
                yield
                for hp in hps:
                    S.op("dve", (lambda a, b, c_: (lambda e: e.max_index(a, b, c_)))(i1_(hp), v1(hp), seg2(hp)), [SC2S[hp], VvS[hp]], [IuS[hp]])
                yield
            B.cp("dve", If[:, :], Iu[:, :], reads=IuS, writes=[If])
            B.ts("dve", sap(If.t, 0, 128, 0, [[32, 8], [1, 16]]), sap(If.t, 0, 128, 0, [[32, 8], [1, 16]]), 128.0, None, ALU.mult, reads=[If], writes=[If])
            c4 = sap(cand.t, 0, 128, 0, [[256, 8], [16, 16], [1, 16]])
            B.tt("dve", c4, sap(Vv.t, 0, 128, 0, [[32, 8], [1, 16], [0, 16]]), sap(Vv.t, 0, 128, 16, [[32, 8], [0, 16], [1, 16]]), ALU.add,
                 reads=VvS, writes=[cand])
            yield
            for g4 in range(2):
                hs = [4 * g4 + q for q in range(4)]
                cs = lambda h: cand[:, h * 256:(h + 1) * 256]
                cs2 = lambda h: SC2[:, h * 256:(h + 1) * 256]
                t0_ = lambda h: TS[:, h * 16:h * 16 + 8]
                t1_ = lambda h: TS[:, h * 16 + 8:h * 16 + 16]
                p0_ = lambda h: PU[:, h * 16:h * 16 + 8]
                p1_ = lambda h: PU[:, h * 16 + 8:h * 16 + 16]
                for h in hs:
                    S.op("dve", (lambda a, b: (lambda e: e.max(a, b)))(t0_(h), cs(h)), [cand], [TSS[h]])
                yield
                for h in hs:
                    S.op("dve", (lambda a, b, c_: (lambda e: e.max_index(a, b, c_)))(p0_(h), t0_(h), cs(h)), [cand, TSS[h]], [PUS[h]])
                yield
                for h in hs:
                    S.op("dve", (lambda a, b, c_: (lambda e: e.match_replace(a, b, c_, NEG)))(cs2(h), t0_(h), cs(h)), [cand, TSS[h]], [SC2S[2 * h], SC2S[2 * h + 1]])
                yield
                for h in hs:
                    S.op("dve", (lambda a, b: (lambda e: e.max(a, b)))(t1_(h), cs2(h)), [SC2S[2 * h], SC2S[2 * h + 1]], [TSS[h]])
                yield
                for h in hs:
                    S.op("dve", (lambda a, b, c_: (lambda e: e.max_index(a, b, c_)))(p1_(h), t1_(h), cs2(h)), [SC2S[2 * h], SC2S[2 * h + 1], TSS[h]], [PUS[h]])
                yield
            S.op("dve", lambda e: e.tensor_single_scalar(PA[:, :], PU[:, :], 4, ALU.logical_shift_right), PUS, [PA])
            S.op("dve", lambda e: e.tensor_single_scalar(PB[:, :], PU[:, :], 15, ALU.bitwise_and), PUS, [PB])
            B.cp("dve", Af[:, :], PA[:, :], reads=[PA], writes=[Af])
            B.cp("dve", Bf[:, :], PB[:, :], reads=[PB], writes=[Bf])
            yield
            for (src, off, dst) in ((Af, 0, sel0), (Bf, 16, sel1)):
                B.tt("dve", sap(eq4.t, 0, 128, 0, [[16, 128], [1, 16]]), sap(src.t, 0, 128, 0, [[1, 128], [0, 16]]),
                     sap(iota16.t, 0, 128, 0, [[0, 128], [1, 16]]), ALU.is_equal, reads=[src, iota16], writes=[eq4])
                yield
                B.tt("dve", sap(eq4.t, 0, 128, 0, [[256, 8], [16, 16], [1, 16]]), sap(eq4.t, 0, 128, 0, [[256, 8], [16, 16], [1, 16]]),
                     sap(If.t, 0, 128, off, [[32, 8], [0, 16], [1, 16]]), ALU.mult, reads=[eq4, If], writes=[eq4])
                yield
                S.op("dve", (lambda d_: (lambda e: e.tensor_reduce(d_[:, :], sap(eq4.t, 0, 128, 0, [[16, 128], [1, 16]]), AX.X, ALU.add)))(dst),
                     [eq4], [dst])
                yield
            B.tt("dve", EI[:, :], sel0[:, :], sel1[:, :], ALU.add, reads=[sel0, sel1], writes=[EI])
            B.ts("dve", EI[:, :], EI[:, :], 16383.0, 0.0, ALU.min, ALU.max, reads=[EI], writes=[EI])
            B.cp("dve", EIi[sl][:, :], EI[:, :], reads=[EI], writes=[EIi[sl]])
            B.ts("dve", negm[:, :], sap(TS.t, 0, 128, 0, [[16, 8]]), -1.0, None, ALU.mult, reads=TSS, writes=[negm])
            yield
            for h in range(8):
                B.act(egt[:, h * 16:(h + 1) * 16], TS[:, h * 16:(h + 1) * 16], AF.Exp, reads=[TSS[h], negm], writes=[egt, Zs],
                      bias=negm[:, h:h + 1], accum_out=Zs[:, h:h + 1])
            S.op("dve", lambda e: e.reciprocal(rZ[:, :], Zs[:, :]), [Zs], [rZ])
            B.tt("dve", sap(gate[sl].t, 0, 128, 0, [[16, 8], [1, 16]]), sap(egt.t, 0, 128, 0, [[16, 8], [1, 16]]), sap(rZ.t, 0, 128, 0, [[1, 8], [0, 16]]),
                 ALU.mult, reads=[egt, rZ], writes=[gate[sl]])
            yield

        def drain(gen, n=None, until=None):
            if gen is None:
                return None
            try:
                if until is not None:
                    while next(gen) != until:
                        pass
                elif n is None:
                    while True:
                        next(gen)
                else:
                    for _ in range(n):
                        next(gen)
            except StopIteration:
                return None
            return gen

        ug_ctr = [0]
        nt_ = len(tiles)
        drain(routing(0))
        for ti in range(nt_):
            tok0, hrow = tiles[ti]
            sl = ti % 2
            H = Hs2[sl]
            nxt = routing(ti + 1) if ti + 1 < nt_ else None
            nxt = drain(nxt, until="front_done")
            pa = [ps[6], ps[7]]
            for col in range(128):
                ug = UG[ug_ctr[0] % NBUF]
                ug_ctr[0] += 1
                S.dma("pool", (lambda ug, col, sl: (lambda e: e.indirect_dma_start(out=ug[:, :], out_offset=None, in_=uvb[:, :],
                      in_offset=bass.IndirectOffsetOnAxis(ap=EIi[sl][:, col:col + 1], axis=0))))(ug, col, sl),
                      reads=[EIi[sl], uvb_res], writes=[ug], semres=ug)
                a4 = col % 4
                B.stt(jkC[:, :], ug[:, 0:1024], 1.0, XN2b[sl][:, :], ALU.mult, ALU.mult, reads=[ug, XN2b[sl]], writes=[AVs[a4]], accum_out=AVs[a4][:, 0:1])
                B.act(GAs[a4][:, :], AVs[a4][:, :], AF.Gelu, reads=[AVs[a4]], writes=[GAs[a4]])
                dg = dgb[a4]
                B.act(Wts[a4][:, :], GAs[a4][:, :], AF.Copy, reads=[GAs[a4], gate[sl]], writes=[Wts[a4]], scale=gate[sl][:, col:col + 1])
                B.act(dg[:, :], identb[:, :], AF.Copy, reads=[identb, Wts[a4]], writes=[dg], scale=Wts[a4][:, 0:1])
                for half in range(2):
                    B.mm(pa[half][:, 0:512], dg[:, :], ug[:, 1024 + half * 512:1024 + (half + 1) * 512], start=(col == 0), stop=(col == 127),
                         reads=[dg, ug], writes=[pa[half]])
                nxt = drain(nxt, 3)
            nxt = drain(nxt)
            yo = YO[sl]
            for half in range(2):
                B.tt("dve", yo[:, half * 512:(half + 1) * 512], H[:, half * 512:(half + 1) * 512], pa[half][:, 0:512], ALU.add,
                     reads=[H, pa[half]], writes=[yo])
            B.act(jkB[:, :], yo[:, :], AF.Square, reads=[yo], writes=[jkB, stq], accum_out=stq[:, 0:1])
            rstd_col(stq, 128, 1024)
            B.stt(yo[:, :], yo[:, :], stq[:, 2:3], nfin[:, :], ALU.mult, ALU.mult, reads=[yo, stq, nfin], writes=[yo])
            if ti < 16:
                B.store(d["y_p"][hrow:hrow + 128, :], yo[:, :], yo)
            else:
                B.store(d["y_s"][:, :], yo[:, :], yo)
        S.emit()
        es_pe.close()
    return nc


_PROG = {}


def _get_prog():
    if "nc" not in _PROG:
        import os
        _PROG["nc"] = build_program(int(os.environ.get("KSTAGE", "9")), int(os.environ.get("KNT", "33")))
    return _PROG["nc"]


def _core_inputs(inp, i, consts):
    f = lambda k: np.asarray(inp[k], dtype=np.float32)
    m = dict(consts)
    xs = f("x_sample")[16 * i:16 * i + 16].reshape(128, 1024)
    m["xall"] = np.ascontiguousarray(np.concatenate([f("meta_tokens"), f("x_prompt")[i], xs], axis=0))
    m["sS"] = np.ascontiguousarray(f("state_gdn_S")[0, 16 * i:16 * i + 16])
    m["sC"] = np.ascontiguousarray(f("state_mlstm_C")[0, 16 * i:16 * i + 16])
    m["sconv_g"] = np.ascontiguousarray(f("state_gdn_conv")[0, 16 * i:16 * i + 16].reshape(48, 1536))
    m["sconv_m"] = np.ascontiguousarray(f("state_mlstm_conv")[0, 16 * i:16 * i + 16].reshape(48, 1024))
    m["sn"] = np.ascontiguousarray(f("state_mlstm_n")[0, 16 * i:16 * i + 16].reshape(16, 512))
    m["m0s"] = np.ascontiguousarray(np.repeat(f("state_mlstm_m")[0, 16 * i:16 * i + 16], 8, axis=0))
    return m


def _consts(inp):
    f = lambda k: np.asarray(inp[k], dtype=np.float32)
    c = {}
    c["w_in"] = np.ascontiguousarray(f("w_in")[0])
    c["w_branch"] = np.ascontiguousarray(f("w_branch")[0])
    c["w_out"] = np.ascontiguousarray(f("w_out")[0])
    c["nffn_bc"] = np.ascontiguousarray(np.broadcast_to(f("norm_ffn")[0][None, :], (128, 1024)))
    c["nfin_bc"] = np.ascontiguousarray(np.broadcast_to(f("norm_final")[None, :], (128, 1024)))
    c["wq"] = np.ascontiguousarray(f("peer_w_query")[0])
    c["subk"] = np.ascontiguousarray(f("peer_sub_keys")[0].transpose(2, 0, 1, 3).reshape(128, 16, 128))
    c["iota16"] = np.ascontiguousarray(np.broadcast_to(np.arange(16, dtype=np.float32)[None, :], (128, 16)))
    c["peer_uv"] = np.ascontiguousarray(np.concatenate([f("peer_u")[0], f("peer_v")[0]], axis=1))
    c["ident"] = np.eye(128, dtype=np.float32)
    c["ones"] = np.ones((128, 128), np.float32)
    c["maskP"] = _maskset(2, 64)
    c["maskS"] = _maskset(16, 8)
    r = np.arange(128)
    valid = (r % 64) < 16
    c["rowmask0"] = np.ascontiguousarray(np.stack([valid.astype(np.float32), np.where(valid, 0.0, NEG).astype(np.float32)], axis=1))
    c["nmix"] = np.ascontiguousarray(f("norm_mix")[0].reshape(8, 128).T)
    c["convg"] = np.ascontiguousarray(f("conv_gdn")[0].T.reshape(12, 128, 4).transpose(1, 0, 2).reshape(128, 48))
    c["convm"] = np.ascontiguousarray(f("conv_mlstm")[0].T.reshape(8, 128, 4).transpose(1, 0, 2).reshape(128, 32))
    headof = np.zeros((6, 128), np.int64)
    for ty in range(2):
        headof[ty] = 2 * ty + (r // 64)
    for ty in range(2, 6):
        headof[ty] = ty - 2
    prs = [f("gdn_a_log")[0], f("gdn_dt_bias")[0], f("mlstm_i_bias")[0], f("mlstm_f_bias")[0]]
    rowp = np.zeros((128, 6, 4), np.float32)
    for ty in range(6):
        for k, p in enumerate(prs):
            rowp[:, ty, k] = p[headof[ty]]
    c["rowp"] = np.ascontiguousarray(rowp.reshape(128, 24))
    c["gn_g"] = np.ascontiguousarray(np.broadcast_to(f("gdn_out_norm")[0][None, :], (128, 128)))
    gm = f("mlstm_out_norm")[0]
    gn_m = np.zeros((128, 6, 128), np.float32)
    for ty in range(6):
        gn_m[:, ty, :] = gm[headof[ty]]
    c["gn_m"] = np.ascontiguousarray(gn_m.reshape(128, 768))
    return c


def kernel(**inputs):
    nc = _get_prog()
    consts = _consts(inputs)
    in_maps = [_core_inputs(inputs, i, consts) for i in range(8)]
    res = run_bass_kernel_spmd(nc, in_maps, core_ids=list(range(8)))
    R = res.results
    g = lambda k: [np.asarray(R[i][k], dtype=np.float32) for i in range(8)]
    y_p = np.stack(g("y_p"))
    y_s = np.concatenate([a.reshape(16, 8, 1024) for a in g("y_s")], 0)
    p_S = np.stack(g("p_S"))[None]
    p_conv = np.stack(g("p_conv"))[None]
    p_C = np.stack(g("p_C"))[None]
    p_n = np.stack(g("p_n"))[None]
    p_m = np.stack([a.reshape(4) for a in g("p_m")])[None]
    p_mconv = np.stack(g("p_mconv"))[None]
    s_S = np.concatenate(g("s_S"), 0)[None]
    s_conv = np.concatenate([a.reshape(16, 3, 1536) for a in g("s_conv")], 0)[None]
    s_C = np.concatenate(g("s_C"), 0)[None]
    s_n = np.concatenate([a.reshape(16, 4, 128) for a in g("s_n")], 0)[None]
    s_m = np.concatenate(g("s_m"), 0)[None]
    s_mconv = np.concatenate([a.reshape(16, 3, 1024) for a in g("s_mconv")], 0)[None]
    return (y_p, y_s, p_S, p_conv, p_C, p_n, p_m, p_mconv, s_S, s_conv, s_C, s_n, s_m, s_mconv)
```

```python
import os
import numpy as np
from contextlib import ExitStack
import concourse.bass as bass
import concourse.mybir as mybir
from concourse.bass_utils import run_bass_kernel_spmd

F32 = mybir.dt.float32
BF16 = mybir.dt.bfloat16
I32 = mybir.dt.int32
U32 = mybir.dt.uint32
AF = mybir.ActivationFunctionType
ALU = mybir.AluOpType
AX = mybir.AxisListType


class Res:
    __slots__ = ("name", "writer", "readers", "dsem", "dcnt", "dim")

    def __init__(self, name):
        self.name = name
        self.writer = None
        self.readers = {}
        self.dsem = None
        self.dcnt = 0
        self.dim = None


class Buf:
    def __init__(self, t, name):
        self.t = t
        self.r = Res(name)

    def __getitem__(self, k):
        return self.t[k]


class Sched:
    ENGS = ("pe", "act", "dve", "pool", "sp")

    def __init__(self, nc, es):
        self.nc = nc
        self.es = es
        self.sems = {}
        for e in self.ENGS[:4]:
            self.sems[e] = es.enter_context(nc.semaphore("s_" + e))
        self.cnt = {e: 0 for e in self.ENGS}
        self.seen = {e: {} for e in self.ENGS}
        self.vc = {}
        self.prog = {e: [] for e in self.ENGS}
        self.nres = 0
        self.out_deps = []
        self.dma_counts = {}

    def _need(self, eng, dep, waits):
        if dep is None:
            return
        dim, c = dep
        se = self.seen[eng]
        if se.get(dim, 0) >= c:
            return
        if dim == eng and eng == "pe":
            return
        if waits.get(dim, 0) < c:
            waits[dim] = c
        for k, v in self.vc[(dim, c)].items():
            if se.get(k, 0) < v:
                se[k] = v

    def _deps(self, eng, reads, writes):
        waits = {}
        for b in reads:
            self._need(eng, b.r.writer, waits)
        for b in writes:
            self._need(eng, b.r.writer, waits)
            for d, c in list(b.r.readers.items()):
                self._need(eng, (d, c), waits)
        return waits

    def op(self, eng, fn, reads=(), writes=()):
        waits = self._deps(eng, reads, writes)
        n = self.cnt[eng] + 1
        self.cnt[eng] = n
        snap = dict(self.seen[eng])
        snap[eng] = n
        self.vc[(eng, n)] = snap
        for b in reads:
            if b.r.readers.get(eng, 0) < n:
                b.r.readers[eng] = n
        for b in writes:
            b.r.writer = (eng, n)
            b.r.readers = {}
        self.prog[eng].append((list(waits.items()), fn, eng, 1))

    def dma(self, q, fn, reads=(), writes=(), semres=None, is_output=False):
        waits = self._deps(q, reads, writes)
        R = semres.r
        if R.dsem is None:
            self.nres += 1
            R.dim = "D%d_%s" % (self.nres, R.name)
            R.dsem = self.es.enter_context(self.nc.semaphore(R.dim))
            self.sems[R.dim] = R.dsem
        R.dcnt += 16
        c = R.dcnt
        dim = R.dim
        snap = dict(self.seen[q])
        snap[dim] = c
        self.vc[(dim, c)] = snap
        self.dma_counts[dim] = c
        for b in reads:
            if b.r.readers.get(dim, 0) < c:
                b.r.readers[dim] = c
        for b in writes:
            b.r.writer = (dim, c)
            b.r.readers = {}
        self.prog[q].append((list(waits.items()), fn, dim, 16))
        if is_output:
            self.out_deps.append((dim, c))
        return (dim, c)

    def barrier(self):
        targets = [(e, self.cnt[e]) for e in self.ENGS[:4] if self.cnt[e] > 0]
        targets += list(self.dma_counts.items())
        for eng in self.ENGS:
            waits = {}
            for dep in targets:
                self._need(eng, dep, waits)
            self.prog[eng].append((list(waits.items()), None, None, 0))

    def emit(self):
        nc = self.nc
        fw = {}
        for d, c in self.out_deps:
            if fw.get(d, 0) < c:
                fw[d] = c
        self.prog["sp"].append((list(fw.items()), None, None, 0))
        sems = self.sems
        prog = self.prog

        def mk(name):
            def body(e):
                for waits, fn, dim, inc in prog[name]:
                    for d, c in waits:
                        e.wait_ge(sems[d], c)
                    if fn is not None:
                        fn(e).then_inc(sems[dim], inc)
            return body

        with nc.Block() as block:
            block.tensor(mk("pe"))
            block.scalar(mk("act"))
            block.vector(mk("dve"))
            block.gpsimd(mk("pool"))
            block.sync(mk("sp"))


class Builder:
    def __init__(self, nc, es):
        self.nc = nc
        self.es = es
        self.S = Sched(nc, es)
        self.aes = es

    def sb(self, name, shape, dt=F32):
        t = self.aes.enter_context(self.nc.sbuf_tensor("sb_" + name, list(shape), dt))
        return Buf(t, name)

    def ps(self, name, shape=(128, 512), dt=F32):
        t = self.es.enter_context(self.nc.psum_tensor("ps_" + name, list(shape), dt))
        return Buf(t, name)

    def dram_in(self, name, shape, dt=F32):
        return self.nc.dram_tensor(name, list(shape), dt, kind="ExternalInput")

    def dram_out(self, name, shape, dt=F32):
        return self.nc.dram_tensor(name, list(shape), dt, kind="ExternalOutput")

    def mm(self, out, lhsT, rhs, start=True, stop=True, reads=(), writes=()):
        self.S.op("pe", lambda e: e.matmul(out, lhsT, rhs, start=start, stop=stop), reads, writes)

    def tr(self, out, in_, ident, reads=(), writes=()):
        self.S.op("pe", lambda e: e.transpose(out, in_, ident), reads, writes)

    def act(self, out, in_, func, reads=(), writes=(), **kw):
        self.S.op("act", lambda e: e.activation(out, in_, func, **kw), reads, writes)

    def tt(self, eng, out, in0, in1, op, reads=(), writes=()):
        self.S.op(eng, lambda e: e.tensor_tensor(out, in0, in1, op), reads, writes)

    def ts(self, eng, out, in0, s1, s2, op0, op1=None, reads=(), writes=(), accum_out=None):
        if op1 is None:
            self.S.op(eng, lambda e: e.tensor_scalar(out, in0, s1, None, op0), reads, writes)
        elif accum_out is None:
            self.S.op(eng, lambda e: e.tensor_scalar(out, in0, s1, s2, op0, op1), reads, writes)
        else:
            self.S.op(eng, lambda e: e.tensor_scalar(out, in0, s1, s2, op0, op1, accum_out=accum_out), reads, writes)

    def stt(self, out, in0, scalar, in1, op0, op1, reads=(), writes=(), accum_out=None):
        if accum_out is None:
            self.S.op("dve", lambda e: e.scalar_tensor_tensor(out, in0, scalar, in1, op0, op1), reads, writes)
        else:
            self.S.op("dve", lambda e: e.scalar_tensor_tensor(out, in0, scalar, in1, op0, op1, accum_out=accum_out), reads, writes)

    def cp(self, eng, out, in_, reads=(), writes=()):
        if eng == "act":
            self.S.op("act", lambda e: e.copy(out, in_), reads, writes)
        else:
            self.S.op(eng, lambda e: e.tensor_copy(out, in_), reads, writes)

    def memset(self, eng, ap, val, writes=()):
        self.S.op(eng, lambda e: e.memset(ap, val), (), writes)

    def load(self, out, in_, buf, q="sp", reads=()):
        self.S.dma(q, lambda e: e.dma_start(out=out, in_=in_), reads=reads, writes=(buf,), semres=buf)

    def store(self, out, in_, buf, q="sp", is_output=True):
        self.S.dma(q, lambda e: e.dma_start(out=out, in_=in_), reads=(buf,), writes=(), semres=buf, is_output=is_output)


def sap(t, p0, pn, f0, dims):
    base = t[:]
    F = base.ap[0][0]
    return bass.AP(t, p0 * F + f0, [[F, pn]] + [list(d) for d in dims])

T_P = 2064
T_S = 128
T = T_P + T_S
NB = [(0, 512), (512, 512), (1024, 512), (1536, 512), (2048, 144)]
EPS = 1e-6
NEG = -1.0e30


def _maskset(nseg, seglen):
    r = np.arange(128)
    seg = r // seglen
    same = seg[:, None] == seg[None, :]
    le = r[:, None] <= r[None, :]
    lt = r[:, None] < r[None, :]
    MinclT = (same & le).astype(np.float32)
    SegB = same.astype(np.float32)
    MstrictNeg = -((same & lt.T).astype(np.float32))
    MstrictTNeg = -((same & lt).astype(np.float32))
    MaddIncl = np.where(same & le.T, 0.0, NEG).astype(np.float32)
    MaddSeg = np.where(same, 0.0, NEG).astype(np.float32)
    segrows = (seg[:, None] == np.arange(nseg)[None, :]).astype(np.float32)
    segfirst = segrows * ((r % seglen) == 0)[:, None].astype(np.float32)
    segT = np.broadcast_to(segrows.T[None], (128, nseg, 128)).reshape(128, nseg * 128)
    return np.ascontiguousarray(np.concatenate(
        [MinclT, SegB, MstrictNeg, MstrictTNeg, MaddIncl, MaddSeg, segrows, segfirst, segT], axis=1).astype(np.float32))


class MaskSet:
    def __init__(self, buf, nseg):
        self.buf = buf
        self.nseg = nseg
        t = buf.t
        self.MinclT = t[:, 0:128]
        self.SegB = t[:, 128:256]
        self.MstrictNeg = t[:, 256:384]
        self.MstrictTNeg = t[:, 384:512]
        self.MaddIncl = t[:, 512:640]
        self.MaddSeg = t[:, 640:768]
        self.segrows = t[:, 768:768 + nseg]
        self.segfirst = t[:, 768 + nseg:768 + 2 * nseg]
        o = 768 + 2 * nseg
        self.segT = sap(t, 0, 128, o, [[128, nseg], [1, 128]])
        self.segrows_b = sap(t, 0, 128, 768, [[1, nseg], [0, 128]])


KCUT = int(os.environ.get("KCUT", "99"))


class _Stop(Exception):
    pass


def build_program(stage=9, ntiles=33):
    nc = bass.Bass("TRN2", target_bir_lowering=False)
    es = ExitStack()
    with es:
        B = Builder(nc, es)
        S = B.S
        d = {}

        def din(name, shape, dt=F32):
            d[name] = B.dram_in(name, shape, dt).ap()
            return d[name]

        def dout(name, shape, dt=F32):
            d[name] = B.dram_out(name, shape, dt).ap()
            return d[name]

        din("xall", [T, 1024]); din("w_in", [1024, 6160]); din("w_branch", [1024, 1024]); din("w_out", [1024, 1024])
        din("nffn_bc", [128, 1024]); din("nfin_bc", [128, 1024]); din("wq", [1024, 2048]); din("subk", [128, 16, 128])
        din("peer_uv", [16384, 2048]); din("iota16", [128, 16])
        din("ident", [128, 128]); din("ones", [128, 128]); din("maskP", [128, 1028]); din("maskS", [128, 2848])
        din("rowmask0", [128, 2]); din("nmix", [128, 8]);
        din("convg", [128, 48]); din("convm", [128, 32]); din("rowp", [128, 24]); din("gn_g", [128, 128])
        din("gn_m", [128, 768]); din("sS", [16, 4, 128, 128]); din("sconv_g", [48, 1536]); din("sC", [16, 4, 128, 128])
        din("sn", [16, 512]); din("m0s", [128, 4]); din("sconv_m", [48, 1024])
        dout("y_p", [2048, 1024]); dout("y_s", [128, 1024]); dout("p_S", [4, 128, 128]); dout("p_conv", [3, 1536])
        dout("p_C", [4, 128, 128]); dout("p_n", [4, 128]); dout("p_m", [1, 4]); dout("p_mconv", [3, 1024])
        dout("s_S", [16, 4, 128, 128]); dout("s_conv", [48, 1536]); dout("s_C", [16, 4, 128, 128]); dout("s_n", [16, 512])
        dout("s_m", [16, 4]); dout("s_mconv", [48, 1024])
        oT_d = nc.dram_tensor("oT_d", [1024, T], BF16, kind="Internal").ap()
        oT_res = Buf(None, "oT_d")

        ident = B.sb("ident", [128, 128]); ones = B.sb("ones", [128, 128]); cst = B.sb("cst", [128, 8])
        ps = [B.ps("b%d" % i) for i in range(8)]
        xnT = B.sb("xnT", [128, 8, T], BF16)
        es_mix = ExitStack()
        B.aes = es_mix
        mP = B.sb("maskP", [128, 1028]); mS = B.sb("maskS", [128, 2848]); rm0 = B.sb("rm0", [128, 2])
        nmix = B.sb("nmix", [128, 8]); convg = B.sb("convg", [128, 48]); convm = B.sb("convm", [128, 32])
        rowp = B.sb("rowp", [128, 24]); gn_g = B.sb("gn_g", [128, 128]); gn_m = B.sb("gn_m", [128, 768])
        m0s = B.sb("m0s", [128, 4]); prm = B.sb("prm", [128, 24])
        for b_, nm in ((ident, "ident"), (ones, "ones"), (mP, "maskP"), (mS, "maskS"), (rm0, "rowmask0"), (nmix, "nmix"),
                       (convg, "convg"), (convm, "convm"), (rowp, "rowp"), (gn_g, "gn_g"), (gn_m, "gn_m"), (m0s, "m0s")):
            B.load(b_[:], d[nm], b_)
        MP = MaskSet(mP, 2)
        MS = MaskSet(mS, 16)
        B.memset("pool", cst[:, 0:1], EPS, writes=[cst])
        B.memset("pool", cst[:, 1:2], 1.0, writes=[cst])
        B.memset("pool", cst[:, 2:3], -0.5 * float(np.log(128.0)), writes=[cst])
        B.memset("pool", cst[:, 3:4], 0.0, writes=[cst])
        c_eps, c_one, c_lnq, c_zero = cst[:, 0:1], cst[:, 1:2], cst[:, 2:3], cst[:, 3:4]
        rp3 = rowp.t[:].rearrange("p (t f) -> p t f", f=4)
        pr3 = prm.t[:].rearrange("p (t f) -> p t f", f=4)
        B.act(pr3[:, :, 0:1], rp3[:, :, 0:1], AF.Exp, reads=[rowp], writes=[prm])
        B.ts("dve", pr3[:, :, 0:1], pr3[:, :, 0:1], -1.0, None, ALU.mult, reads=[prm], writes=[prm])
        B.cp("dve", pr3[:, :, 1:3], rp3[:, :, 1:3], reads=[rowp], writes=[prm])
        B.ts("dve", pr3[:, :, 3:4], rp3[:, :, 3:4], -1.0, None, ALU.mult, reads=[rowp], writes=[prm])

        uvb = nc.dram_tensor("uvb", [16384, 2048], BF16, kind="Internal").ap()
        uvb_res = Buf(None, "uvb")
        cvd = [B.sb("cvd%d" % i, [128, 1]) for i in range(4)]
        NCVD = 64
        rws = 16384 // NCVD
        for i in range(NCVD):
            S.dma("pool", (lambda i: (lambda e: e.dma_start(out=uvb[i * rws:(i + 1) * rws, :], in_=d["peer_uv"][i * rws:(i + 1) * rws, :])))(i),
                  reads=[], writes=[uvb_res], semres=cvd[i % 4])

        EX1 = B.sb("EX1", [128, 2048]); EX2 = B.sb("EX2", [128, 2048]); EX3 = B.sb("EX3", [128, 2048])
        st = [B.sb("st%d" % i, [128, 4]) for i in range(2)]

        class _V:
            def __init__(self, buf, c0):
                self.buf = buf; self.c0 = c0
            def ap(self, r0, r1, a, b):
                return self.buf.t[r0:r1, self.c0 + a:self.c0 + b]
        xtv = [_V(EX1, 0), _V(EX1, 1024)]
        xsv = [_V(EX2, 0), _V(EX2, 1024)]
        junkv = _V(EX3, 0)

        def rstd_col(stb, rows, width):
            B.act(stb[0:rows, 1:2], stb[0:rows, 0:1], AF.Ln, reads=[stb, cst], writes=[stb], scale=1.0 / width, bias=c_eps[0:rows])
            B.act(stb[0:rows, 2:3], stb[0:rows, 1:2], AF.Exp, reads=[stb], writes=[stb], scale=-0.5)

        ntile = (T + 127) // 128
        for i in range(ntile):
            t0 = i * 128
            rows = min(128, T - t0)
            sl = i % 2
            B.load(xtv[sl].ap(0, rows, 0, 1024), d["xall"][t0:t0 + rows, :], EX1)
            B.act(junkv.ap(0, rows, 0, 1024), xtv[sl].ap(0, rows, 0, 1024), AF.Square, reads=[EX1], writes=[EX3, st[sl]], accum_out=st[sl][0:rows, 0:1])
            rstd_col(st[sl], rows, 1024)
            B.act(xsv[sl].ap(0, rows, 0, 1024), xtv[sl].ap(0, rows, 0, 1024), AF.Copy, reads=[EX1, st[sl]], writes=[EX2], scale=st[sl][0:rows, 2:3])
            for c in range(8):
                pb = ps[2 * sl + c // 4]
                B.tr(pb[:, (c % 4) * 128:(c % 4) * 128 + rows], xsv[sl].ap(0, rows, c * 128, (c + 1) * 128), ident[0:rows, 0:rows],
                     reads=[EX2, ident], writes=[pb])
            for h in range(2):
                pb = ps[2 * sl + h]
                B.tt("dve", xnT[:, 4 * h:4 * h + 4, t0:t0 + rows], sap(pb.t, 0, 128, 0, [[128, 4], [1, rows]]),
                     sap(nmix.t, 0, 128, 4 * h, [[1, 4], [0, rows]]), ALU.mult, reads=[pb, nmix], writes=[xnT])

        if stage <= 1:
            S.emit()
            es_mix.close()
            return nc
        Wb = B.sb("Wb", [128, 8, 1032], BF16)
        Wst = [B.sb("Wst0", [128, 1032])]
        E = B.sb("E", [128, 2243])
        Y = B.sb("Y", [128, T])
        Rt = B.sb("Rt", [128, 512])
        CO = [B.sb("CO%d" % i, [48, 256]) for i in range(2)]
        qT = B.sb("qT", [128, 2, T]); kT = B.sb("kT", [128, 2, T]); vT = B.sb("vT", [128, 2, T])
        B.memset("pool", E[:, 0:3], 0.0, writes=[E])
        gc = B.sb("gc", [128, 48])
        D2 = B.sb("D2", [128, 256]); XX = B.sb("XX", [128, 256]); DD = B.sb("DD", [128, 256]); Xb = B.sb("Xb", [128, 128])
        t1 = B.sb("t1", [128, 128]); t2 = B.sb("t2", [128, 128]); t3 = B.sb("t3", [128, 128]); t4 = B.sb("t4", [128, 128])
        Mk = [B.sb("Mk%d" % i, [128, 128]) for i in range(2)]; Nk = [B.sb("Nk%d" % i, [128, 128]) for i in range(2)]
        IM = B.sb("IM", [128, 128]); TT = [B.sb("TT%d" % i, [128, 128]) for i in range(2)]
        qkmT = B.sb("qkmT", [128, 128]); QT = B.sb("QT", [128, 128]); KT = B.sb("KT", [128, 128]); VT = B.sb("VT", [128, 128])
        bv = B.sb("bv", [128, 128]); bgk = B.sb("bgk", [128, 128]); ke = B.sb("ke", [128, 128])
        dsb = B.sb("dsb", [128, 128]); o1 = B.sb("o1", [128, 128]); ob = B.sb("ob", [128, 128]); on = B.sb("on", [128, 128])
        ez = B.sb("ez", [128, 128]); zsb = B.sb("zsb", [128, 128]); og = B.sb("og", [128, 128]); jk2 = B.sb("jk2", [128, 128])
        obf = [B.sb("obf%d" % i, [128, 128], BF16) for i in range(2)]
        GEt = B.sb("GEt", [128, 16]); gebc = B.sb("gebc", [128, 16])
        ST = B.sb("ST", [128, 2048]); Sio = B.sb("Sio", [128, 2048])
        nT = B.sb("nT", [128, 16]); mrow = B.sb("mrow", [128, 1]); nio = B.sb("nio", [16, 512]); ntm = B.sb("ntm", [16, 128])
        msm = B.sb("msm", [128, 4])

        def G(i):
            return gc[:, i:i + 1]

        obf_ctr = [0]
        co_ctr = [0]
        wst_ctr = [0]

        def load_wpair(base, j):
            rngs = [(base + 2 * j * 128, 256, 0), (base + 512 + 2 * j * 128, 256, 256), (base + 1024 + 2 * j * 128, 256, 512),
                    (base + 1536, 8, 768), (base + 1544 + 2 * j * 128, 256, 776)]
            for c in range(8):
                ws = Wst[0]
                wst_ctr[0] += 1
                for (c0, n, o) in rngs:
                    B.load(ws[:, o:o + n], d["w_in"][c * 128:(c + 1) * 128, c0:c0 + n], ws)
                eng = ("pool", "dve")[c % 2]
                B.cp(eng, Wb[:, c, :], ws[:, :], reads=[ws], writes=[Wb])

        def project(e, evac):
            for bi, (t0, n) in enumerate(NB):
                pb = ps[bi % 4]
                for c in range(8):
                    B.mm(pb[:, 0:n], Wb[:, c, e * 128:(e + 1) * 128], xnT[:, c, t0:t0 + n], start=(c == 0), stop=(c == 7),
                         reads=[Wb, xnT], writes=[pb])
                evac(bi, pb, t0, n)

        Esamp = sap(E.t, 0, 128, 2067, [[11, 16], [1, 11]])

        def evac_to_E(bi, pb, t0, n):
            if bi < 4:
                B.cp("act", E[:, 3 + t0:3 + t0 + n], pb[:, 0:n], reads=[pb], writes=[E])
            else:
                B.cp("act", E[:, 3 + 2048:3 + 2064], pb[:, 0:16], reads=[pb], writes=[E])
                B.cp("act", sap(E.t, 0, 128, 2067 + 3, [[11, 16], [1, 8]]), sap(pb.t, 0, 128, 16, [[8, 16], [1, 8]]), reads=[pb], writes=[E])

        def conv_chunk(convw, cc, hist_cols, out_state_p, out_state_s, ch0):
            pb = ps[4]
            B.tr(pb[:, 0:48], Sio[0:48, hist_cols:hist_cols + 128], ident[0:48, 0:48], reads=[Sio, ident], writes=[pb])
            B.cp("act", sap(E.t, 0, 128, 2067, [[11, 16], [1, 3]]), sap(pb.t, 0, 128, 0, [[3, 16], [1, 3]]), reads=[pb], writes=[E])
            w = lambda jj: convw[:, cc * 4 + jj:cc * 4 + jj + 1]
            B.ts("dve", Y[:, 0:T_P], E[:, 0:T_P], w(0), None, ALU.mult, reads=[E, convw], writes=[Y])
            for jj in range(1, 4):
                B.stt(Y[:, 0:T_P], E[:, jj:jj + T_P], w(jj), Y[:, 0:T_P], ALU.mult, ALU.add, reads=[E, convw, Y], writes=[Y])
            Ys = sap(Y.t, 0, 128, T_P, [[8, 16], [1, 8]])
            B.ts("dve", Ys, sap(E.t, 0, 128, 2067, [[11, 16], [1, 8]]), w(0), None, ALU.mult, reads=[E, convw], writes=[Y])
            for jj in range(1, 4):
                B.stt(Ys, sap(E.t, 0, 128, 2067 + jj, [[11, 16], [1, 8]]), w(jj), Ys, ALU.mult, ALU.add, reads=[E, convw, Y], writes=[Y])
            co = CO[co_ctr[0] % 2]
            co_ctr[0] += 1
            pb2 = ps[5]
            B.tr(pb2[0:3, 0:128], E[:, 2064:2067], ident[:, :], reads=[E, ident], writes=[pb2])
            B.cp("pool", sap(Rt.t, 0, 128, 0, [[3, 16], [1, 3]]), sap(E.t, 0, 128, 2067 + 8, [[11, 16], [1, 3]]), reads=[E], writes=[Rt])
            B.tr(pb2[0:48, 128:256], Rt[:, 0:48], ident[:, :], reads=[Rt, ident], writes=[pb2])
            B.cp("act", co[0:3, 0:128], pb2[0:3, 0:128], reads=[pb2], writes=[co])
            B.cp("act", co[0:48, 128:256], pb2[0:48, 128:256], reads=[pb2], writes=[co])
            B.store(out_state_p[0:3, ch0:ch0 + 128], co[0:3, 0:128], co)
            B.store(out_state_s[0:48, ch0:ch0 + 128], co[0:48, 128:256], co)

        def l2norm_to(dst3, hh, lnbias):
            B.tt("pool", E[:, 0:T], Y[:, :], Y[:, :], ALU.mult, reads=[Y], writes=[E])
            for bi, (t0, n) in enumerate(NB):
                pb = ps[bi % 4]
                B.mm(pb[:, 0:n], ones[:, :], E[:, t0:t0 + n], reads=[ones, E], writes=[pb])
                B.act(Rt[:, 0:n], pb[:, 0:n], AF.Ln, reads=[pb, cst], writes=[Rt], bias=c_eps)
                B.act(Rt[:, 0:n], Rt[:, 0:n], AF.Exp, reads=[Rt, cst], writes=[Rt], scale=-0.5, bias=lnbias)
                B.tt("dve", dst3[:, hh, t0:t0 + n], Y[:, t0:t0 + n], Rt[:, 0:n], ALU.mult, reads=[Y, Rt], writes=[dst_buf[0]])
            B.memset("pool", E[:, 0:3], 0.0, writes=[E])

        dst_buf = [None]

        def run_tile(kind, M, groups, qa, ka, va, ty, hl_of, STv, first_chunk, m_levels, out_rows, mx, gain_ap):
            nseg = M.nseg
            PR = lambda f: prm[:, ty * 4 + f:ty * 4 + f + 1]
            pg = ps[0]
            for (r0, nr, tk, hg, hl) in groups:
                for c in range(8):
                    B.mm(pg[r0:r0 + nr, 0:2], xnT[:, c, tk:tk + nr], sap(Wb.t, 0, 128, c * 1032 + 768 + hg, [[4, 2]]), start=(c == 0), stop=(c == 7),
                         reads=[xnT, Wb], writes=[pg])
            if KCUT <= 1:
                raise _Stop()
            pZ = ps[7]
            for (r0, nr, tk, hg, hl) in groups:
                for c in range(8):
                    B.mm(pZ[r0:r0 + nr, 0:128], xnT[:, c, tk:tk + nr], Wb[:, c, 776 + hl * 128:776 + (hl + 1) * 128], start=(c == 0), stop=(c == 7),
                         reads=[xnT, Wb], writes=[pZ])
            B.cp("dve", zsb[:, :], pZ[:, 0:128], reads=[pZ], writes=[zsb])
            B.act(ez[:, :], zsb[:, :], AF.Exp, reads=[zsb], writes=[ez], scale=-1.0)
            B.ts("pool", ez[:, :], ez[:, :], 1.0, None, ALU.add, reads=[ez], writes=[ez])
            S.op("dve", lambda e: e.reciprocal(ez[:, :], ez[:, :]), [ez], [ez])
            if kind == "gdn":
                B.act(G(0), pg[:, 0:1], AF.Exp, reads=[pg, prm], writes=[gc], bias=PR(1))
                B.act(G(1), G(0), AF.Ln, reads=[gc, cst], writes=[gc], bias=c_one)
                if first_chunk:
                    B.ts("dve", G(2), G(1), PR(0), rm0[:, 0:1], ALU.mult, ALU.mult, reads=[gc, prm, rm0], writes=[gc])
                else:
                    B.ts("dve", G(2), G(1), PR(0), None, ALU.mult, reads=[gc, prm], writes=[gc])
                B.act(G(3), pg[:, 1:2], AF.Exp, reads=[pg], writes=[gc], scale=-1.0)
                B.ts("dve", G(3), G(3), 1.0, None, ALU.add, reads=[gc], writes=[gc])
                S.op("dve", lambda e: e.reciprocal(G(4), G(3)), [gc], [gc])
                if first_chunk:
                    B.ts("dve", G(4), G(4), rm0[:, 0:1], None, ALU.mult, reads=[gc, rm0], writes=[gc])
                lg = G(2)
            else:
                B.act(G(5), pg[:, 0:1], AF.Identity, reads=[pg, prm], writes=[gc], bias=PR(2))
                if first_chunk:
                    B.ts("dve", G(5), G(5), rm0[:, 1:2], None, ALU.add, reads=[gc, rm0], writes=[gc])
                B.act(G(0), pg[:, 1:2], AF.Exp, reads=[pg, prm], writes=[gc], scale=-1.0, bias=PR(3))
                B.act(G(1), G(0), AF.Ln, reads=[gc, cst], writes=[gc], bias=c_one)
                if first_chunk:
                    B.ts("dve", G(2), G(1), -1.0, rm0[:, 0:1], ALU.mult, ALU.mult, reads=[gc, rm0], writes=[gc])
                else:
                    B.ts("dve", G(2), G(1), -1.0, None, ALU.mult, reads=[gc], writes=[gc])
                lg = G(2)
            if KCUT <= 2:
                raise _Stop()
            B.mm(pg[:, 2:3], M.MinclT, lg, reads=[M.buf, gc], writes=[pg])
            B.mm(pg[:, 3:4], M.SegB, lg, reads=[M.buf, gc], writes=[pg])
            B.cp("act", gc[:, 6:8], pg[:, 2:4], reads=[pg], writes=[gc])
            if KCUT <= 3:
                raise _Stop()
            pR = ps[6]
            B.cp("pool", KT[:, :], ka, reads=[kT], writes=[KT])
            B.cp("act", VT[:, :], va, reads=[vT], writes=[VT])
            B.cp("pool", QT[:, :], qa, reads=[qT], writes=[QT])
            B.tr(pR[:, 0:128], KT[:, :], ident[:, :], reads=[KT, ident], writes=[pR])
            B.tr(pR[:, 128:256], VT[:, :], ident[:, :], reads=[VT, ident], writes=[pR])
            B.tt("pool", sap(EX2.t, 0, 128, 0, [[128, nseg], [1, 128]]), sap(QT.t, 0, 128, 0, [[0, nseg], [1, 128]]), M.segT, ALU.mult,
                 reads=[QT, M.buf], writes=[EX2])
            if KCUT <= 4:
                raise _Stop()
            pB = ps[1]
            pKQ = ps[2]
            p5 = ps[5]
            if kind == "gdn":
                B.act(G(8), G(6), AF.Exp, reads=[gc], writes=[gc])
                B.act(G(9), G(7), AF.Exp, reads=[gc], writes=[gc])
                B.tt("dve", G(10), G(7), G(6), ALU.subtract, reads=[gc], writes=[gc])
                B.act(G(10), G(10), AF.Exp, reads=[gc], writes=[gc])
                B.tt("dve", G(11), G(4), G(8), ALU.mult, reads=[gc], writes=[gc])
                B.ts("pool", D2[:, 0:128], ident[:, :], G(6), None, ALU.mult, reads=[ident, gc], writes=[D2])
                B.ts("pool", D2[:, 128:256], ident[:, :], G(4), None, ALU.mult, reads=[ident, gc], writes=[D2])
                B.mm(pB[:, 0:256], ones[:, :], D2[:, :], reads=[ones, D2], writes=[pB])
                B.ts("dve", Xb[:, :], pB[:, 0:128], G(6), None, ALU.subtract, reads=[pB, gc], writes=[Xb])
                B.ts("dve", XX[:, 0:128], Xb[:, :], 0.0, -1.0, ALU.max, ALU.mult, reads=[Xb], writes=[XX])
                B.ts("pool", XX[:, 128:256], Xb[:, :], 0.0, None, ALU.min, reads=[Xb], writes=[XX])
                B.act(DD[:, :], XX[:, :], AF.Exp, reads=[XX], writes=[DD])
                if KCUT <= 5:
                    raise _Stop()
                B.mm(pKQ[:, 0:128], KT[:, :], KT[:, :], reads=[KT], writes=[pKQ])
                B.mm(pKQ[:, 128:256], KT[:, :], QT[:, :], reads=[KT, QT], writes=[pKQ])
                B.tt("dve", t1[:, :], pKQ[:, 0:128], DD[:, 0:128], ALU.mult, reads=[pKQ, DD], writes=[t1])
                B.stt(Mk[0][:, :], t1[:, :], G(4), M.MstrictNeg, ALU.mult, ALU.mult, reads=[t1, gc, M.buf], writes=[Mk[0]])
                B.tt("dve", t2[:, :], pKQ[:, 0:128], DD[:, 128:256], ALU.mult, reads=[pKQ, DD], writes=[t2])
                B.tt("dve", t3[:, :], t2[:, :], pB[:, 128:256], ALU.mult, reads=[t2, pB], writes=[t3])
                B.tt("pool", Nk[0][:, :], t3[:, :], M.MstrictTNeg, ALU.mult, reads=[t3, M.buf], writes=[Nk[0]])
                B.tt("dve", t4[:, :], pKQ[:, 128:256], DD[:, 128:256], ALU.mult, reads=[pKQ, DD], writes=[t4])
                B.tt("pool", qkmT[:, :], t4[:, :], M.MinclT, ALU.mult, reads=[t4, M.buf], writes=[qkmT])
                B.tt("pool", TT[0][:, :], Nk[0][:, :], ident[:, :], ALU.add, reads=[Nk[0], ident], writes=[TT[0]])
                if KCUT <= 6:
                    raise _Stop()
                pc = ps[3]
                pt = ps[4]
                cur = 0
                for k in range(1, m_levels + 1):
                    nxt = 1 - cur
                    B.mm(pc[:, 0:128], Nk[cur][:, :], Mk[cur][:, :], reads=[Nk[cur], Mk[cur]], writes=[pc])
                    if k < m_levels:
                        B.mm(pc[:, 128:256], Mk[cur][:, :], Nk[cur][:, :], reads=[Nk[cur], Mk[cur]], writes=[pc])
                    B.tt("dve", IM[:, :], pc[:, 0:128], ident[:, :], ALU.add, reads=[pc, ident], writes=[IM])
                    if k < m_levels:
                        B.cp("dve", Mk[nxt][:, :], pc[:, 0:128], reads=[pc], writes=[Mk[nxt]])
                        B.cp("dve", Nk[nxt][:, :], pc[:, 128:256], reads=[pc], writes=[Nk[nxt]])
                    B.mm(pt[:, 0:128], IM[:, :], TT[cur][:, :], reads=[IM, TT[cur]], writes=[pt])
                    B.cp("dve", TT[nxt][:, :], pt[:, 0:128], reads=[pt], writes=[TT[nxt]])
                    cur = nxt
                if KCUT <= 7:
                    raise _Stop()
                TTf = TT[cur]
                B.ts("dve", bv[:, :], pR[:, 128:256], G(4), None, ALU.mult, reads=[pR, gc], writes=[bv])
                B.ts("dve", bgk[:, :], pR[:, 0:128], G(11), None, ALU.mult, reads=[pR, gc], writes=[bgk])
                B.ts("dve", ke[:, :], pR[:, 0:128], G(10), None, ALU.mult, reads=[pR, gc], writes=[ke])
                B.mm(p5[:, 0:128], bgk[:, :], TTf[:, :], reads=[bgk, TTf], writes=[p5])
                B.stt(sap(EX1.t, 0, 128, 0, [[128, nseg], [1, 128]]), sap(p5.t, 0, 128, 0, [[0, nseg], [1, 128]]), -1.0, M.segT,
                      ALU.mult, ALU.mult, reads=[p5, M.buf], writes=[EX1])
                if KCUT <= 8:
                    raise _Stop()
                B.mm(p5[:, 128:256], TTf[:, :], bv[:, :], start=True, stop=False, reads=[TTf, bv], writes=[p5])
                for b in range(nseg):
                    B.mm(p5[:, 128:256], EX1[:, b * 128:(b + 1) * 128], STv[:, b * 128:(b + 1) * 128], start=False, stop=(b == nseg - 1),
                         reads=[EX1, ST], writes=[p5])
                for b in range(nseg):
                    B.mm(p5[:, 256:384], EX2[:, b * 128:(b + 1) * 128], STv[:, b * 128:(b + 1) * 128], start=(b == 0), stop=(b == nseg - 1),
                         reads=[EX2, ST], writes=[p5])
                B.cp("dve", dsb[:, :], p5[:, 128:256], reads=[p5], writes=[dsb])
                B.tt("dve", sap(EX3.t, 0, 128, 0, [[128, nseg], [1, 128]]), sap(p5.t, 0, 128, 128, [[0, nseg], [1, 128]]), M.segrows_b,
                     ALU.mult, reads=[p5, M.buf], writes=[EX3])
                B.mm(p5[:, 384:512], qkmT[:, :], dsb[:, :], reads=[qkmT, dsb], writes=[p5])
                B.ts("dve", o1[:, :], p5[:, 256:384], G(8), None, ALU.mult, reads=[p5, gc], writes=[o1])
                B.tt("dve", ob[:, :], o1[:, :], p5[:, 384:512], ALU.add, reads=[o1, p5], writes=[ob])
                lhs_state = ke
                dec_col = G(9)
            else:
                B.tt("dve", G(12), G(5), G(6), ALU.subtract, reads=[gc], writes=[gc])
                B.tt("dve", G(13), G(12), G(7), ALU.add, reads=[gc], writes=[gc])
                B.ts("pool", D2[:, 0:128], ident[:, :], G(12), None, ALU.mult, reads=[ident, gc], writes=[D2])
                B.ts("pool", D2[:, 128:256], ident[:, :], G(13), None, ALU.mult, reads=[ident, gc], writes=[D2])
                B.mm(pB[:, 0:256], ones[:, :], D2[:, :], reads=[ones, D2], writes=[pB])
                B.stt(Xb[:, :], pB[:, 0:128], G(6), M.MaddIncl, ALU.add, ALU.add, reads=[pB, gc, M.buf], writes=[Xb])
                S.op("dve", lambda e: e.tensor_reduce(G(14), Xb[:, :], AX.X, ALU.max), [Xb], [gc])
                B.tt("dve", t1[:, :], pB[:, 128:256], M.MaddSeg, ALU.add, reads=[pB, M.buf], writes=[t1])
                S.op("dve", lambda e: e.tensor_reduce(G(15), t1[:, :], AX.X, ALU.max), [t1], [gc])
                B.tt("dve", G(16), G(6), mrow[:, 0:1], ALU.add, reads=[gc, mrow], writes=[gc])
                B.tt("dve", G(17), G(16), G(14), ALU.max, reads=[gc], writes=[gc])
                B.ts("dve", G(18), G(17), -1.0, None, ALU.mult, reads=[gc], writes=[gc])
                B.tt("dve", G(19), G(16), G(17), ALU.subtract, reads=[gc], writes=[gc])
                B.act(G(19), G(19), AF.Exp, reads=[gc], writes=[gc])
                B.act(t2[:, :], Xb[:, :], AF.Exp, reads=[Xb, gc], writes=[t2], bias=G(18))
                B.mm(pKQ[:, 0:128], QT[:, :], KT[:, :], reads=[KT, QT], writes=[pKQ])
                B.stt(t3[:, :], t2[:, :], 1.0, pKQ[:, 0:128], ALU.mult, ALU.mult, reads=[t2, pKQ], writes=[t3, gc], accum_out=G(20))
                pc = ps[3]
                B.tr(pc[:, 0:128], t3[:, :], ident[:, :], reads=[t3, ident], writes=[pc])
                B.cp("dve", t4[:, :], pc[:, 0:128], reads=[pc], writes=[t4])
                B.cp("dve", bv[:, :], pR[:, 128:256], reads=[pR], writes=[bv])
                B.tt("dve", G(21), G(7), mrow[:, 0:1], ALU.add, reads=[gc, mrow], writes=[gc])
                B.tt("dve", G(22), G(21), G(15), ALU.max, reads=[gc], writes=[gc])
                B.tt("dve", G(23), G(21), G(22), ALU.subtract, reads=[gc], writes=[gc])
                B.act(G(23), G(23), AF.Exp, reads=[gc], writes=[gc])
                B.tt("dve", G(24), G(13), G(22), ALU.subtract, reads=[gc], writes=[gc])
                B.act(G(24), G(24), AF.Exp, reads=[gc], writes=[gc])
                B.ts("dve", ke[:, :], pR[:, 0:128], G(24), None, ALU.mult, reads=[pR, gc], writes=[ke])
                for b in range(nseg):
                    B.mm(p5[:, 256:384], EX2[:, b * 128:(b + 1) * 128], STv[:, b * 128:(b + 1) * 128], start=(b == 0), stop=(b == nseg - 1),
                         reads=[EX2, ST], writes=[p5])
                B.mm(p5[:, 384:512], t4[:, :], bv[:, :], reads=[t4, bv], writes=[p5])
                B.mm(pg[:, 32:32 + nseg], QT[:, :], nT[:, 0:nseg], reads=[QT, nT], writes=[pg])
                B.stt(jk2[:, 0:nseg], pg[:, 32:32 + nseg], 1.0, M.segrows, ALU.mult, ALU.mult, reads=[pg, M.buf], writes=[jk2, gc], accum_out=G(25))
                B.stt(G(26), G(25), G(19), G(20), ALU.mult, ALU.add, reads=[gc], writes=[gc])
                B.ts("dve", G(29), G(26), -1.0, None, ALU.mult, reads=[gc], writes=[gc])
                B.tt("dve", G(26), G(26), G(29), ALU.max, reads=[gc], writes=[gc])
                B.act(G(27), G(18), AF.Exp, reads=[gc], writes=[gc])
                B.tt("dve", G(26), G(26), G(27), ALU.max, reads=[gc], writes=[gc])
                S.op("dve", lambda e: e.reciprocal(G(28), G(26)), [gc], [gc])
                B.ts("dve", o1[:, :], p5[:, 256:384], G(19), None, ALU.mult, reads=[p5, gc], writes=[o1])
                B.tt("dve", ob[:, :], o1[:, :], p5[:, 384:512], ALU.add, reads=[o1, p5], writes=[ob])
                B.ts("dve", ob[:, :], ob[:, :], G(28), None, ALU.mult, reads=[ob, gc], writes=[ob])
                B.tt("dve", sap(EX3.t, 0, 128, 0, [[128, nseg], [1, 128]]), sap(bv.t, 0, 128, 0, [[0, nseg], [1, 128]]), M.segrows_b,
                     ALU.mult, reads=[bv, M.buf], writes=[EX3])
                lhs_state = ke
                dec_col = G(23)
            if KCUT <= 9:
                raise _Stop()
            B.ts("pool", GEt[:, 0:nseg], M.segfirst, dec_col, None, ALU.mult, reads=[M.buf, gc], writes=[GEt])
            B.mm(pg[:, 8:8 + nseg], ones[:, :], GEt[:, 0:nseg], reads=[ones, GEt], writes=[pg])
            B.cp("act", gebc[:, 0:nseg], pg[:, 8:8 + nseg], reads=[pg], writes=[gebc])
            B.tt("pool", sap(ST.t, 0, 128, 0, [[128, nseg], [1, 128]]), sap(ST.t, 0, 128, 0, [[128, nseg], [1, 128]]),
                 sap(gebc.t, 0, 128, 0, [[1, nseg], [0, 128]]), ALU.mult, reads=[ST, gebc], writes=[ST])
            B.act(jk2[:, :], ob[:, :], AF.Square, reads=[ob], writes=[jk2, gc], accum_out=G(30))
            B.act(G(31), G(30), AF.Ln, reads=[gc, cst], writes=[gc], scale=1.0 / 128, bias=c_eps)
            B.act(G(32), G(31), AF.Exp, reads=[gc], writes=[gc], scale=-0.5)
            B.stt(on[:, :], ob[:, :], G(32), gain_ap, ALU.mult, ALU.mult, reads=[ob, gc, gn_g, gn_m], writes=[on])
            if KCUT <= 10:
                raise _Stop()
            pZ = ps[7]
            if kind == "gdn":
                B.tt("dve", og[:, :], on[:, :], zsb[:, :], ALU.mult, reads=[on, zsb], writes=[og])
                B.tt("pool", og[:, :], og[:, :], ez[:, :], ALU.mult, reads=[og, ez], writes=[og])
            else:
                B.tt("pool", og[:, :], on[:, :], ez[:, :], ALU.mult, reads=[on, ez], writes=[og])
            B.tr(pZ[:, 128:256], og[:, :], ident[:, :], reads=[og, ident], writes=[pZ])
            of = obf[obf_ctr[0] % 2]
            obf_ctr[0] += 1
            B.cp("dve", of[:, :], pZ[:, 128:256], reads=[pZ], writes=[of])
            for (r0, nv, tk, hg) in (out_rows if os.environ.get("KSKIP", "") != "store" else []):
                S.dma("sp", (lambda o_, i_: (lambda e: e.dma_start(out=o_, in_=i_)))(oT_d[(mx * 4 + hg) * 128:(mx * 4 + hg + 1) * 128, tk:tk + nv], of[:, r0:r0 + nv]),
                      reads=[of], writes=[oT_res], semres=of)
            if KCUT <= 11:
                raise _Stop()
            ncols = nseg * 128
            banks = [ps[1]] if nseg == 2 else [ps[1], ps[2], ps[3], ps[4]]
            for bi in range((ncols + 511) // 512):
                w = min(512, ncols - bi * 512)
                B.mm(banks[bi][:, 0:w], lhs_state[:, :], EX3[:, bi * 512:bi * 512 + w], reads=[lhs_state, EX3], writes=[banks[bi]])
            for bi in range((ncols + 511) // 512):
                w = min(512, ncols - bi * 512)
                B.tt("dve", ST[:, bi * 512:bi * 512 + w], ST[:, bi * 512:bi * 512 + w], banks[bi][:, 0:w], ALU.add, reads=[ST, banks[bi]], writes=[ST])
            if kind == "mls":
                B.mm(pg[:, 64:64 + nseg], ke[:, :], M.segrows, reads=[ke, M.buf], writes=[pg])
                B.tt("dve", nT[:, 0:nseg], nT[:, 0:nseg], gebc[:, 0:nseg], ALU.mult, reads=[nT, gebc], writes=[nT])
                B.tt("dve", nT[:, 0:nseg], nT[:, 0:nseg], pg[:, 64:64 + nseg], ALU.add, reads=[nT, pg], writes=[nT])
                B.cp("dve", mrow[:, 0:1], G(22), reads=[gc], writes=[mrow])

        def store_states(nseg, dstS_fn):
            for b in range(nseg):
                pb = ps[1 + (b // 4) % 4]
                B.tr(pb[:, (b % 4) * 128:(b % 4 + 1) * 128], ST[:, b * 128:(b + 1) * 128], ident[:, :], reads=[ST, ident], writes=[pb])
                if b % 4 == 3 or b == nseg - 1:
                    g0 = (b // 4) * 4
                    w = (b - g0 + 1) * 128
                    B.cp("act", Sio[:, g0 * 128:g0 * 128 + w], pb[:, 0:w], reads=[pb], writes=[Sio])
            dstS_fn()

        try:
          for mx, kind in enumerate(("gdn", "mls")):
              base = 0 if kind == "gdn" else 2056
              convw = convg if kind == "gdn" else convm
              nconv = 1536 if kind == "gdn" else 1024
              out_p = d["p_conv"] if kind == "gdn" else d["p_mconv"]
              out_s = d["s_conv"] if kind == "gdn" else d["s_mconv"]
              for j in range(2):
                  B.load(Sio[0:48, 0:nconv], d["sconv_g" if kind == "gdn" else "sconv_m"], Sio)
                  load_wpair(base, j)
                  for e in range(6):
                      which = e // 2
                      hh = e % 2
                      dstb = (qT, kT, vT)[which]
                      dst_buf[0] = dstb
                      if kind == "mls" and which == 2:
                          def ev(bi, pb, t0, n, dstb=dstb, hh=hh):
                              B.cp("act", dstb[:, hh, t0:t0 + n], pb[:, 0:n], reads=[pb], writes=[dstb])
                          project(e, ev)
                          continue
                      project(e, evac_to_E)
                      cc = which * 4 + 2 * j + hh
                      conv_chunk(convw, cc, cc * 128, out_p, out_s, cc * 128)
                      if kind == "gdn" and which < 2:
                          B.act(Y[:, :], Y[:, :], AF.Silu, reads=[Y], writes=[Y])
                          l2norm_to(dstb, hh, c_lnq if which == 0 else c_zero)
                      elif kind == "mls" and which == 1:
                          B.act(Y[:, :], Y[:, :], AF.Silu, reads=[Y], writes=[Y])
                          B.ts("pool", dstb[:, hh, :], Y[:, :], float(128.0 ** -0.5), None, ALU.mult, reads=[Y], writes=[dstb])
                      else:
                          B.act(dstb[:, hh, :], Y[:, :], AF.Silu, reads=[Y], writes=[dstb])
                  if stage <= 2:
                      raise _Stop()
                  STp = ST
                  B.memset("pool", ST[:, 0:256], 0.0, writes=[ST])
                  if kind == "mls":
                      B.memset("pool", nT[:, 0:2], 0.0, writes=[nT])
                      B.memset("pool", mrow[:, :], 0.0, writes=[mrow])
                  for ci in range(ntiles):
                      if ci == 0:
                          tk, nv = 0, 16
                      else:
                          tk, nv = 16 + 64 * (ci - 1), 64
                      groups = [(0, 64, tk, 2 * j, 0), (64, 64, tk, 2 * j + 1, 1)]
                      qa = qT[:, 0:2, tk:tk + 64]; ka = kT[:, 0:2, tk:tk + 64]; va = vT[:, 0:2, tk:tk + 64]
                      outr = [(0, nv, tk, 2 * j), (64, nv, tk, 2 * j + 1)]
                      gain_ap = gn_g[:, :] if kind == "gdn" else gn_m[:, j * 128:(j + 1) * 128]
                      run_tile(kind, MP, groups, qa, ka, va, j, None, ST, ci == 0, 5, outr, mx, gain_ap)
                  if stage <= 3:
                      raise _Stop()
                  def dstP(j=j, kind=kind):
                      dS = d["p_S"] if kind == "gdn" else d["p_C"]
                      B.store(dS[2 * j:2 * j + 2].rearrange("h v k -> v h k"), sap(Sio.t, 0, 128, 0, [[128, 2], [1, 128]]), Sio)
                  store_states(2, dstP)
                  if kind == "mls":
                      pb = ps[5]
                      B.tr(pb[0:2, 0:128], nT[:, 0:2], ident[:, :], reads=[nT, ident], writes=[pb])
                      B.cp("act", ntm[0:2, :], pb[0:2, 0:128], reads=[pb], writes=[ntm])
                      B.store(d["p_n"][2 * j:2 * j + 2, :], ntm[0:2, :], ntm)
                      B.store(d["p_m"][0:1, 2 * j:2 * j + 1], mrow[0:1, 0:1], mrow)
                      B.store(d["p_m"][0:1, 2 * j + 1:2 * j + 2], mrow[64:65, 0:1], mrow)
                  if stage <= 4:
                      raise _Stop()
                  for hl in range(2):
                      hg = 2 * j + hl
                      dS_in = d["sS"] if kind == "gdn" else d["sC"]
                      B.load(sap(Sio.t, 0, 128, 0, [[128, 16], [1, 128]]), dS_in[:, hg].rearrange("b v k -> v b k"), Sio)
                      for b in range(16):
                          pb = ps[1 + (b // 4) % 4]
                          B.tr(pb[:, (b % 4) * 128:(b % 4 + 1) * 128], Sio[:, b * 128:(b + 1) * 128], ident[:, :], reads=[Sio, ident], writes=[pb])
                          if b % 4 == 3:
                              g0 = (b // 4) * 4
                              B.cp("act", ST[:, g0 * 128:g0 * 128 + 512], pb[:, 0:512], reads=[pb], writes=[ST])
                      if kind == "mls":
                          B.load(nio[:, :], d["sn"], nio)
                          pb = ps[5]
                          B.tr(pb[:, 0:16], nio[0:16, hg * 128:(hg + 1) * 128], ident[0:16, 0:16], reads=[nio, ident], writes=[pb])
                          B.cp("act", nT[:, 0:16], pb[:, 0:16], reads=[pb], writes=[nT])
                          B.cp("dve", mrow[:, 0:1], m0s[:, hg:hg + 1], reads=[m0s], writes=[mrow])
                      groups = [(0, 128, T_P, hg, hl)]
                      qa = qT[:, hl, T_P:T]; ka = kT[:, hl, T_P:T]; va = vT[:, hl, T_P:T]
                      outr = [(0, 128, T_P, hg)]
                      gain_ap = gn_g[:, :] if kind == "gdn" else gn_m[:, (2 + hg) * 128:(3 + hg) * 128]
                      run_tile(kind, MS, groups, qa, ka, va, 2 + hg, None, ST, False, 2, outr, mx, gain_ap)
                      def dstS(hg=hg, kind=kind):
                          dS = d["s_S"] if kind == "gdn" else d["s_C"]
                          B.store(dS[:, hg].rearrange("b v k -> v b k"), sap(Sio.t, 0, 128, 0, [[128, 16], [1, 128]]), Sio)
                      store_states(16, dstS)
                      if kind == "mls":
                          pb = ps[5]
                          B.tr(pb[0:16, 0:128], nT[:, 0:16], ident[:, :], reads=[nT, ident], writes=[pb])
                          B.cp("act", ntm[0:16, :], pb[0:16, 0:128], reads=[pb], writes=[ntm])
                          B.store(d["s_n"][:, hg * 128:(hg + 1) * 128], ntm[0:16, :], ntm)
                          B.cp("dve", msm[:, hg:hg + 1], mrow[:, 0:1], reads=[mrow], writes=[msm])
              if kind == "mls":
                  B.store(d["s_m"], bass.AP(msm.t, 0, [[32, 16], [1, 4]]), msm)

        except _Stop:
            pass
        if stage <= 5:
            S.emit()
            es_mix.close()
            return nc

        def dma(out, in_, reads, writes, semres, q="sp", is_output=False):
            S.dma(q, lambda e: e.dma_start(out=out, in_=in_), reads=reads, writes=writes, semres=semres, is_output=is_output)

        tiles = [(16 + 128 * i, 128 * i) for i in range(16)] + [(T_P, 2048)]
        if ntiles < 33:
            tiles = tiles[:2]

        S.barrier()
        es_mix.close()
        es_mg = ExitStack()
        B.aes = es_mg
        Hd = nc.dram_tensor("H_d", [2176, 1024], F32, kind="Internal").ap()
        Hd_res = Buf(None, "H_d")
        Wg = B.sb("Wg", [128, 8, 2048], BF16); wbr = B.sb("wbr", [128, 8, 1024], BF16); wout = B.sb("wout", [128, 8, 1024], BF16)
        wstg = [B.sb("wstg%d" % i, [128, 2048]) for i in range(2)]
        wc = 0
        for (dst, src, c0, ncol) in ((Wg, "w_in", 4112, 2048), (wbr, "w_branch", 0, 1024), (wout, "w_out", 0, 1024)):
            for k in range(8):
                ws = wstg[wc % 2]
                B.load(ws[:, 0:ncol], d[src][k * 128:(k + 1) * 128, c0:c0 + ncol], ws)
                B.cp(("pool", "dve")[wc % 2], dst[:, k, :], ws[:, 0:ncol], reads=[ws], writes=[dst])
                wc += 1
        OT = [B.sb("OT%d" % i, [128, 8, 128], BF16) for i in range(2)]
        XT = [B.sb("XT%d" % i, [128, 1024]) for i in range(2)]
        sgm = [B.sb("sgm%d" % i, [128, 256]) for i in range(2)]
        mm1 = B.sb("mm1", [128, 128]); mm2 = B.sb("mm2", [128, 128])
        mgT = [B.sb("mgT%d" % i, [128, 8, 128], BF16) for i in range(2)]
        Hs = [B.sb("Hs%d" % i, [128, 1024]) for i in range(2)]
        oT_v = oT_d.rearrange("(c p) t -> p c t", p=128)
        for ti, (tok0, hrow) in enumerate(tiles):
            sl = ti % 2
            dma(OT[sl][:, :, :], oT_v[:, :, tok0:tok0 + 128], [oT_res], [OT[sl]], OT[sl])
            B.load(XT[sl][:, :], d["xall"][tok0:tok0 + 128, :], XT[sl])
            for c in range(8):
                gA = ps[c % 2]
                yB = ps[2 + c % 2]
                for g in range(2):
                    for k in range(8):
                        B.mm(gA[:, g * 128:(g + 1) * 128], Wg[:, k, g * 1024 + c * 128:g * 1024 + (c + 1) * 128], xnT[:, k, tok0:tok0 + 128],
                             start=(k == 0), stop=(k == 7), reads=[Wg, xnT], writes=[gA])
                for g in range(2):
                    for h in range(4):
                        B.mm(yB[:, g * 128:(g + 1) * 128], wbr[:, g * 4 + h, c * 128:(c + 1) * 128], OT[sl][:, g * 4 + h, :],
                             start=(h == 0), stop=(h == 3), reads=[wbr, OT[sl]], writes=[yB])
                sgb = sgm[c % 2]
                B.act(sgb[:, :], gA[:, 0:256], AF.Sigmoid, reads=[gA], writes=[sgb])
                B.tt("dve", mm1[:, :], sgb[:, 0:128], yB[:, 0:128], ALU.mult, reads=[sgb, yB], writes=[mm1])
                B.tt("dve", mm2[:, :], sgb[:, 128:256], yB[:, 128:256], ALU.mult, reads=[sgb, yB], writes=[mm2])
                B.tt("pool", mgT[sl][:, c, :], mm1[:, :], mm2[:, :], ALU.add, reads=[mm1, mm2], writes=[mgT[sl]])
            for half in range(2):
                pb = ps[4 + half]
                for c in range(8):
                    B.mm(pb[:, 0:512], mgT[sl][:, c, :], wout[:, c, half * 512:(half + 1) * 512], start=(c == 0), stop=(c == 7),
                         reads=[mgT[sl], wout], writes=[pb])
                B.tt("dve", Hs[sl][:, half * 512:(half + 1) * 512], XT[sl][:, half * 512:(half + 1) * 512], pb[:, 0:512], ALU.add,
                     reads=[XT[sl], pb], writes=[Hs[sl]])
            dma(Hd[hrow:hrow + 128, :], Hs[sl][:, :], [Hs[sl]], [Hd_res], Hs[sl])
        if stage <= 6:
            S.emit()
            es_mg.close()
            return nc

        S.barrier()
        es_mg.close()
        es_pe = ExitStack()
        B.aes = es_pe
        wq = B.sb("wq", [128, 8, 2048], BF16); skT = B.sb("skT", [128, 16, 128], BF16)
        nffn = B.sb("nffn", [128, 1024]); nfin = B.sb("nfin", [128, 1024]); identb = B.sb("identb", [128, 128], BF16); iota16 = B.sb("iota16", [128, 16])
        es_tmp = ExitStack()
        B.aes = es_tmp
        skn = B.sb("skn", [128, 16, 128])
        wst2 = [B.sb("wst2_%d" % i, [128, 2048]) for i in range(2)]
        B.load(nffn[:, :], d["nffn_bc"], nffn)
        B.load(nfin[:, :], d["nfin_bc"], nfin)
        B.load(iota16[:, :], d["iota16"], iota16)
        B.cp("dve", identb[:, :], ident[:, :], reads=[ident], writes=[identb])
        for k in range(8):
            ws = wst2[k % 2]
            B.load(ws[:, :], d["wq"][k * 128:(k + 1) * 128, :], ws)
            B.cp(("pool", "dve")[k % 2], wq[:, k, :], ws[:, :], reads=[ws], writes=[wq])
        B.load(skn[:, :, :], d["subk"], skn)
        for hp in range(16):
            pb = ps[hp // 4]
            B.tr(pb[:, (hp % 4) * 128:(hp % 4 + 1) * 128], skn[:, hp, :], ident[:, :], reads=[skn, ident], writes=[pb])
            if hp % 4 == 3:
                g0 = hp - 3
                B.cp("dve", skT[:, g0:g0 + 4, :], sap(pb.t, 0, 128, 0, [[128, 4], [1, 128]]), reads=[pb], writes=[skT])
        S.barrier()
        es_tmp.close()
        B.aes = es_pe
        NBUF = 10
        Hs2 = [B.sb("Hs2_%d" % i, [128, 1024]) for i in range(2)]
        XN2 = [B.sb("XN2_%d" % i, [128, 1024]) for i in range(2)]
        XN2b = [B.sb("XN2b_%d" % i, [128, 1024], BF16) for i in range(2)]
        EIi = [B.sb("EIi%d" % i, [128, 128], I32) for i in range(2)]
        gate = [B.sb("gate%d" % i, [128, 128]) for i in range(2)]
        jkA = B.sb("jkA", [128, 256]); jkB = B.sb("jkB", [128, 1024]); jkC = B.sb("jkC", [128, 1024], BF16); stp = B.sb("stp", [128, 4]); stq = B.sb("stq", [128, 4])
        x2T = B.sb("x2T", [128, 8, 128], BF16); qTs = B.sb("qTs", [128, 16, 128], BF16)
        Ssb = B.sb("Ssb", [128, 2048]); SC2 = B.sb("SC2", [128, 2048]); cand = B.sb("cand", [128, 2048]); eq4 = B.sb("eq4", [128, 2048])
        Vv = B.sb("Vv", [128, 256]); Iu = B.sb("Iu", [128, 256], U32); If = B.sb("If", [128, 256])
        TS = B.sb("TS", [128, 128]); negm = B.sb("negm", [128, 8]); Zs = B.sb("Zs", [128, 8]); rZ = B.sb("rZ", [128, 8])
        EI = B.sb("EI", [128, 128]); egt = B.sb("egt", [128, 128])
        PU = B.sb("PU", [128, 128], U32); PA = B.sb("PA", [128, 128], U32); PB = B.sb("PB", [128, 128], U32)
        Af = B.sb("Af", [128, 128]); Bf = B.sb("Bf", [128, 128]); sel0 = B.sb("sel0", [128, 128]); sel1 = B.sb("sel1", [128, 128])
        AVs = [B.sb("AVs%d" % i, [128, 1]) for i in range(4)]
        GAs = [B.sb("GAs%d" % i, [128, 1]) for i in range(4)]
        Wts = [B.sb("Wts%d" % i, [128, 1]) for i in range(4)]
        dgb = [B.sb("dgb%d" % i, [128, 128], BF16) for i in range(4)]
        UG = [B.sb("UG%d" % i, [128, 2048], BF16) for i in range(NBUF)]
        YO = [B.sb("YO%d" % i, [128, 1024]) for i in range(2)]

        def subs(buf, n):
            return [Buf(buf.t, "%s_s%d" % (buf.r.name, i)) for i in range(n)]
        VvS = subs(Vv, 16); IuS = subs(Iu, 16); SC2S = subs(SC2, 16); TSS = subs(TS, 8); PUS = subs(PU, 8)

        def routing(ti):
            tok0, hrow = tiles[ti]
            sl = ti % 2
            H = Hs2[sl]
            X2 = XN2[sl]
            dma(H[:, :], Hd[hrow:hrow + 128, :], [Hd_res], [H], H)
            B.act(SC2[:, 0:1024], H[:, :], AF.Square, reads=[H], writes=SC2S[0:8] + [stp], accum_out=stp[:, 0:1])
            rstd_col(stp, 128, 1024)
            B.stt(X2[:, :], H[:, :], stp[:, 2:3], nffn[:, :], ALU.mult, ALU.mult, reads=[H, stp, nffn], writes=[X2])
            B.cp("pool", XN2b[sl][:, :], X2[:, :], reads=[X2], writes=[XN2b[sl]])
            yield
            for c in range(8):
                pb = ps[c // 4]
                B.tr(pb[:, (c % 4) * 128:(c % 4 + 1) * 128], X2[:, c * 128:(c + 1) * 128], ident[:, :], reads=[X2, ident], writes=[pb])
            for hh in range(2):
                B.cp("act", x2T[:, 4 * hh:4 * hh + 4, :], sap(ps[hh].t, 0, 128, 0, [[128, 4], [1, 128]]), reads=[ps[hh]], writes=[x2T])
                yield
            for g in range(4):
                pb = ps[2 + g % 2]
                for q4 in range(4):
                    hp = 4 * g + q4
                    for k in range(8):
                        B.mm(pb[:, q4 * 128:(q4 + 1) * 128], wq[:, k, hp * 128:(hp + 1) * 128], x2T[:, k, :], start=(k == 0), stop=(k == 7),
                             reads=[wq, x2T], writes=[pb])
                B.cp("act", qTs[:, 4 * g:4 * g + 4, :], sap(pb.t, 0, 128, 0, [[128, 4], [1, 128]]), reads=[pb], writes=[qTs])
                yield
            for hp in range(16):
                pb = ps[4 + (hp // 4) % 2]
                B.mm(pb[:, (hp % 4) * 128:(hp % 4 + 1) * 128], qTs[:, hp, :], skT[:, hp, :], reads=[qTs, skT], writes=[pb])
                if hp % 4 == 3:
                    g = hp // 4
                    B.cp("act", Ssb[:, g * 512:(g + 1) * 512], pb[:, 0:512], reads=[pb], writes=[Ssb])
                    yield
            yield "front_done"
            for g4 in range(4):
                hps = [4 * g4 + q for q in range(4)]
                seg = lambda hp: Ssb[:, hp * 128:(hp + 1) * 128]
                seg2 = lambda hp: SC2[:, hp * 128:(hp + 1) * 128]
                v0 = lambda hp: Vv[:, hp * 16:hp * 16 + 8]
                v1 = lambda hp: Vv[:, hp * 16 + 8:hp * 16 + 16]
                i0_ = lambda hp: Iu[:, hp * 16:hp * 16 + 8]
                i1_ = lambda hp: Iu[:, hp * 16 + 8:hp * 16 + 16]
                for hp in hps:
                    S.op("dve", (lambda a, b: (lambda e: e.max(a, b)))(v0(hp), seg(hp)), [Ssb], [VvS[hp]])
                yield
                for hp in hps:
                    S.op("dve", (lambda a, b, c_: (lambda e: e.max_index(a, b, c_)))(i0_(hp), v0(hp), seg(hp)), [Ssb, VvS[hp]], [IuS[hp]])
                yield
                for hp in hps:
                    S.op("dve", (lambda a, b, c_: (lambda e: e.match_replace(a, b, c_, NEG)))(seg2(hp), v0(hp), seg(hp)), [Ssb, VvS[hp]], [SC2S[hp]])
                yield
                for hp in hps:
                    S.op("dve", (lambda a, b: (lambda e: e.max(a, b)))(v1(hp), seg2(hp)), [SC2S[hp]], [VvS[hp]])
                yield
                for hp in hps:
                    S.op("dve", (lambda a, b, c_: (lambda e: e.max_index(a, b, c_)))(i1_(hp), v1(hp), seg2(hp)), [SC2S[hp], VvS[hp]], [IuS[hp]])
                yield
            B.cp("dve", If[:, :], Iu[:, :], reads=IuS, writes=[If])
            B.ts("dve", sap(If.t, 0, 128, 0, [[32, 8], [1, 16]]), sap(If.t, 0, 128, 0, [[32, 8], [1, 16]]), 128.0, None, ALU.mult, reads=[If], writes=[If])
            c4 = sap(cand.t, 0, 128, 0, [[256, 8], [16, 16], [1, 16]])
            B.tt("dve", c4, sap(Vv.t, 0, 128, 0, [[32, 8], [1, 16], [0, 16]]), sap(Vv.t, 0, 128, 16, [[32, 8], [0, 16], [1, 16]]), ALU.add,
                 reads=VvS, writes=[cand])
            yield
            for g4 in range(2):
                hs = [4 * g4 + q for q in range(4)]
                cs = lambda h: cand[:, h * 256:(h + 1) * 256]
                cs2 = lambda h: SC2[:, h * 256:(h + 1) * 256]
                t0_ = lambda h: TS[:, h * 16:h * 16 + 8]
                t1_ = lambda h: TS[:, h * 16 + 8:h * 16 + 16]
                p0_ = lambda h: PU[:, h * 16:h * 16 + 8]
                p1_ = lambda h: PU[:, h * 16 + 8:h * 16 + 16]
                for h in hs:
                    S.op("dve", (lambda a, b: (lambda e: e.max(a, b)))(t0_(h), cs(h)), [cand], [TSS[h]])
                yield
                for h in hs:
                    S.op("dve", (lambda a, b, c_: (lambda e: e.max_index(a, b, c_)))(p0_(h), t0_(h), cs(h)), [cand, TSS[h]], [PUS[h]])
                yield
                for h in hs:
                    S.op("dve", (lambda a, b, c_: (lambda e: e.match_replace(a, b, c_, NEG)))(cs2(h), t0_(h), cs(h)), [cand, TSS[h]], [SC2S[2 * h], SC2S[2 * h + 1]])
                yield
                for h in hs:
                    S.op("dve", (lambda a, b: (lambda e: e.max(a, b)))(t1_(h), cs2(h)), [SC2S[2 * h], SC2S[2 * h + 1]], [TSS[h]])
                yield
                for h in hs:
                    S.op("dve", (lambda a, b, c_: (lambda e: e.max_index(a, b, c_)))(p1_(h), t1_(h), cs2(h)), [SC2S[2 * h], SC2S[2 * h + 1], TSS[h]], [PUS[h]])
                yield
            S.op("dve", lambda e: e.tensor_single_scalar(PA[:, :], PU[:, :], 4, ALU.logical_shift_right), PUS, [PA])
            S.op("dve", lambda e: e.tensor_single_scalar(PB[:, :], PU[:, :], 15, ALU.bitwise_and), PUS, [PB])
            B.cp("dve", Af[:, :], PA[:, :], reads=[PA], writes=[Af])
            B.cp("dve", Bf[:, :], PB[:, :], reads=[PB], writes=[Bf])
            yield
            for (src, off, dst) in ((Af, 0, sel0), (Bf, 16, sel1)):
                B.tt("dve", sap(eq4.t, 0, 128, 0, [[16, 128], [1, 16]]), sap(src.t, 0, 128, 0, [[1, 128], [0, 16]]),
                     sap(iota16.t, 0, 128, 0, [[0, 128], [1, 16]]), ALU.is_equal, reads=[src, iota16], writes=[eq4])
                yield
                B.tt("dve", sap(eq4.t, 0, 128, 0, [[256, 8], [16, 16], [1, 16]]), sap(eq4.t, 0, 128, 0, [[256, 8], [16, 16], [1, 16]]),
                     sap(If.t, 0, 128, off, [[32, 8], [0, 16], [1, 16]]), ALU.mult, reads=[eq4, If], writes=[eq4])
                yield
                S.op("dve", (lambda d_: (lambda e: e.tensor_reduce(d_[:, :], sap(eq4.t, 0, 128, 0, [[16, 128], [1, 16]]), AX.X, ALU.add)))(dst),
                     [eq4], [dst])
                yield
            B.tt("dve", EI[:, :], sel0[:, :], sel1[:, :], ALU.add, reads=[sel0, sel1], writes=[EI])
            B.ts("dve", EI[:, :], EI[:, :], 16383.0, 0.0, ALU.min, ALU.max, reads=[EI], writes=[EI])
            B.cp("dve", EIi[sl][:, :], EI[:, :], reads=[EI], writes=[EIi[sl]])
            B.ts("dve", negm[:, :], sap(TS.t, 0, 128, 0, [[16, 8]]), -1.0, None, ALU.mult, reads=TSS, writes=[negm])
            yield
            for h in range(8):
                B.act(egt[:, h * 16:(h + 1) * 16], TS[:, h * 16:(h + 1) * 16], AF.Exp, reads=[TSS[h], negm], writes=[egt, Zs],
                      bias=negm[:, h:h + 1], accum_out=Zs[:, h:h + 1])
            S.op("dve", lambda e: e.reciprocal(rZ[:, :], Zs[:, :]), [Zs], [rZ])
            B.tt("dve", sap(gate[sl].t, 0, 128, 0, [[16, 8], [1, 16]]), sap(egt.t, 0, 128, 0, [[16, 8], [1, 16]]), sap(rZ.t, 0, 128, 0, [[1, 8], [0, 16]]),
                 ALU.mult, reads=[egt, rZ], writes=[gate[sl]])
            yield

        def drain(gen, n=None, until=None):
            if gen is None:
                return None
            try:
                if until is not None:
                    while next(gen) != until:
                        pass
                elif n is None:
                    while True:
                        next(gen)
                else:
                    for _ in range(n):
                        next(gen)
            except StopIteration:
                return None
            return gen

        ug_ctr = [0]
        nt_ = len(tiles)
        drain(routing(0))
        for ti in range(nt_):
            tok0, hrow = tiles[ti]
            sl = ti % 2
            H = Hs2[sl]
            nxt = routing(ti + 1) if ti + 1 < nt_ else None
            nxt = drain(nxt, until="front_done")
            pa = [ps[6], ps[7]]
            for col in range(128):
                ug = UG[ug_ctr[0] % NBUF]
                ug_ctr[0] += 1
                S.dma("pool", (lambda ug, col, sl: (lambda e: e.indirect_dma_start(out=ug[:, :], out_offset=None, in_=uvb[:, :],
                      in_offset=bass.IndirectOffsetOnAxis(ap=EIi[sl][:, col:col + 1], axis=0))))(ug, col, sl),
                      reads=[EIi[sl], uvb_res], writes=[ug], semres=ug)
                a4 = col % 4
                B.stt(jkC[:, :], ug[:, 0:1024], 1.0, XN2b[sl][:, :], ALU.mult, ALU.mult, reads=[ug, XN2b[sl]], writes=[AVs[a4]], accum_out=AVs[a4][:, 0:1])
                B.act(GAs[a4][:, :], AVs[a4][:, :], AF.Gelu, reads=[AVs[a4]], writes=[GAs[a4]])
                dg = dgb[a4]
                B.act(Wts[a4][:, :], GAs[a4][:, :], AF.Copy, reads=[GAs[a4], gate[sl]], writes=[Wts[a4]], scale=gate[sl][:, col:col + 1])
                B.act(dg[:, :], identb[:, :], AF.Copy, reads=[identb, Wts[a4]], writes=[dg], scale=Wts[a4][:, 0:1])
                for half in range(2):
                    B.mm(pa[half][:, 0:512], dg[:, :], ug[:, 1024 + half * 512:1024 + (half + 1) * 512], start=(col == 0), stop=(col == 127),
                         reads=[dg, ug], writes=[pa[half]])
                nxt = drain(nxt, 3)
            nxt = drain(nxt)
            yo = YO[sl]
            for half in range(2):
                B.tt("dve", yo[:, half * 512:(half + 1) * 512], H[:, half * 512:(half + 1) * 512], pa[half][:, 0:512], ALU.add,
                     reads=[H, pa[half]], writes=[yo])
            B.act(jkB[:, :], yo[:, :], AF.Square, reads=[yo], writes=[jkB, stq], accum_out=stq[:, 0:1])
            rstd_col(stq, 128, 1024)
            B.stt(yo[:, :], yo[:, :], stq[:, 2:3], nfin[:, :], ALU.mult, ALU.mult, reads=[yo, stq, nfin], writes=[yo])
            if ti < 16:
                B.store(d["y_p"][hrow:hrow + 128, :], yo[:, :], yo)
            else:
                B.store(d["y_s"][:, :], yo[:, :], yo)
        S.emit()
        es_pe.close()
    return nc


_PROG = {}


def _get_prog():
    if "nc" not in _PROG:
        import os
        _PROG["nc"] = build_program(int(os.environ.get("KSTAGE", "9")), int(os.environ.get("KNT", "33")))
    return _PROG["nc"]


def _core_inputs(inp, i, consts):
    f = lambda k: np.asarray(inp[k], dtype=np.float32)
    m = dict(consts)
    xs = f("x_sample")[16 * i:16 * i + 16].reshape(128, 1024)
    m["xall"] = np.ascontiguousarray(np.concatenate([f("meta_tokens"), f("x_prompt")[i], xs], axis=0))
    m["sS"] = np.ascontiguousarray(f("state_gdn_S")[0, 16 * i:16 * i + 16])
    m["sC"] = np.ascontiguousarray(f("state_mlstm_C")[0, 16 * i:16 * i + 16])
    m["sconv_g"] = np.ascontiguousarray(f("state_gdn_conv")[0, 16 * i:16 * i + 16].reshape(48, 1536))
    m["sconv_m"] = np.ascontiguousarray(f("state_mlstm_conv")[0, 16 * i:16 * i + 16].reshape(48, 1024))
    m["sn"] = np.ascontiguousarray(f("state_mlstm_n")[0, 16 * i:16 * i + 16].reshape(16, 512))
    m["m0s"] = np.ascontiguousarray(np.repeat(f("state_mlstm_m")[0, 16 * i:16 * i + 16], 8, axis=0))
    return m


def _consts(inp):
    f = lambda k: np.asarray(inp[k], dtype=np.float32)
    c = {}
    c["w_in"] = np.ascontiguousarray(f("w_in")[0])
    c["w_branch"] = np.ascontiguousarray(f("w_branch")[0])
    c["w_out"] = np.ascontiguousarray(f("w_out")[0])
    c["nffn_bc"] = np.ascontiguousarray(np.broadcast_to(f("norm_ffn")[0][None, :], (128, 1024)))
    c["nfin_bc"] = np.ascontiguousarray(np.broadcast_to(f("norm_final")[None, :], (128, 1024)))
    c["wq"] = np.ascontiguousarray(f("peer_w_query")[0])
    c["subk"] = np.ascontiguousarray(f("peer_sub_keys")[0].transpose(2, 0, 1, 3).reshape(128, 16, 128))
    c["iota16"] = np.ascontiguousarray(np.broadcast_to(np.arange(16, dtype=np.float32)[None, :], (128, 16)))
    c["peer_uv"] = np.ascontiguousarray(np.concatenate([f("peer_u")[0], f("peer_v")[0]], axis=1))
    c["ident"] = np.eye(128, dtype=np.float32)
    c["ones"] = np.ones((128, 128), np.float32)
    c["maskP"] = _maskset(2, 64)
    c["maskS"] = _maskset(16, 8)
    r = np.arange(128)
    valid = (r % 64) < 16
    c["rowmask0"] = np.ascontiguousarray(np.stack([valid.astype(np.float32), np.where(valid, 0.0, NEG).astype(np.float32)], axis=1))
    c["nmix"] = np.ascontiguousarray(f("norm_mix")[0].reshape(8, 128).T)
    c["convg"] = np.ascontiguousarray(f("conv_gdn")[0].T.reshape(12, 128, 4).transpose(1, 0, 2).reshape(128, 48))
    c["convm"] = np.ascontiguousarray(f("conv_mlstm")[0].T.reshape(8, 128, 4).transpose(1, 0, 2).reshape(128, 32))
    headof = np.zeros((6, 128), np.int64)
    for ty in range(2):
        headof[ty] = 2 * ty + (r // 64)
    for ty in range(2, 6):
        headof[ty] = ty - 2
    prs = [f("gdn_a_log")[0], f("gdn_dt_bias")[0], f("mlstm_i_bias")[0], f("mlstm_f_bias")[0]]
    rowp = np.zeros((128, 6, 4), np.float32)
    for ty in range(6):
        for k, p in enumerate(prs):
            rowp[:, ty, k] = p[headof[ty]]
    c["rowp"] = np.ascontiguousarray(rowp.reshape(128, 24))
    c["gn_g"] = np.ascontiguousarray(np.broadcast_to(f("gdn_out_norm")[0][None, :], (128, 128)))
    gm = f("mlstm_out_norm")[0]
    gn_m = np.zeros((128, 6, 128), np.float32)
    for ty in range(6):
        gn_m[:, ty, :] = gm[headof[ty]]
    c["gn_m"] = np.ascontiguousarray(gn_m.reshape(128, 768))
    return c


def kernel(**inputs):
    nc = _get_prog()
    consts = _consts(inputs)
    in_maps = [_core_inputs(inputs, i, consts) for i in range(8)]
    res = run_bass_kernel_spmd(nc, in_maps, core_ids=list(range(8)))
    R = res.results
    g = lambda k: [np.asarray(R[i][k], dtype=np.float32) for i in range(8)]
    y_p = np.stack(g("y_p"))
    y_s = np.concatenate([a.reshape(16, 8, 1024) for a in g("y_s")], 0)
    p_S = np.stack(g("p_S"))[None]
    p_conv = np.stack(g("p_conv"))[None]
    p_C = np.stack(g("p_C"))[None]
    p_n = np.stack(g("p_n"))[None]
    p_m = np.stack([a.reshape(4) for a in g("p_m")])[None]
    p_mconv = np.stack(g("p_mconv"))[None]
    s_S = np.concatenate(g("s_S"), 0)[None]
    s_conv = np.concatenate([a.reshape(16, 3, 1536) for a in g("s_conv")], 0)[None]
    s_C = np.concatenate(g("s_C"), 0)[None]
    s_n = np.concatenate([a.reshape(16, 4, 128) for a in g("s_n")], 0)[None]
    s_m = np.concatenate(g("s_m"), 0)[None]
    s_mconv = np.concatenate([a.reshape(16, 3, 1024) for a in g("s_mconv")], 0)[None]
    return (y_p, y_s, p_S, p_conv, p_C, p_n, p_m, p_mconv, s_S, s_conv, s_C, s_n, s_m, s_mconv)
```

```python
import os
import numpy as np
from contextlib import ExitStack
import concourse.bass as bass
import concourse.mybir as mybir
from concourse.bass_utils import run_bass_kernel_spmd

F32 = mybir.dt.float32
BF16 = mybir.dt.bfloat16
I32 = mybir.dt.int32
U32 = mybir.dt.uint32
AF = mybir.ActivationFunctionType
ALU = mybir.AluOpType
AX = mybir.AxisListType


class Res:
    __slots__ = ("name", "writer", "readers", "dsem", "dcnt", "dim")

    def __init__(self, name):
        self.name = name
        self.writer = None
        self.readers = {}
        self.dsem = None
        self.dcnt = 0
        self.dim = None


class Buf:
    def __init__(self, t, name):
        self.t = t
        self.r = Res(name)

    def __getitem__(self, k):
        return self.t[k]


class Sched:
    ENGS = ("pe", "act", "dve", "pool", "sp")

    def __init__(self, nc, es):
        self.nc = nc
        self.es = es
        self.sems = {}
        for e in self.ENGS[:4]:
            self.sems[e] = es.enter_context(nc.semaphore("s_" + e))
        self.cnt = {e: 0 for e in self.ENGS}
        self.seen = {e: {} for e in self.ENGS}
        self.vc = {}
        self.prog = {e: [] for e in self.ENGS}
        self.nres = 0
        self.out_deps = []
        self.dma_counts = {}

    def _need(self, eng, dep, waits):
        if dep is None:
            return
        dim, c = dep
        se = self.seen[eng]
        if se.get(dim, 0) >= c:
            return
        if dim == eng and eng == "pe":
            return
        if waits.get(dim, 0) < c:
            waits[dim] = c
        for k, v in self.vc[(dim, c)].items():
            if se.get(k, 0) < v:
                se[k] = v

    def _deps(self, eng, reads, writes):
        waits = {}
        for b in reads:
            self._need(eng, b.r.writer, waits)
        for b in writes:
            self._need(eng, b.r.writer, waits)
            for d, c in list(b.r.readers.items()):
                self._need(eng, (d, c), waits)
        return waits

    def op(self, eng, fn, reads=(), writes=()):
        waits = self._deps(eng, reads, writes)
        n = self.cnt[eng] + 1
        self.cnt[eng] = n
        snap = dict(self.seen[eng])
        snap[eng] = n
        self.vc[(eng, n)] = snap
        for b in reads:
            if b.r.readers.get(eng, 0) < n:
                b.r.readers[eng] = n
        for b in writes:
            b.r.writer = (eng, n)
            b.r.readers = {}
        self.prog[eng].append((list(waits.items()), fn, eng, 1))

    def dma(self, q, fn, reads=(), writes=(), semres=None, is_output=False):
        waits = self._deps(q, reads, writes)
        R = semres.r
        if R.dsem is None:
            self.nres += 1
            R.dim = "D%d_%s" % (self.nres, R.name)
            R.dsem = self.es.enter_context(self.nc.semaphore(R.dim))
            self.sems[R.dim] = R.dsem
        R.dcnt += 16
        c = R.dcnt
        dim = R.dim
        snap = dict(self.seen[q])
        snap[dim] = c
        self.vc[(dim, c)] = snap
        self.dma_counts[dim] = c
        for b in reads:
            if b.r.readers.get(dim, 0) < c:
                b.r.readers[dim] = c
        for b in writes:
            b.r.writer = (dim, c)
            b.r.readers = {}
        self.prog[q].append((list(waits.items()), fn, dim, 16))
        if is_output:
            self.out_deps.append((dim, c))
        return (dim, c)

    def barrier(self):
        targets = [(e, self.cnt[e]) for e in self.ENGS[:4] if self.cnt[e] > 0]
        targets += list(self.dma_counts.items())
        for eng in self.ENGS:
            waits = {}
            for dep in targets:
                self._need(eng, dep, waits)
            self.prog[eng].append((list(waits.items()), None, None, 0))

    def emit(self):
        nc = self.nc
        fw = {}
        for d, c in self.out_deps:
            if fw.get(d, 0) < c:
                fw[d] = c
        self.prog["sp"].append((list(fw.items()), None, None, 0))
        sems = self.sems
        prog = self.prog

        def mk(name):
            def body(e):
                for waits, fn, dim, inc in prog[name]:
                    for d, c in waits:
                        e.wait_ge(sems[d], c)
                    if fn is not None:
                        fn(e).then_inc(sems[dim], inc)
            return body

        with nc.Block() as block:
            block.tensor(mk("pe"))
            block.scalar(mk("act"))
            block.vector(mk("dve"))
            block.gpsimd(mk("pool"))
            block.sync(mk("sp"))


class Builder:
    def __init__(self, nc, es):
        self.nc = nc
        self.es = es
        self.S = Sched(nc, es)
        self.aes = es

    def sb(self, name, shape, dt=F32):
        t = self.aes.enter_context(self.nc.sbuf_tensor("sb_" + name, list(shape), dt))
        return Buf(t, name)

    def ps(self, name, shape=(128, 512), dt=F32):
        t = self.es.enter_context(self.nc.psum_tensor("ps_" + name, list(shape), dt))
        return Buf(t, name)

    def dram_in(self, name, shape, dt=F32):
        return self.nc.dram_tensor(name, list(shape), dt, kind="ExternalInput")

    def dram_out(self, name, shape, dt=F32):
        return self.nc.dram_tensor(name, list(shape), dt, kind="ExternalOutput")

    def mm(self, out, lhsT, rhs, start=True, stop=True, reads=(), writes=()):
        self.S.op("pe", lambda e: e.matmul(out, lhsT, rhs, start=start, stop=stop), reads, writes)

    def tr(self, out, in_, ident, reads=(), writes=()):
        self.S.op("pe", lambda e: e.transpose(out, in_, ident), reads, writes)

    def act(self, out, in_, func, reads=(), writes=(), **kw):
        self.S.op("act", lambda e: e.activation(out, in_, func, **kw), reads, writes)

    def tt(self, eng, out, in0, in1, op, reads=(), writes=()):
        self.S.op(eng, lambda e: e.tensor_tensor(out, in0, in1, op), reads, writes)

    def ts(self, eng, out, in0, s1, s2, op0, op1=None, reads=(), writes=(), accum_out=None):
        if op1 is None:
            self.S.op(eng, lambda e: e.tensor_scalar(out, in0, s1, None, op0), reads, writes)
        elif accum_out is None:
            self.S.op(eng, lambda e: e.tensor_scalar(out, in0, s1, s2, op0, op1), reads, writes)
        else:
            self.S.op(eng, lambda e: e.tensor_scalar(out, in0, s1, s2, op0, op1, accum_out=accum_out), reads, writes)

    def stt(self, out, in0, scalar, in1, op0, op1, reads=(), writes=(), accum_out=None):
        if accum_out is None:
            self.S.op("dve", lambda e: e.scalar_tensor_tensor(out, in0, scalar, in1, op0, op1), reads, writes)
        else:
            self.S.op("dve", lambda e: e.scalar_tensor_tensor(out, in0, scalar, in1, op0, op1, accum_out=accum_out), reads, writes)

    def cp(self, eng, out, in_, reads=(), writes=()):
        if eng == "act":
            self.S.op("act", lambda e: e.copy(out, in_), reads, writes)
        else:
            self.S.op(eng, lambda e: e.tensor_copy(out, in_), reads, writes)

    def memset(self, eng, ap, val, writes=()):
        self.S.op(eng, lambda e: e.memset(ap, val), (), writes)

    def load(self, out, in_, buf, q="sp", reads=()):
        self.S.dma(q, lambda e: e.dma_start(out=out, in_=in_), reads=reads, writes=(buf,), semres=buf)

    def store(self, out, in_, buf, q="sp", is_output=True):
        self.S.dma(q, lambda e: e.dma_start(out=out, in_=in_), reads=(buf,), writes=(), semres=buf, is_output=is_output)


def sap(t, p0, pn, f0, dims):
    base = t[:]
    F = base.ap[0][0]
    return bass.AP(t, p0 * F + f0, [[F, pn]] + [list(d) for d in dims])

T_P = 2064
T_S = 128
T = T_P + T_S
NB = [(0, 512), (512, 512), (1024, 512), (1536, 512), (2048, 144)]
EPS = 1e-6
NEG = -1.0e30


def _maskset(nseg, seglen):
    r = np.arange(128)
    seg = r // seglen
    same = seg[:, None] == seg[None, :]
    le = r[:, None] <= r[None, :]
    lt = r[:, None] < r[None, :]
    MinclT = (same & le).astype(np.float32)
    SegB = same.astype(np.float32)
    MstrictNeg = -((same & lt.T).astype(np.float32))
    MstrictTNeg = -((same & lt).astype(np.float32))
    MaddIncl = np.where(same & le.T, 0.0, NEG).astype(np.float32)
    MaddSeg = np.where(same, 0.0, NEG).astype(np.float32)
    segrows = (seg[:, None] == np.arange(nseg)[None, :]).astype(np.float32)
    segfirst = segrows * ((r % seglen) == 0)[:, None].astype(np.float32)
    segT = np.broadcast_to(segrows.T[None], (128, nseg, 128)).reshape(128, nseg * 128)
    return np.ascontiguousarray(np.concatenate(
        [MinclT, SegB, MstrictNeg, MstrictTNeg, MaddIncl, MaddSeg, segrows, segfirst, segT], axis=1).astype(np.float32))


class MaskSet:
    def __init__(self, buf, nseg):
        self.buf = buf
        self.nseg = nseg
        t = buf.t
        self.MinclT = t[:, 0:128]
        self.SegB = t[:, 128:256]
        self.MstrictNeg = t[:, 256:384]
        self.MstrictTNeg = t[:, 384:512]
        self.MaddIncl = t[:, 512:640]
        self.MaddSeg = t[:, 640:768]
        self.segrows = t[:, 768:768 + nseg]
        self.segfirst = t[:, 768 + nseg:768 + 2 * nseg]
        o = 768 + 2 * nseg
        self.segT = sap(t, 0, 128, o, [[128, nseg], [1, 128]])
        self.segrows_b = sap(t, 0, 128, 768, [[1, nseg], [0, 128]])


KCUT = int(os.environ.get("KCUT", "99"))


class _Stop(Exception):
    pass


def build_program(stage=9, ntiles=33):
    nc = bass.Bass("TRN2", target_bir_lowering=False)
    es = ExitStack()
    with es:
        B = Builder(nc, es)
        S = B.S
        d = {}

        def din(name, shape, dt=F32):
            d[name] = B.dram_in(name, shape, dt).ap()
            return d[name]

        def dout(name, shape, dt=F32):
            d[name] = B.dram_out(name, shape, dt).ap()
            return d[name]

        din("xall", [T, 1024]); din("w_in", [1024, 6160]); din("w_branch", [1024, 1024]); din("w_out", [1024, 1024])
        din("nffn_bc", [128, 1024]); din("nfin_bc", [128, 1024]); din("wq", [1024, 2048]); din("subk", [128, 16, 128])
        din("peer_uv", [16384, 2048]); din("iota16", [128, 16])
        din("ident", [128, 128]); din("ones", [128, 128]); din("maskP", [128, 1028]); din("maskS", [128, 2848])
        din("rowmask0", [128, 2]); din("nmix", [128, 8]);
        din("convg", [128, 48]); din("convm", [128, 32]); din("rowp", [128, 24]); din("gn_g", [128, 128])
        din("gn_m", [128, 768]); din("sS", [16, 4, 128, 128]); din("sconv_g", [48, 1536]); din("sC", [16, 4, 128, 128])
        din("sn", [16, 512]); din("m0s", [128, 4]); din("sconv_m", [48, 1024])
        dout("y_p", [2048, 1024]); dout("y_s", [128, 1024]); dout("p_S", [4, 128, 128]); dout("p_conv", [3, 1536])
        dout("p_C", [4, 128, 128]); dout("p_n", [4, 128]); dout("p_m", [1, 4]); dout("p_mconv", [3, 1024])
        dout("s_S", [16, 4, 128, 128]); dout("s_conv", [48, 1536]); dout("s_C", [16, 4, 128, 128]); dout("s_n", [16, 512])
        dout("s_m", [16, 4]); dout("s_mconv", [48, 1024])
        oT_d = nc.dram_tensor("oT_d", [1024, T], BF16, kind="Internal").ap()
        oT_res = Buf(None, "oT_d")

        ident = B.sb("ident", [128, 128]); ones = B.sb("ones", [128, 128]); cst = B.sb("cst", [128, 8])
        ps = [B.ps("b%d" % i) for i in range(8)]
        xnT = B.sb("xnT", [128, 8, T], BF16)
        es_mix = ExitStack()
        B.aes = es_mix
        mP = B.sb("maskP", [128, 1028]); mS = B.sb("maskS", [128, 2848]); rm0 = B.sb("rm0", [128, 2])
        nmix = B.sb("nmix", [128, 8]); convg = B.sb("convg", [128, 48]); convm = B.sb("convm", [128, 32])
        rowp = B.sb("rowp", [128, 24]); gn_g = B.sb("gn_g", [128, 128]); gn_m = B.sb("gn_m", [128, 768])
        m0s = B.sb("m0s", [128, 4]); prm = B.sb("prm", [128, 24])
        for b_, nm in ((ident, "ident"), (ones, "ones"), (mP, "maskP"), (mS, "maskS"), (rm0, "rowmask0"), (nmix, "nmix"),
                       (convg, "convg"), (convm, "convm"), (rowp, "rowp"), (gn_g, "gn_g"), (gn_m, "gn_m"), (m0s, "m0s")):
            B.load(b_[:], d[nm], b_)
        MP = MaskSet(mP, 2)
        MS = MaskSet(mS, 16)
        B.memset("pool", cst[:, 0:1], EPS, writes=[cst])
        B.memset("pool", cst[:, 1:2], 1.0, writes=[cst])
        B.memset("pool", cst[:, 2:3], -0.5 * float(np.log(128.0)), writes=[cst])
        B.memset("pool", cst[:, 3:4], 0.0, writes=[cst])
        c_eps, c_one, c_lnq, c_zero = cst[:, 0:1], cst[:, 1:2], cst[:, 2:3], cst[:, 3:4]
        rp3 = rowp.t[:].rearrange("p (t f) -> p t f", f=4)
        pr3 = prm.t[:].rearrange("p (t f) -> p t f", f=4)
        B.act(pr3[:, :, 0:1], rp3[:, :, 0:1], AF.Exp, reads=[rowp], writes=[prm])
        B.ts("dve", pr3[:, :, 0:1], pr3[:, :, 0:1], -1.0, None, ALU.mult, reads=[prm], writes=[prm])
        B.cp("dve", pr3[:, :, 1:3], rp3[:, :, 1:3], reads=[rowp], writes=[prm])
        B.ts("dve", pr3[:, :, 3:4], rp3[:, :, 3:4], -1.0, None, ALU.mult, reads=[rowp], writes=[prm])

        EX1 = B.sb("EX1", [128, 2048]); EX2 = B.sb("EX2", [128, 2048]); EX3 = B.sb("EX3", [128, 2048])
        st = [B.sb("st%d" % i, [128, 4]) for i in range(2)]

        class _V:
            def __init__(self, buf, c0):
                self.buf = buf; self.c0 = c0
            def ap(self, r0, r1, a, b):
                return self.buf.t[r0:r1, self.c0 + a:self.c0 + b]
        xtv = [_V(EX1, 0), _V(EX1, 1024)]
        xsv = [_V(EX2, 0), _V(EX2, 1024)]
        junkv = _V(EX3, 0)

        def rstd_col(stb, rows, width):
            B.act(stb[0:rows, 1:2], stb[0:rows, 0:1], AF.Ln, reads=[stb, cst], writes=[stb], scale=1.0 / width, bias=c_eps[0:rows])
            B.act(stb[0:rows, 2:3], stb[0:rows, 1:2], AF.Exp, reads=[stb], writes=[stb], scale=-0.5)

        ntile = (T + 127) // 128
        for i in range(ntile):
            t0 = i * 128
            rows = min(128, T - t0)
            sl = i % 2
            B.load(xtv[sl].ap(0, rows, 0, 1024), d["xall"][t0:t0 + rows, :], EX1)
            B.act(junkv.ap(0, rows, 0, 1024), xtv[sl].ap(0, rows, 0, 1024), AF.Square, reads=[EX1], writes=[EX3, st[sl]], accum_out=st[sl][0:rows, 0:1])
            rstd_col(st[sl], rows, 1024)
            B.act(xsv[sl].ap(0, rows, 0, 1024), xtv[sl].ap(0, rows, 0, 1024), AF.Copy, reads=[EX1, st[sl]], writes=[EX2], scale=st[sl][0:rows, 2:3])
            for c in range(8):
                pb = ps[2 * sl + c // 4]
                B.tr(pb[:, (c % 4) * 128:(c % 4) * 128 + rows], xsv[sl].ap(0, rows, c * 128, (c + 1) * 128), ident[0:rows, 0:rows],
                     reads=[EX2, ident], writes=[pb])
            for h in range(2):
                pb = ps[2 * sl + h]
                B.tt("dve", xnT[:, 4 * h:4 * h + 4, t0:t0 + rows], sap(pb.t, 0, 128, 0, [[128, 4], [1, rows]]),
                     sap(nmix.t, 0, 128, 4 * h, [[1, 4], [0, rows]]), ALU.mult, reads=[pb, nmix], writes=[xnT])

        if stage <= 1:
            S.emit()
            es_mix.close()
            return nc
        Wb = B.sb("Wb", [128, 8, 1032], BF16)
        Wst = [B.sb("Wst0", [128, 1032])]
        E = B.sb("E", [128, 2243])
        Y = B.sb("Y", [128, T])
        Rt = B.sb("Rt", [128, 512])
        CO = [B.sb("CO%d" % i, [48, 256]) for i in range(2)]
        qT = B.sb("qT", [128, 2, T]); kT = B.sb("kT", [128, 2, T]); vT = B.sb("vT", [128, 2, T])
        B.memset("pool", E[:, 0:3], 0.0, writes=[E])
        gc = B.sb("gc", [128, 48])
        D2 = B.sb("D2", [128, 256]); XX = B.sb("XX", [128, 256]); DD = B.sb("DD", [128, 256]); Xb = B.sb("Xb", [128, 128])
        t1 = B.sb("t1", [128, 128]); t2 = B.sb("t2", [128, 128]); t3 = B.sb("t3", [128, 128]); t4 = B.sb("t4", [128, 128])
        Mk = [B.sb("Mk%d" % i, [128, 128]) for i in range(2)]; Nk = [B.sb("Nk%d" % i, [128, 128]) for i in range(2)]
        IM = B.sb("IM", [128, 128]); TT = [B.sb("TT%d" % i, [128, 128]) for i in range(2)]
        qkmT = B.sb("qkmT", [128, 128]); QT = B.sb("QT", [128, 128]); KT = B.sb("KT", [128, 128]); VT = B.sb("VT", [128, 128])
        bv = B.sb("bv", [128, 128]); bgk = B.sb("bgk", [128, 128]); ke = B.sb("ke", [128, 128])
        dsb = B.sb("dsb", [128, 128]); o1 = B.sb("o1", [128, 128]); ob = B.sb("ob", [128, 128]); on = B.sb("on", [128, 128])
        ez = B.sb("ez", [128, 128]); zsb = B.sb("zsb", [128, 128]); og = B.sb("og", [128, 128]); jk2 = B.sb("jk2", [128, 128])
        obf = [B.sb("obf%d" % i, [128, 128], BF16) for i in range(2)]
        GEt = B.sb("GEt", [128, 16]); gebc = B.sb("gebc", [128, 16])
        ST = B.sb("ST", [128, 2048]); Sio = B.sb("Sio", [128, 2048])
        nT = B.sb("nT", [128, 16]); mrow = B.sb("mrow", [128, 1]); nio = B.sb("nio", [16, 512]); ntm = B.sb("ntm", [16, 128])
        msm = B.sb("msm", [128, 4])

        def G(i):
            return gc[:, i:i + 1]

        obf_ctr = [0]
        co_ctr = [0]
        wst_ctr = [0]

        def load_wpair(base, j):
            rngs = [(base + 2 * j * 128, 256, 0), (base + 512 + 2 * j * 128, 256, 256), (base + 1024 + 2 * j * 128, 256, 512),
                    (base + 1536, 8, 768), (base + 1544 + 2 * j * 128, 256, 776)]
            for c in range(8):
                ws = Wst[0]
                wst_ctr[0] += 1
                for (c0, n, o) in rngs:
                    B.load(ws[:, o:o + n], d["w_in"][c * 128:(c + 1) * 128, c0:c0 + n], ws)
                eng = ("pool", "dve")[c % 2]
                B.cp(eng, Wb[:, c, :], ws[:, :], reads=[ws], writes=[Wb])

        def project(e, evac):
            for bi, (t0, n) in enumerate(NB):
                pb = ps[bi % 4]
                for c in range(8):
                    B.mm(pb[:, 0:n], Wb[:, c, e * 128:(e + 1) * 128], xnT[:, c, t0:t0 + n], start=(c == 0), stop=(c == 7),
                         reads=[Wb, xnT], writes=[pb])
                evac(bi, pb, t0, n)

        Esamp = sap(E.t, 0, 128, 2067, [[11, 16], [1, 11]])

        def evac_to_E(bi, pb, t0, n):
            if bi < 4:
                B.cp("act", E[:, 3 + t0:3 + t0 + n], pb[:, 0:n], reads=[pb], writes=[E])
            else:
                B.cp("act", E[:, 3 + 2048:3 + 2064], pb[:, 0:16], reads=[pb], writes=[E])
                B.cp("act", sap(E.t, 0, 128, 2067 + 3, [[11, 16], [1, 8]]), sap(pb.t, 0, 128, 16, [[8, 16], [1, 8]]), reads=[pb], writes=[E])

        def conv_chunk(convw, cc, hist_cols, out_state_p, out_state_s, ch0):
            pb = ps[4]
            B.tr(pb[:, 0:48], Sio[0:48, hist_cols:hist_cols + 128], ident[0:48, 0:48], reads=[Sio, ident], writes=[pb])
            B.cp("act", sap(E.t, 0, 128, 2067, [[11, 16], [1, 3]]), sap(pb.t, 0, 128, 0, [[3, 16], [1, 3]]), reads=[pb], writes=[E])
            w = lambda jj: convw[:, cc * 4 + jj:cc * 4 + jj + 1]
            B.ts("dve", Y[:, 0:T_P], E[:, 0:T_P], w(0), None, ALU.mult, reads=[E, convw], writes=[Y])
            for jj in range(1, 4):
                B.stt(Y[:, 0:T_P], E[:, jj:jj + T_P], w(jj), Y[:, 0:T_P], ALU.mult, ALU.add, reads=[E, convw, Y], writes=[Y])
            Ys = sap(Y.t, 0, 128, T_P, [[8, 16], [1, 8]])
            B.ts("dve", Ys, sap(E.t, 0, 128, 2067, [[11, 16], [1, 8]]), w(0), None, ALU.mult, reads=[E, convw], writes=[Y])
            for jj in range(1, 4):
                B.stt(Ys, sap(E.t, 0, 128, 2067 + jj, [[11, 16], [1, 8]]), w(jj), Ys, ALU.mult, ALU.add, reads=[E, convw, Y], writes=[Y])
            co = CO[co_ctr[0] % 2]
            co_ctr[0] += 1
            pb2 = ps[5]
            B.tr(pb2[0:3, 0:128], E[:, 2064:2067], ident[:, :], reads=[E, ident], writes=[pb2])
            B.cp("pool", sap(Rt.t, 0, 128, 0, [[3, 16], [1, 3]]), sap(E.t, 0, 128, 2067 + 8, [[11, 16], [1, 3]]), reads=[E], writes=[Rt])
            B.tr(pb2[0:48, 128:256], Rt[:, 0:48], ident[:, :], reads=[Rt, ident], writes=[pb2])
            B.cp("act", co[0:3, 0:128], pb2[0:3, 0:128], reads=[pb2], writes=[co])
            B.cp("act", co[0:48, 128:256], pb2[0:48, 128:256], reads=[pb2], writes=[co])
            B.store(out_state_p[0:3, ch0:ch0 + 128], co[0:3, 0:128], co)
            B.store(out_state_s[0:48, ch0:ch0 + 128], co[0:48, 128:256], co)

        def l2norm_to(dst3, hh, lnbias):
            B.tt("pool", E[:, 0:T], Y[:, :], Y[:, :], ALU.mult, reads=[Y], writes=[E])
            for bi, (t0, n) in enumerate(NB):
                pb = ps[bi % 4]
                B.mm(pb[:, 0:n], ones[:, :], E[:, t0:t0 + n], reads=[ones, E], writes=[pb])
                B.act(Rt[:, 0:n], pb[:, 0:n], AF.Ln, reads=[pb, cst], writes=[Rt], bias=c_eps)
                B.act(Rt[:, 0:n], Rt[:, 0:n], AF.Exp, reads=[Rt, cst], writes=[Rt], scale=-0.5, bias=lnbias)
                B.tt("dve", dst3[:, hh, t0:t0 + n], Y[:, t0:t0 + n], Rt[:, 0:n], ALU.mult, reads=[Y, Rt], writes=[dst_buf[0]])
            B.memset("pool", E[:, 0:3], 0.0, writes=[E])

        dst_buf = [None]

        def run_tile(kind, M, groups, qa, ka, va, ty, hl_of, STv, first_chunk, m_levels, out_rows, mx, gain_ap):
            nseg = M.nseg
            PR = lambda f: prm[:, ty * 4 + f:ty * 4 + f + 1]
            pg = ps[0]
            for (r0, nr, tk, hg, hl) in groups:
                for c in range(8):
                    B.mm(pg[r0:r0 + nr, 0:2], xnT[:, c, tk:tk + nr], sap(Wb.t, 0, 128, c * 1032 + 768 + hg, [[4, 2]]), start=(c == 0), stop=(c == 7),
                         reads=[xnT, Wb], writes=[pg])
            if KCUT <= 1:
                raise _Stop()
            pZ = ps[7]
            for (r0, nr, tk, hg, hl) in groups:
                for c in range(8):
                    B.mm(pZ[r0:r0 + nr, 0:128], xnT[:, c, tk:tk + nr], Wb[:, c, 776 + hl * 128:776 + (hl + 1) * 128], start=(c == 0), stop=(c == 7),
                         reads=[xnT, Wb], writes=[pZ])
            B.cp("dve", zsb[:, :], pZ[:, 0:128], reads=[pZ], writes=[zsb])
            B.act(ez[:, :], zsb[:, :], AF.Exp, reads=[zsb], writes=[ez], scale=-1.0)
            B.ts("pool", ez[:, :], ez[:, :], 1.0, None, ALU.add, reads=[ez], writes=[ez])
            S.op("dve", lambda e: e.reciprocal(ez[:, :], ez[:, :]), [ez], [ez])
            if kind == "gdn":
                B.act(G(0), pg[:, 0:1], AF.Exp, reads=[pg, prm], writes=[gc], bias=PR(1))
                B.act(G(1), G(0), AF.Ln, reads=[gc, cst], writes=[gc], bias=c_one)
                if first_chunk:
                    B.ts("dve", G(2), G(1), PR(0), rm0[:, 0:1], ALU.mult, ALU.mult, reads=[gc, prm, rm0], writes=[gc])
                else:
                    B.ts("dve", G(2), G(1), PR(0), None, ALU.mult, reads=[gc, prm], writes=[gc])
                B.act(G(3), pg[:, 1:2], AF.Exp, reads=[pg], writes=[gc], scale=-1.0)
                B.ts("dve", G(3), G(3), 1.0, None, ALU.add, reads=[gc], writes=[gc])
                S.op("dve", lambda e: e.reciprocal(G(4), G(3)), [gc], [gc])
                if first_chunk:
                    B.ts("dve", G(4), G(4), rm0[:, 0:1], None, ALU.mult, reads=[gc, rm0], writes=[gc])
                lg = G(2)
            else:
                B.act(G(5), pg[:, 0:1], AF.Identity, reads=[pg, prm], writes=[gc], bias=PR(2))
                if first_chunk:
                    B.ts("dve", G(5), G(5), rm0[:, 1:2], None, ALU.add, reads=[gc, rm0], writes=[gc])
                B.act(G(0), pg[:, 1:2], AF.Exp, reads=[pg, prm], writes=[gc], scale=-1.0, bias=PR(3))
                B.act(G(1), G(0), AF.Ln, reads=[gc, cst], writes=[gc], bias=c_one)
                if first_chunk:
                    B.ts("dve", G(2), G(1), -1.0, rm0[:, 0:1], ALU.mult, ALU.mult, reads=[gc, rm0], writes=[gc])
                else:
                    B.ts("dve", G(2), G(1), -1.0, None, ALU.mult, reads=[gc], writes=[gc])
                lg = G(2)
            if KCUT <= 2:
                raise _Stop()
            B.mm(pg[:, 2:3], M.MinclT, lg, reads=[M.buf, gc], writes=[pg])
            B.mm(pg[:, 3:4], M.SegB, lg, reads=[M.buf, gc], writes=[pg])
            B.cp("act", gc[:, 6:8], pg[:, 2:4], reads=[pg], writes=[gc])
            if KCUT <= 3:
                raise _Stop()
            pR = ps[6]
            B.cp("pool", KT[:, :], ka, reads=[kT], writes=[KT])
            B.cp("act", VT[:, :], va, reads=[vT], writes=[VT])
            B.cp("pool", QT[:, :], qa, reads=[qT], writes=[QT])
            B.tr(pR[:, 0:128], KT[:, :], ident[:, :], reads=[KT, ident], writes=[pR])
            B.tr(pR[:, 128:256], VT[:, :], ident[:, :], reads=[VT, ident], writes=[pR])
            B.tt("pool", sap(EX2.t, 0, 128, 0, [[128, nseg], [1, 128]]), sap(QT.t, 0, 128, 0, [[0, nseg], [1, 128]]), M.segT, ALU.mult,
                 reads=[QT, M.buf], writes=[EX2])
            if KCUT <= 4:
                raise _Stop()
            pB = ps[1]
            pKQ = ps[2]
            p5 = ps[5]
            if kind == "gdn":
                B.act(G(8), G(6), AF.Exp, reads=[gc], writes=[gc])
                B.act(G(9), G(7), AF.Exp, reads=[gc], writes=[gc])
                B.tt("dve", G(10), G(7), G(6), ALU.subtract, reads=[gc], writes=[gc])
                B.act(G(10), G(10), AF.Exp, reads=[gc], writes=[gc])
                B.tt("dve", G(11), G(4), G(8), ALU.mult, reads=[gc], writes=[gc])
                B.ts("pool", D2[:, 0:128], ident[:, :], G(6), None, ALU.mult, reads=[ident, gc], writes=[D2])
                B.ts("pool", D2[:, 128:256], ident[:, :], G(4), None, ALU.mult, reads=[ident, gc], writes=[D2])
                B.mm(pB[:, 0:256], ones[:, :], D2[:, :], reads=[ones, D2], writes=[pB])
                B.ts("dve", Xb[:, :], pB[:, 0:128], G(6), None, ALU.subtract, reads=[pB, gc], writes=[Xb])
                B.ts("dve", XX[:, 0:128], Xb[:, :], 0.0, -1.0, ALU.max, ALU.mult, reads=[Xb], writes=[XX])
                B.ts("pool", XX[:, 128:256], Xb[:, :], 0.0, None, ALU.min, reads=[Xb], writes=[XX])
                B.act(DD[:, :], XX[:, :], AF.Exp, reads=[XX], writes=[DD])
                if KCUT <= 5:
                    raise _Stop()
                B.mm(pKQ[:, 0:128], KT[:, :], KT[:, :], reads=[KT], writes=[pKQ])
                B.mm(pKQ[:, 128:256], KT[:, :], QT[:, :], reads=[KT, QT], writes=[pKQ])
                B.tt("dve", t1[:, :], pKQ[:, 0:128], DD[:, 0:128], ALU.mult, reads=[pKQ, DD], writes=[t1])
                B.stt(Mk[0][:, :], t1[:, :], G(4), M.MstrictNeg, ALU.mult, ALU.mult, reads=[t1, gc, M.buf], writes=[Mk[0]])
                B.tt("dve", t2[:, :], pKQ[:, 0:128], DD[:, 128:256], ALU.mult, reads=[pKQ, DD], writes=[t2])
                B.tt("dve", t3[:, :], t2[:, :], pB[:, 128:256], ALU.mult, reads=[t2, pB], writes=[t3])
                B.tt("pool", Nk[0][:, :], t3[:, :], M.MstrictTNeg, ALU.mult, reads=[t3, M.buf], writes=[Nk[0]])
                B.tt("dve", t4[:, :], pKQ[:, 128:256], DD[:, 128:256], ALU.mult, reads=[pKQ, DD], writes=[t4])
                B.tt("pool", qkmT[:, :], t4[:, :], M.MinclT, ALU.mult, reads=[t4, M.buf], writes=[qkmT])
                B.tt("pool", TT[0][:, :], Nk[0][:, :], ident[:, :], ALU.add, reads=[Nk[0], ident], writes=[TT[0]])
                if KCUT <= 6:
                    raise _Stop()
                pc = ps[3]
                pt = ps[4]
                cur = 0
                for k in range(1, m_levels + 1):
                    nxt = 1 - cur
                    B.mm(pc[:, 0:128], Nk[cur][:, :], Mk[cur][:, :], reads=[Nk[cur], Mk[cur]], writes=[pc])
                    if k < m_levels:
                        B.mm(pc[:, 128:256], Mk[cur][:, :], Nk[cur][:, :], reads=[Nk[cur], Mk[cur]], writes=[pc])
                    B.tt("dve", IM[:, :], pc[:, 0:128], ident[:, :], ALU.add, reads=[pc, ident], writes=[IM])
                    if k < m_levels:
                        B.cp("dve", Mk[nxt][:, :], pc[:, 0:128], reads=[pc], writes=[Mk[nxt]])
                        B.cp("dve", Nk[nxt][:, :], pc[:, 128:256], reads=[pc], writes=[Nk[nxt]])
                    B.mm(pt[:, 0:128], IM[:, :], TT[cur][:, :], reads=[IM, TT[cur]], writes=[pt])
                    B.cp("dve", TT[nxt][:, :], pt[:, 0:128], reads=[pt], writes=[TT[nxt]])
                    cur = nxt
                if KCUT <= 7:
                    raise _Stop()
                TTf = TT[cur]
                B.ts("dve", bv[:, :], pR[:, 128:256], G(4), None, ALU.mult, reads=[pR, gc], writes=[bv])
                B.ts("dve", bgk[:, :], pR[:, 0:128], G(11), None, ALU.mult, reads=[pR, gc], writes=[bgk])
                B.ts("dve", ke[:, :], pR[:, 0:128], G(10), None, ALU.mult, reads=[pR, gc], writes=[ke])
                B.mm(p5[:, 0:128], bgk[:, :], TTf[:, :], reads=[bgk, TTf], writes=[p5])
                B.stt(sap(EX1.t, 0, 128, 0, [[128, nseg], [1, 128]]), sap(p5.t, 0, 128, 0, [[0, nseg], [1, 128]]), -1.0, M.segT,
                      ALU.mult, ALU.mult, reads=[p5, M.buf], writes=[EX1])
                if KCUT <= 8:
                    raise _Stop()
                B.mm(p5[:, 128:256], TTf[:, :], bv[:, :], start=True, stop=False, reads=[TTf, bv], writes=[p5])
                for b in range(nseg):
                    B.mm(p5[:, 128:256], EX1[:, b * 128:(b + 1) * 128], STv[:, b * 128:(b + 1) * 128], start=False, stop=(b == nseg - 1),
                         reads=[EX1, ST], writes=[p5])
                for b in range(nseg):
                    B.mm(p5[:, 256:384], EX2[:, b * 128:(b + 1) * 128], STv[:, b * 128:(b + 1) * 128], start=(b == 0), stop=(b == nseg - 1),
                         reads=[EX2, ST], writes=[p5])
                B.cp("dve", dsb[:, :], p5[:, 128:256], reads=[p5], writes=[dsb])
                B.tt("dve", sap(EX3.t, 0, 128, 0, [[128, nseg], [1, 128]]), sap(p5.t, 0, 128, 128, [[0, nseg], [1, 128]]), M.segrows_b,
                     ALU.mult, reads=[p5, M.buf], writes=[EX3])
                B.mm(p5[:, 384:512], qkmT[:, :], dsb[:, :], reads=[qkmT, dsb], writes=[p5])
                B.ts("dve", o1[:, :], p5[:, 256:384], G(8), None, ALU.mult, reads=[p5, gc], writes=[o1])
                B.tt("dve", ob[:, :], o1[:, :], p5[:, 384:512], ALU.add, reads=[o1, p5], writes=[ob])
                lhs_state = ke
                dec_col = G(9)
            else:
                B.tt("dve", G(12), G(5), G(6), ALU.subtract, reads=[gc], writes=[gc])
                B.tt("dve", G(13), G(12), G(7), ALU.add, reads=[gc], writes=[gc])
                B.ts("pool", D2[:, 0:128], ident[:, :], G(12), None, ALU.mult, reads=[ident, gc], writes=[D2])
                B.ts("pool", D2[:, 128:256], ident[:, :], G(13), None, ALU.mult, reads=[ident, gc], writes=[D2])
                B.mm(pB[:, 0:256], ones[:, :], D2[:, :], reads=[ones, D2], writes=[pB])
                B.stt(Xb[:, :], pB[:, 0:128], G(6), M.MaddIncl, ALU.add, ALU.add, reads=[pB, gc, M.buf], writes=[Xb])
                S.op("dve", lambda e: e.tensor_reduce(G(14), Xb[:, :], AX.X, ALU.max), [Xb], [gc])
                B.tt("dve", t1[:, :], pB[:, 128:256], M.MaddSeg, ALU.add, reads=[pB, M.buf], writes=[t1])
                S.op("dve", lambda e: e.tensor_reduce(G(15), t1[:, :], AX.X, ALU.max), [t1], [gc])
                B.tt("dve", G(16), G(6), mrow[:, 0:1], ALU.add, reads=[gc, mrow], writes=[gc])
                B.tt("dve", G(17), G(16), G(14), ALU.max, reads=[gc], writes=[gc])
                B.ts("dve", G(18), G(17), -1.0, None, ALU.mult, reads=[gc], writes=[gc])
                B.tt("dve", G(19), G(16), G(17), ALU.subtract, reads=[gc], writes=[gc])
                B.act(G(19), G(19), AF.Exp, reads=[gc], writes=[gc])
                B.act(t2[:, :], Xb[:, :], AF.Exp, reads=[Xb, gc], writes=[t2], bias=G(18))
                B.mm(pKQ[:, 0:128], QT[:, :], KT[:, :], reads=[KT, QT], writes=[pKQ])
                B.stt(t3[:, :], t2[:, :], 1.0, pKQ[:, 0:128], ALU.mult, ALU.mult, reads=[t2, pKQ], writes=[t3, gc], accum_out=G(20))
                pc = ps[3]
                B.tr(pc[:, 0:128], t3[:, :], ident[:, :], reads=[t3, ident], writes=[pc])
                B.cp("dve", t4[:, :], pc[:, 0:128], reads=[pc], writes=[t4])
                B.cp("dve", bv[:, :], pR[:, 128:256], reads=[pR], writes=[bv])
                B.tt("dve", G(21), G(7), mrow[:, 0:1], ALU.add, reads=[gc, mrow], writes=[gc])
                B.tt("dve", G(22), G(21), G(15), ALU.max, reads=[gc], writes=[gc])
                B.tt("dve", G(23), G(21), G(22), ALU.subtract, reads=[gc], writes=[gc])
                B.act(G(23), G(23), AF.Exp, reads=[gc], writes=[gc])
                B.tt("dve", G(24), G(13), G(22), ALU.subtract, reads=[gc], writes=[gc])
                B.act(G(24), G(24), AF.Exp, reads=[gc], writes=[gc])
                B.ts("dve", ke[:, :], pR[:, 0:128], G(24), None, ALU.mult, reads=[pR, gc], writes=[ke])
                for b in range(nseg):
                    B.mm(p5[:, 256:384], EX2[:, b * 128:(b + 1) * 128], STv[:, b * 128:(b + 1) * 128], start=(b == 0), stop=(b == nseg - 1),
                         reads=[EX2, ST], writes=[p5])
                B.mm(p5[:, 384:512], t4[:, :], bv[:, :], reads=[t4, bv], writes=[p5])
                B.mm(pg[:, 32:32 + nseg], QT[:, :], nT[:, 0:nseg], reads=[QT, nT], writes=[pg])
                B.stt(jk2[:, 0:nseg], pg[:, 32:32 + nseg], 1.0, M.segrows, ALU.mult, ALU.mult, reads=[pg, M.buf], writes=[jk2, gc], accum_out=G(25))
                B.stt(G(26), G(25), G(19), G(20), ALU.mult, ALU.add, reads=[gc], writes=[gc])
                B.ts("dve", G(29), G(26), -1.0, None, ALU.mult, reads=[gc], writes=[gc])
                B.tt("dve", G(26), G(26), G(29), ALU.max, reads=[gc], writes=[gc])
                B.act(G(27), G(18), AF.Exp, reads=[gc], writes=[gc])
                B.tt("dve", G(26), G(26), G(27), ALU.max, reads=[gc], writes=[gc])
                S.op("dve", lambda e: e.reciprocal(G(28), G(26)), [gc], [gc])
                B.ts("dve", o1[:, :], p5[:, 256:384], G(19), None, ALU.mult, reads=[p5, gc], writes=[o1])
                B.tt("dve", ob[:, :], o1[:, :], p5[:, 384:512], ALU.add, reads=[o1, p5], writes=[ob])
                B.ts("dve", ob[:, :], ob[:, :], G(28), None, ALU.mult, reads=[ob, gc], writes=[ob])
                B.tt("dve", sap(EX3.t, 0, 128, 0, [[128, nseg], [1, 128]]), sap(bv.t, 0, 128, 0, [[0, nseg], [1, 128]]), M.segrows_b,
                     ALU.mult, reads=[bv, M.buf], writes=[EX3])
                lhs_state = ke
                dec_col = G(23)
            if KCUT <= 9:
                raise _Stop()
            B.ts("pool", GEt[:, 0:nseg], M.segfirst, dec_col, None, ALU.mult, reads=[M.buf, gc], writes=[GEt])
            B.mm(pg[:, 8:8 + nseg], ones[:, :], GEt[:, 0:nseg], reads=[ones, GEt], writes=[pg])
            B.cp("act", gebc[:, 0:nseg], pg[:, 8:8 + nseg], reads=[pg], writes=[gebc])
            B.tt("pool", sap(ST.t, 0, 128, 0, [[128, nseg], [1, 128]]), sap(ST.t, 0, 128, 0, [[128, nseg], [1, 128]]),
                 sap(gebc.t, 0, 128, 0, [[1, nseg], [0, 128]]), ALU.mult, reads=[ST, gebc], writes=[ST])
            B.act(jk2[:, :], ob[:, :], AF.Square, reads=[ob], writes=[jk2, gc], accum_out=G(30))
            B.act(G(31), G(30), AF.Ln, reads=[gc, cst], writes=[gc], scale=1.0 / 128, bias=c_eps)
            B.act(G(32), G(31), AF.Exp, reads=[gc], writes=[gc], scale=-0.5)
            B.stt(on[:, :], ob[:, :], G(32), gain_ap, ALU.mult, ALU.mult, reads=[ob, gc, gn_g, gn_m], writes=[on])
            if KCUT <= 10:
                raise _Stop()
            pZ = ps[7]
            if kind == "gdn":
                B.tt("dve", og[:, :], on[:, :], zsb[:, :], ALU.mult, reads=[on, zsb], writes=[og])
                B.tt("pool", og[:, :], og[:, :], ez[:, :], ALU.mult, reads=[og, ez], writes=[og])
            else:
                B.tt("pool", og[:, :], on[:, :], ez[:, :], ALU.mult, reads=[on, ez], writes=[og])
            B.tr(pZ[:, 128:256], og[:, :], ident[:, :], reads=[og, ident], writes=[pZ])
            of = obf[obf_ctr[0] % 2]
            obf_ctr[0] += 1
            B.cp("dve", of[:, :], pZ[:, 128:256], reads=[pZ], writes=[of])
            for (r0, nv, tk, hg) in (out_rows if os.environ.get("KSKIP", "") != "store" else []):
                S.dma("sp", (lambda o_, i_: (lambda e: e.dma_start(out=o_, in_=i_)))(oT_d[(mx * 4 + hg) * 128:(mx * 4 + hg + 1) * 128, tk:tk + nv], of[:, r0:r0 + nv]),
                      reads=[of], writes=[oT_res], semres=of)
            if KCUT <= 11:
                raise _Stop()
            ncols = nseg * 128
            banks = [ps[1]] if nseg == 2 else [ps[1], ps[2], ps[3], ps[4]]
            for bi in range((ncols + 511) // 512):
                w = min(512, ncols - bi * 512)
                B.mm(banks[bi][:, 0:w], lhs_state[:, :], EX3[:, bi * 512:bi * 512 + w], reads=[lhs_state, EX3], writes=[banks[bi]])
            for bi in range((ncols + 511) // 512):
                w = min(512, ncols - bi * 512)
                B.tt("dve", ST[:, bi * 512:bi * 512 + w], ST[:, bi * 512:bi * 512 + w], banks[bi][:, 0:w], ALU.add, reads=[ST, banks[bi]], writes=[ST])
            if kind == "mls":
                B.mm(pg[:, 64:64 + nseg], ke[:, :], M.segrows, reads=[ke, M.buf], writes=[pg])
                B.tt("dve", nT[:, 0:nseg], nT[:, 0:nseg], gebc[:, 0:nseg], ALU.mult, reads=[nT, gebc], writes=[nT])
                B.tt("dve", nT[:, 0:nseg], nT[:, 0:nseg], pg[:, 64:64 + nseg], ALU.add, reads=[nT, pg], writes=[nT])
                B.cp("dve", mrow[:, 0:1], G(22), reads=[gc], writes=[mrow])

        def store_states(nseg, dstS_fn):
            for b in range(nseg):
                pb = ps[1 + (b // 4) % 4]
                B.tr(pb[:, (b % 4) * 128:(b % 4 + 1) * 128], ST[:, b * 128:(b + 1) * 128], ident[:, :], reads=[ST, ident], writes=[pb])
                if b % 4 == 3 or b == nseg - 1:
                    g0 = (b // 4) * 4
                    w = (b - g0 + 1) * 128
                    B.cp("act", Sio[:, g0 * 128:g0 * 128 + w], pb[:, 0:w], reads=[pb], writes=[Sio])
            dstS_fn()

        try:
          for mx, kind in enumerate(("gdn", "mls")):
              base = 0 if kind == "gdn" else 2056
              convw = convg if kind == "gdn" else convm
              nconv = 1536 if kind == "gdn" else 1024
              out_p = d["p_conv"] if kind == "gdn" else d["p_mconv"]
              out_s = d["s_conv"] if kind == "gdn" else d["s_mconv"]
              for j in range(2):
                  B.load(Sio[0:48, 0:nconv], d["sconv_g" if kind == "gdn" else "sconv_m"], Sio)
                  load_wpair(base, j)
                  for e in range(6):
                      which = e // 2
                      hh = e % 2
                      dstb = (qT, kT, vT)[which]
                      dst_buf[0] = dstb
                      if kind == "mls" and which == 2:
                          def ev(bi, pb, t0, n, dstb=dstb, hh=hh):
                              B.cp("act", dstb[:, hh, t0:t0 + n], pb[:, 0:n], reads=[pb], writes=[dstb])
                          project(e, ev)
                          continue
                      project(e, evac_to_E)
                      cc = which * 4 + 2 * j + hh
                      conv_chunk(convw, cc, cc * 128, out_p, out_s, cc * 128)
                      if kind == "gdn" and which < 2:
                          B.act(Y[:, :], Y[:, :], AF.Silu, reads=[Y], writes=[Y])
                          l2norm_to(dstb, hh, c_lnq if which == 0 else c_zero)
                      elif kind == "mls" and which == 1:
                          B.act(Y[:, :], Y[:, :], AF.Silu, reads=[Y], writes=[Y])
                          B.ts("pool", dstb[:, hh, :], Y[:, :], float(128.0 ** -0.5), None, ALU.mult, reads=[Y], writes=[dstb])
                      else:
                          B.act(dstb[:, hh, :], Y[:, :], AF.Silu, reads=[Y], writes=[dstb])
                  if stage <= 2:
                      raise _Stop()
                  STp = ST
                  B.memset("pool", ST[:, 0:256], 0.0, writes=[ST])
                  if kind == "mls":
                      B.memset("pool", nT[:, 0:2], 0.0, writes=[nT])
                      B.memset("pool", mrow[:, :], 0.0, writes=[mrow])
                  for ci in range(ntiles):
                      if ci == 0:
                          tk, nv = 0, 16
                      else:
                          tk, nv = 16 + 64 * (ci - 1), 64
                      groups = [(0, 64, tk, 2 * j, 0), (64, 64, tk, 2 * j + 1, 1)]
                      qa = qT[:, 0:2, tk:tk + 64]; ka = kT[:, 0:2, tk:tk + 64]; va = vT[:, 0:2, tk:tk + 64]
                      outr = [(0, nv, tk, 2 * j), (64, nv, tk, 2 * j + 1)]
                      gain_ap = gn_g[:, :] if kind == "gdn" else gn_m[:, j * 128:(j + 1) * 128]
                      run_tile(kind, MP, groups, qa, ka, va, j, None, ST, ci == 0, 5, outr, mx, gain_ap)
                  if stage <= 3:
                      raise _Stop()
                  def dstP(j=j, kind=kind):
                      dS = d["p_S"] if kind == "gdn" else d["p_C"]
                      B.store(dS[2 * j:2 * j + 2].rearrange("h v k -> v h k"), sap(Sio.t, 0, 128, 0, [[128, 2], [1, 128]]), Sio)
                  store_states(2, dstP)
                  if kind == "mls":
                      pb = ps[5]
                      B.tr(pb[0:2, 0:128], nT[:, 0:2], ident[:, :], reads=[nT, ident], writes=[pb])
                      B.cp("act", ntm[0:2, :], pb[0:2, 0:128], reads=[pb], writes=[ntm])
                      B.store(d["p_n"][2 * j:2 * j + 2, :], ntm[0:2, :], ntm)
                      B.store(d["p_m"][0:1, 2 * j:2 * j + 1], mrow[0:1, 0:1], mrow)
                      B.store(d["p_m"][0:1, 2 * j + 1:2 * j + 2], mrow[64:65, 0:1], mrow)
                  if stage <= 4:
                      raise _Stop()
                  for hl in range(2):
                      hg = 2 * j + hl
                      dS_in = d["sS"] if kind == "gdn" else d["sC"]
                      B.load(sap(Sio.t, 0, 128, 0, [[128, 16], [1, 128]]), dS_in[:, hg].rearrange("b v k -> v b k"), Sio)
                      for b in range(16):
                          pb = ps[1 + (b // 4) % 4]
                          B.tr(pb[:, (b % 4) * 128:(b % 4 + 1) * 128], Sio[:, b * 128:(b + 1) * 128], ident[:, :], reads=[Sio, ident], writes=[pb])
                          if b % 4 == 3:
                              g0 = (b // 4) * 4
                              B.cp("act", ST[:, g0 * 128:g0 * 128 + 512], pb[:, 0:512], reads=[pb], writes=[ST])
                      if kind == "mls":
                          B.load(nio[:, :], d["sn"], nio)
                          pb = ps[5]
                          B.tr(pb[:, 0:16], nio[0:16, hg * 128:(hg + 1) * 128], ident[0:16, 0:16], reads=[nio, ident], writes=[pb])
                          B.cp("act", nT[:, 0:16], pb[:, 0:16], reads=[pb], writes=[nT])
                          B.cp("dve", mrow[:, 0:1], m0s[:, hg:hg + 1], reads=[m0s], writes=[mrow])
                      groups = [(0, 128, T_P, hg, hl)]
                      qa = qT[:, hl, T_P:T]; ka = kT[:, hl, T_P:T]; va = vT[:, hl, T_P:T]
                      outr = [(0, 128, T_P, hg)]
                      gain_ap = gn_g[:, :] if kind == "gdn" else gn_m[:, (2 + hg) * 128:(3 + hg) * 128]
                      run_tile(kind, MS, groups, qa, ka, va, 2 + hg, None, ST, False, 2, outr, mx, gain_ap)
                      def dstS(hg=hg, kind=kind):
                          dS = d["s_S"] if kind == "gdn" else d["s_C"]
                          B.store(dS[:, hg].rearrange("b v k -> v b k"), sap(Sio.t, 0, 128, 0, [[128, 16], [1, 128]]), Sio)
                      store_states(16, dstS)
                      if kind == "mls":
                          pb = ps[5]
                          B.tr(pb[0:16, 0:128], nT[:, 0:16], ident[:, :], reads=[nT, ident], writes=[pb])
                          B.cp("act", ntm[0:16, :], pb[0:16, 0:128], reads=[pb], writes=[ntm])
                          B.store(d["s_n"][:, hg * 128:(hg + 1) * 128], ntm[0:16, :], ntm)
                          B.cp("dve", msm[:, hg:hg + 1], mrow[:, 0:1], reads=[mrow], writes=[msm])
              if kind == "mls":
                  B.store(d["s_m"], bass.AP(msm.t, 0, [[32, 16], [1, 4]]), msm)

        except _Stop:
            pass
        if stage <= 5:
            S.emit()
            es_mix.close()
            return nc

        def dma(out, in_, reads, writes, semres, q="sp", is_output=False):
            S.dma(q, lambda e: e.dma_start(out=out, in_=in_), reads=reads, writes=writes, semres=semres, is_output=is_output)

        tiles = [(16 + 128 * i, 128 * i) for i in range(16)] + [(T_P, 2048)]
        if ntiles < 33:
            tiles = tiles[:2]

        S.barrier()
        es_mix.close()
        es_mg = ExitStack()
        B.aes = es_mg
        Hd = nc.dram_tensor("H_d", [2176, 1024], F32, kind="Internal").ap()
        Hd_res = Buf(None, "H_d")
        Wg = B.sb("Wg", [128, 8, 2048], BF16); wbr = B.sb("wbr", [128, 8, 1024], BF16); wout = B.sb("wout", [128, 8, 1024], BF16)
        wstg = [B.sb("wstg%d" % i, [128, 2048]) for i in range(2)]
        wc = 0
        for (dst, src, c0, ncol) in ((Wg, "w_in", 4112, 2048), (wbr, "w_branch", 0, 1024), (wout, "w_out", 0, 1024)):
            for k in range(8):
                ws = wstg[wc % 2]
                B.load(ws[:, 0:ncol], d[src][k * 128:(k + 1) * 128, c0:c0 + ncol], ws)
                B.cp(("pool", "dve")[wc % 2], dst[:, k, :], ws[:, 0:ncol], reads=[ws], writes=[dst])
                wc += 1
        OT = [B.sb("OT%d" % i, [128, 8, 128], BF16) for i in range(2)]
        XT = [B.sb("XT%d" % i, [128, 1024]) for i in range(2)]
        sgm = [B.sb("sgm%d" % i, [128, 256]) for i in range(2)]
        mm1 = B.sb("mm1", [128, 128]); mm2 = B.sb("mm2", [128, 128])
        mgT = [B.sb("mgT%d" % i, [128, 8, 128], BF16) for i in range(2)]
        Hs = [B.sb("Hs%d" % i, [128, 1024]) for i in range(2)]
        oT_v = oT_d.rearrange("(c p) t -> p c t", p=128)
        uvb = nc.dram_tensor("uvb", [16384, 2048], BF16, kind="Internal").ap()
        uvb_res = Buf(None, "uvb")
        NCV = 3
        cvi = [B.sb("cvi%d" % i, [128, 2048]) for i in range(NCV)]
        cvo = [B.sb("cvo%d" % i, [128, 2048], BF16) for i in range(NCV)]
        cv_state = [0, 0]

        def cv_load():
            r = cv_state[0]
            if r < 128:
                a = cvi[r % NCV]
                dma(a[:, :], d["peer_uv"][r * 128:(r + 1) * 128, :], [], [a], a, q="act")
                cv_state[0] += 1

        def cv_step():
            r = cv_state[1]
            if r >= 128:
                return
            a = cvi[r % NCV]; b_ = cvo[r % NCV]
            B.cp("act", b_[:, :], a[:, :], reads=[a], writes=[b_])
            dma(uvb[r * 128:(r + 1) * 128, :], b_[:, :], [b_], [uvb_res], b_, q="act")
            cv_state[1] += 1
            cv_load()

        for _ in range(NCV):
            cv_load()
        for ti, (tok0, hrow) in enumerate(tiles):
            sl = ti % 2
            for _ in range(8):
                cv_step()
            dma(OT[sl][:, :, :], oT_v[:, :, tok0:tok0 + 128], [oT_res], [OT[sl]], OT[sl])
            B.load(XT[sl][:, :], d["xall"][tok0:tok0 + 128, :], XT[sl])
            for c in range(8):
                gA = ps[c % 2]
                yB = ps[2 + c % 2]
                for g in range(2):
                    for k in range(8):
                        B.mm(gA[:, g * 128:(g + 1) * 128], Wg[:, k, g * 1024 + c * 128:g * 1024 + (c + 1) * 128], xnT[:, k, tok0:tok0 + 128],
                             start=(k == 0), stop=(k == 7), reads=[Wg, xnT], writes=[gA])
                for g in range(2):
                    for h in range(4):
                        B.mm(yB[:, g * 128:(g + 1) * 128], wbr[:, g * 4 + h, c * 128:(c + 1) * 128], OT[sl][:, g * 4 + h, :],
                             start=(h == 0), stop=(h == 3), reads=[wbr, OT[sl]], writes=[yB])
                sgb = sgm[c % 2]
                B.act(sgb[:, :], gA[:, 0:256], AF.Sigmoid, reads=[gA], writes=[sgb])
                B.tt("dve", mm1[:, :], sgb[:, 0:128], yB[:, 0:128], ALU.mult, reads=[sgb, yB], writes=[mm1])
                B.tt("dve", mm2[:, :], sgb[:, 128:256], yB[:, 128:256], ALU.mult, reads=[sgb, yB], writes=[mm2])
                B.tt("pool", mgT[sl][:, c, :], mm1[:, :], mm2[:, :], ALU.add, reads=[mm1, mm2], writes=[mgT[sl]])
            for half in range(2):
                pb = ps[4 + half]
                for c in range(8):
                    B.mm(pb[:, 0:512], mgT[sl][:, c, :], wout[:, c, half * 512:(half + 1) * 512], start=(c == 0), stop=(c == 7),
                         reads=[mgT[sl], wout], writes=[pb])
                B.tt("dve", Hs[sl][:, half * 512:(half + 1) * 512], XT[sl][:, half * 512:(half + 1) * 512], pb[:, 0:512], ALU.add,
                     reads=[XT[sl], pb], writes=[Hs[sl]])
            dma(Hd[hrow:hrow + 128, :], Hs[sl][:, :], [Hs[sl]], [Hd_res], Hs[sl])
        while cv_state[1] < 128:
            cv_step()
        if stage <= 6:
            S.emit()
            es_mg.close()
            return nc

        S.barrier()
        es_mg.close()
        es_pe = ExitStack()
        B.aes = es_pe
        wq = B.sb("wq", [128, 8, 2048], BF16); skT = B.sb("skT", [128, 16, 128], BF16)
        nffn = B.sb("nffn", [128, 1024]); nfin = B.sb("nfin", [128, 1024]); identb = B.sb("identb", [128, 128], BF16); iota16 = B.sb("iota16", [128, 16])
        es_tmp = ExitStack()
        B.aes = es_tmp
        skn = B.sb("skn", [128, 16, 128])
        wst2 = [B.sb("wst2_%d" % i, [128, 2048]) for i in range(2)]
        B.load(nffn[:, :], d["nffn_bc"], nffn)
        B.load(nfin[:, :], d["nfin_bc"], nfin)
        B.load(iota16[:, :], d["iota16"], iota16)
        B.cp("dve", identb[:, :], ident[:, :], reads=[ident], writes=[identb])
        for k in range(8):
            ws = wst2[k % 2]
            B.load(ws[:, :], d["wq"][k * 128:(k + 1) * 128, :], ws)
            B.cp(("pool", "dve")[k % 2], wq[:, k, :], ws[:, :], reads=[ws], writes=[wq])
        B.load(skn[:, :, :], d["subk"], skn)
        for hp in range(16):
            pb = ps[hp // 4]
            B.tr(pb[:, (hp % 4) * 128:(hp % 4 + 1) * 128], skn[:, hp, :], ident[:, :], reads=[skn, ident], writes=[pb])
            if hp % 4 == 3:
                g0 = hp - 3
                B.cp("dve", skT[:, g0:g0 + 4, :], sap(pb.t, 0, 128, 0, [[128, 4], [1, 128]]), reads=[pb], writes=[skT])
        S.barrier()
        es_tmp.close()
        B.aes = es_pe
        NBUF = 10
        Hs2 = [B.sb("Hs2_%d" % i, [128, 1024]) for i in range(2)]
        XN2 = [B.sb("XN2_%d" % i, [128, 1024]) for i in range(2)]
        XN2b = [B.sb("XN2b_%d" % i, [128, 1024], BF16) for i in range(2)]
        EIi = [B.sb("EIi%d" % i, [128, 128], I32) for i in range(2)]
        gate = [B.sb("gate%d" % i, [128, 128]) for i in range(2)]
        jkA = B.sb("jkA", [128, 256]); jkB = B.sb("jkB", [128, 1024]); jkC = [B.sb("jkC%d" % i, [128, 1024], BF16) for i in range(2)]; stp = B.sb("stp", [128, 4]); stq = B.sb("stq", [128, 4])
        x2T = B.sb("x2T", [128, 8, 128], BF16); qTs = B.sb("qTs", [128, 16, 128], BF16)
        Ssb = B.sb("Ssb", [128, 2048]); SC2 = B.sb("SC2", [128, 2048]); cand = B.sb("cand", [128, 2048]); eq4 = B.sb("eq4", [128, 2048])
        Vv = B.sb("Vv", [128, 256]); Iu = B.sb("Iu", [128, 256], U32); If = B.sb("If", [128, 256])
        TS = B.sb("TS", [128, 128]); negm = B.sb("negm", [128, 8]); Zs = B.sb("Zs", [128, 8]); rZ = B.sb("rZ", [128, 8])
        EI = B.sb("EI", [128, 128]); egt = B.sb("egt", [128, 128])
        PU = B.sb("PU", [128, 128], U32); PA = B.sb("PA", [128, 128], U32); PB = B.sb("PB", [128, 128], U32)
        Af = B.sb("Af", [128, 128]); Bf = B.sb("Bf", [128, 128]); sel0 = B.sb("sel0", [128, 128]); sel1 = B.sb("sel1", [128, 128])
        AVs = [B.sb("AVs%d" % i, [128, 1]) for i in range(4)]
        GAs = [B.sb("GAs%d" % i, [128, 1]) for i in range(4)]
        Wts = [B.sb("Wts%d" % i, [128, 1]) for i in range(4)]
        dgb = [B.sb("dgb%d" % i, [128, 128], BF16) for i in range(4)]
        UG = [B.sb("UG%d" % i, [128, 2048], BF16) for i in range(NBUF)]
        YO = [B.sb("YO%d" % i, [128, 1024]) for i in range(2)]

        def subs(buf, n):
            return [Buf(buf.t, "%s_s%d" % (buf.r.name, i)) for i in range(n)]
        VvS = subs(Vv, 16); IuS = subs(Iu, 16); SC2S = subs(SC2, 16); TSS = subs(TS, 8); PUS = subs(PU, 8)

        def routing(ti):
            tok0, hrow = tiles[ti]
            sl = ti % 2
            H = Hs2[sl]
            X2 = XN2[sl]
            dma(H[:, :], Hd[hrow:hrow + 128, :], [Hd_res], [H], H)
            B.act(SC2[:, 0:1024], H[:, :], AF.Square, reads=[H], writes=SC2S[0:8] + [stp], accum_out=stp[:, 0:1])
            rstd_col(stp, 128, 1024)
            B.stt(X2[:, :], H[:, :], stp[:, 2:3], nffn[:, :], ALU.mult, ALU.mult, reads=[H, stp, nffn], writes=[X2])
            B.cp("pool", XN2b[sl][:, :], X2[:, :], reads=[X2], writes=[XN2b[sl]])
            yield
            for c in range(8):
                pb = ps[c // 4]
                B.tr(pb[:, (c % 4) * 128:(c % 4 + 1) * 128], X2[:, c * 128:(c + 1) * 128], ident[:, :], reads=[X2, ident], writes=[pb])
            for hh in range(2):
                B.cp("act", x2T[:, 4 * hh:4 * hh + 4, :], sap(ps[hh].t, 0, 128, 0, [[128, 4], [1, 128]]), reads=[ps[hh]], writes=[x2T])
                yield
            for g in range(4):
                pb = ps[2 + g % 2]
                for q4 in range(4):
                    hp = 4 * g + q4
                    for k in range(8):
                        B.mm(pb[:, q4 * 128:(q4 + 1) * 128], wq[:, k, hp * 128:(hp + 1) * 128], x2T[:, k, :], start=(k == 0), stop=(k == 7),
                             reads=[wq, x2T], writes=[pb])
                B.cp("act", qTs[:, 4 * g:4 * g + 4, :], sap(pb.t, 0, 128, 0, [[128, 4], [1, 128]]), reads=[pb], writes=[qTs])
                yield
            for hp in range(16):
                pb = ps[4 + (hp // 4) % 2]
                B.mm(pb[:, (hp % 4) * 128:(hp % 4 + 1) * 128], qTs[:, hp, :], skT[:, hp, :], reads=[qTs, skT], writes=[pb])
                if hp % 4 == 3:
                    g = hp // 4
                    B.cp("act", Ssb[:, g * 512:(g + 1) * 512], pb[:, 0:512], reads=[pb], writes=[Ssb])
                    yield
            yield "front_done"
            for g4 in range(4):
                hps = [4 * g4 + q for q in range(4)]
                seg = lambda hp: Ssb[:, hp * 128:(hp + 1) * 128]
                seg2 = lambda hp: SC2[:, hp * 128:(hp + 1) * 128]
                v0 = lambda hp: Vv[:, hp * 16:hp * 16 + 8]
                v1 = lambda hp: Vv[:, hp * 16 + 8:hp * 16 + 16]
                i0_ = lambda hp: Iu[:, hp * 16:hp * 16 + 8]
                i1_ = lambda hp: Iu[:, hp * 16 + 8:hp * 16 + 16]
                for hp in hps:
                    S.op("dve", (lambda a, b: (lambda e: e.max(a, b)))(v0(hp), seg(hp)), [Ssb], [VvS[hp]])
                yield
                for hp in hps:
                    S.op("dve", (lambda a, b, c_: (lambda e: e.max_index(a, b, c_)))(i0_(hp), v0(hp), seg(hp)), [Ssb, VvS[hp]], [IuS[hp]])
                yield
                for hp in hps:
                    S.op("dve", (lambda a, b, c_: (lambda e: e.match_replace(a, b, c_, NEG)))(seg2(hp), v0(hp), seg(hp)), [Ssb, VvS[hp]], [SC2S[hp]])
                yield
                for hp in hps:
                    S.op("dve", (lambda a, b: (lambda e: e.max(a, b)))(v1(hp), seg2(hp)), [SC2S[hp]], [VvS[hp]])
                yield
                for hp in hps:
                    S.op("dve", (lambda a, b, c_: (lambda e: e.max_index(a, b, c_)))(i1_(hp), v1(hp), seg2(hp)), [SC2S[hp], VvS[hp]], [IuS[hp]])
                yield
            B.cp("dve", If[:, :], Iu[:, :], reads=IuS, writes=[If])
            B.ts("dve", sap(If.t, 0, 128, 0, [[32, 8], [1, 16]]), sap(If.t, 0, 128, 0, [[32, 8], [1, 16]]), 128.0, None, ALU.mult, reads=[If], writes=[If])
            c4 = sap(cand.t, 0, 128, 0, [[256, 8], [16, 16], [1, 16]])
            B.tt("dve", c4, sap(Vv.t, 0, 128, 0, [[32, 8], [1, 16], [0, 16]]), sap(Vv.t, 0, 128, 16, [[32, 8], [0, 16], [1, 16]]), ALU.add,
                 reads=VvS, writes=[cand])
            yield
            for g4 in range(2):
                hs = [4 * g4 + q for q in range(4)]
                cs = lambda h: cand[:, h * 256:(h + 1) * 256]
                cs2 = lambda h: SC2[:, h * 256:(h + 1) * 256]
                t0_ = lambda h: TS[:, h * 16:h * 16 + 8]
                t1_ = lambda h: TS[:, h * 16 + 8:h * 16 + 16]
                p0_ = lambda h: PU[:, h * 16:h * 16 + 8]
                p1_ = lambda h: PU[:, h * 16 + 8:h * 16 + 16]
                for h in hs:
                    S.op("dve", (lambda a, b: (lambda e: e.max(a, b)))(t0_(h), cs(h)), [cand], [TSS[h]])
                yield
                for h in hs:
                    S.op("dve", (lambda a, b, c_: (lambda e: e.max_index(a, b, c_)))(p0_(h), t0_(h), cs(h)), [cand, TSS[h]], [PUS[h]])
                yield
                for h in hs:
                    S.op("dve", (lambda a, b, c_: (lambda e: e.match_replace(a, b, c_, NEG)))(cs2(h), t0_(h), cs(h)), [cand, TSS[h]], [SC2S[2 * h], SC2S[2 * h + 1]])
                yield
                for h in hs:
                    S.op("dve", (lambda a, b: (lambda e: e.max(a, b)))(t1_(h), cs2(h)), [SC2S[2 * h], SC2S[2 * h + 1]], [TSS[h]])
                yield
                for h in hs:
                    S.op("dve", (lambda a, b, c_: (lambda e: e.max_index(a, b, c_)))(p1_(h), t1_(h), cs2(h)), [SC2S[2 * h], SC2S[2 * h + 1], TSS[h]], [PUS[h]])
                yield
            S.op("dve", lambda e: e.tensor_single_scalar(PA[:, :], PU[:, :], 4, ALU.logical_shift_right), PUS, [PA])
            S.op("dve", lambda e: e.tensor_single_scalar(PB[:, :], PU[:, :], 15, ALU.bitwise_and), PUS, [PB])
            B.cp("dve", Af[:, :], PA[:, :], reads=[PA], writes=[Af])
            B.cp("dve", Bf[:, :], PB[:, :], reads=[PB], writes=[Bf])
            yield
            for (src, off, dst) in ((Af, 0, sel0), (Bf, 16, sel1)):
                B.tt("dve", sap(eq4.t, 0, 128, 0, [[16, 128], [1, 16]]), sap(src.t, 0, 128, 0, [[1, 128], [0, 16]]),
                     sap(iota16.t, 0, 128, 0, [[0, 128], [1, 16]]), ALU.is_equal, reads=[src, iota16], writes=[eq4])
                yield
                B.tt("dve", sap(eq4.t, 0, 128, 0, [[256, 8], [16, 16], [1, 16]]), sap(eq4.t, 0, 128, 0, [[256, 8], [16, 16], [1, 16]]),
                     sap(If.t, 0, 128, off, [[32, 8], [0, 16], [1, 16]]), ALU.mult, reads=[eq4, If], writes=[eq4])
                yield
                S.op("dve", (lambda d_: (lambda e: e.tensor_reduce(d_[:, :], sap(eq4.t, 0, 128, 0, [[16, 128], [1, 16]]), AX.X, ALU.add)))(dst),
                     [eq4], [dst])
                yield
            B.tt("dve", EI[:, :], sel0[:, :], sel1[:, :], ALU.add, reads=[sel0, sel1], writes=[EI])
            B.ts("dve", EI[:, :], EI[:, :], 16383.0, 0.0, ALU.min, ALU.max, reads=[EI], writes=[EI])
            B.cp("dve", EIi[sl][:, :], EI[:, :], reads=[EI], writes=[EIi[sl]])
            B.ts("dve", negm[:, :], sap(TS.t, 0, 128, 0, [[16, 8]]), -1.0, None, ALU.mult, reads=TSS, writes=[negm])
            yield
            for h in range(8):
                B.act(egt[:, h * 16:(h + 1) * 16], TS[:, h * 16:(h + 1) * 16], AF.Exp, reads=[TSS[h], negm], writes=[egt, Zs],
                      bias=negm[:, h:h + 1], accum_out=Zs[:, h:h + 1])
            S.op("dve", lambda e: e.reciprocal(rZ[:, :], Zs[:, :]), [Zs], [rZ])
            B.tt("dve", sap(gate[sl].t, 0, 128, 0, [[16, 8], [1, 16]]), sap(egt.t, 0, 128, 0, [[16, 8], [1, 16]]), sap(rZ.t, 0, 128, 0, [[1, 8], [0, 16]]),
                 ALU.mult, reads=[egt, rZ], writes=[gate[sl]])
            yield

        def drain(gen, n=None, until=None):
            if gen is None:
                return None
            try:
                if until is not None:
                    while next(gen) != until:
                        pass
                elif n is None:
                    while True:
                        next(gen)
                else:
                    for _ in range(n):
                        next(gen)
            except StopIteration:
                return None
            return gen

        ug_ctr = [0]
        nt_ = len(tiles)
        drain(routing(0))
        for ti in range(nt_):
            tok0, hrow = tiles[ti]
            sl = ti % 2
            H = Hs2[sl]
            nxt = routing(ti + 1) if ti + 1 < nt_ else None
            nxt = drain(nxt, until="front_done")
            pa = [ps[6], ps[7]]
            for col in range(128):
                ug = UG[ug_ctr[0] % NBUF]
                ug_ctr[0] += 1
                S.dma("pool", (lambda ug, col, sl: (lambda e: e.indirect_dma_start(out=ug[:, :], out_offset=None, in_=uvb[:, :],
                      in_offset=bass.IndirectOffsetOnAxis(ap=EIi[sl][:, col:col + 1], axis=0))))(ug, col, sl),
                      reads=[EIi[sl], uvb_res], writes=[ug], semres=ug)
                a4 = col % 4
                B.stt(jkC[col % 2][:, :], ug[:, 0:1024], 1.0, XN2b[sl][:, :], ALU.mult, ALU.mult, reads=[ug, XN2b[sl]], writes=[jkC[col % 2], AVs[a4]], accum_out=AVs[a4][:, 0:1])
                B.act(GAs[a4][:, :], AVs[a4][:, :], AF.Gelu, reads=[AVs[a4]], writes=[GAs[a4]])
                dg = dgb[a4]
                B.act(Wts[a4][:, :], GAs[a4][:, :], AF.Copy, reads=[GAs[a4], gate[sl]], writes=[Wts[a4]], scale=gate[sl][:, col:col + 1])
                B.act(dg[:, :], identb[:, :], AF.Copy, reads=[identb, Wts[a4]], writes=[dg], scale=Wts[a4][:, 0:1])
                for half in range(2):
                    B.mm(pa[half][:, 0:512], dg[:, :], ug[:, 1024 + half * 512:1024 + (half + 1) * 512], start=(col == 0), stop=(col == 127),
                         reads=[dg, ug], writes=[pa[half]])
                nxt = drain(nxt, 3)
            nxt = drain(nxt)
            yo = YO[sl]
            for half in range(2):
                B.tt("dve", yo[:, half * 512:(half + 1) * 512], H[:, half * 512:(half + 1) * 512], pa[half][:, 0:512], ALU.add,
                     reads=[H, pa[half]], writes=[yo])
            B.act(jkB[:, :], yo[:, :], AF.Square, reads=[yo], writes=[jkB, stq], accum_out=stq[:, 0:1])
            rstd_col(stq, 128, 1024)
            B.stt(yo[:, :], yo[:, :], stq[:, 2:3], nfin[:, :], ALU.mult, ALU.mult, reads=[yo, stq, nfin], writes=[yo])
            if ti < 16:
                B.store(d["y_p"][hrow:hrow + 128, :], yo[:, :], yo)
            else:
                B.store(d["y_s"][:, :], yo[:, :], yo)
        S.emit()
        es_pe.close()
    return nc


_PROG = {}


def _get_prog():
    if "nc" not in _PROG:
        import os
        _PROG["nc"] = build_program(int(os.environ.get("KSTAGE", "9")), int(os.environ.get("KNT", "33")))
    return _PROG["nc"]


def _core_inputs(inp, i, consts):
    f = lambda k: np.asarray(inp[k], dtype=np.float32)
    m = dict(consts)
    xs = f("x_sample")[16 * i:16 * i + 16].reshape(128, 1024)
    m["xall"] = np.ascontiguousarray(np.concatenate([f("meta_tokens"), f("x_prompt")[i], xs], axis=0))
    m["sS"] = np.ascontiguousarray(f("state_gdn_S")[0, 16 * i:16 * i + 16])
    m["sC"] = np.ascontiguousarray(f("state_mlstm_C")[0, 16 * i:16 * i + 16])
    m["sconv_g"] = np.ascontiguousarray(f("state_gdn_conv")[0, 16 * i:16 * i + 16].reshape(48, 1536))
    m["sconv_m"] = np.ascontiguousarray(f("state_mlstm_conv")[0, 16 * i:16 * i + 16].reshape(48, 1024))
    m["sn"] = np.ascontiguousarray(f("state_mlstm_n")[0, 16 * i:16 * i + 16].reshape(16, 512))
    m["m0s"] = np.ascontiguousarray(np.repeat(f("state_mlstm_m")[0, 16 * i:16 * i + 16], 8, axis=0))
    return m


def _consts(inp):
    f = lambda k: np.asarray(inp[k], dtype=np.float32)
    c = {}
    c["w_in"] = np.ascontiguousarray(f("w_in")[0])
    c["w_branch"] = np.ascontiguousarray(f("w_branch")[0])
    c["w_out"] = np.ascontiguousarray(f("w_out")[0])
    c["nffn_bc"] = np.ascontiguousarray(np.broadcast_to(f("norm_ffn")[0][None, :], (128, 1024)))
    c["nfin_bc"] = np.ascontiguousarray(np.broadcast_to(f("norm_final")[None, :], (128, 1024)))
    c["wq"] = np.ascontiguousarray(f("peer_w_query")[0])
    c["subk"] = np.ascontiguousarray(f("peer_sub_keys")[0].transpose(2, 0, 1, 3).reshape(128, 16, 128))
    c["iota16"] = np.ascontiguousarray(np.broadcast_to(np.arange(16, dtype=np.float32)[None, :], (128, 16)))
    c["peer_uv"] = np.ascontiguousarray(np.concatenate([f("peer_u")[0], f("peer_v")[0]], axis=1))
    c["ident"] = np.eye(128, dtype=np.float32)
    c["ones"] = np.ones((128, 128), np.float32)
    c["maskP"] = _maskset(2, 64)
    c["maskS"] = _maskset(16, 8)
    r = np.arange(128)
    valid = (r % 64) < 16
    c["rowmask0"] = np.ascontiguousarray(np.stack([valid.astype(np.float32), np.where(valid, 0.0, NEG).astype(np.float32)], axis=1))
    c["nmix"] = np.ascontiguousarray(f("norm_mix")[0].reshape(8, 128).T)
    c["convg"] = np.ascontiguousarray(f("conv_gdn")[0].T.reshape(12, 128, 4).transpose(1, 0, 2).reshape(128, 48))
    c["convm"] = np.ascontiguousarray(f("conv_mlstm")[0].T.reshape(8, 128, 4).transpose(1, 0, 2).reshape(128, 32))
    headof = np.zeros((6, 128), np.int64)
    for ty in range(2):
        headof[ty] = 2 * ty + (r // 64)
    for ty in range(2, 6):
        headof[ty] = ty - 2
    prs = [f("gdn_a_log")[0], f("gdn_dt_bias")[0], f("mlstm_i_bias")[0], f("mlstm_f_bias")[0]]
    rowp = np.zeros((128, 6, 4), np.float32)
    for ty in range(6):
        for k, p in enumerate(prs):
            rowp[:, ty, k] = p[headof[ty]]
    c["rowp"] = np.ascontiguousarray(rowp.reshape(128, 24))
    c["gn_g"] = np.ascontiguousarray(np.broadcast_to(f("gdn_out_norm")[0][None, :], (128, 128)))
    gm = f("mlstm_out_norm")[0]
    gn_m = np.zeros((128, 6, 128), np.float32)
    for ty in range(6):
        gn_m[:, ty, :] = gm[headof[ty]]
    c["gn_m"] = np.ascontiguousarray(gn_m.reshape(128, 768))
    return c


def kernel(**inputs):
    nc = _get_prog()
    consts = _consts(inputs)
    in_maps = [_core_inputs(inputs, i, consts) for i in range(8)]
    res = run_bass_kernel_spmd(nc, in_maps, core_ids=list(range(8)))
    R = res.results
    g = lambda k: [np.asarray(R[i][k], dtype=np.float32) for i in range(8)]
    y_p = np.stack(g("y_p"))
    y_s = np.concatenate([a.reshape(16, 8, 1024) for a in g("y_s")], 0)
    p_S = np.stack(g("p_S"))[None]
    p_conv = np.stack(g("p_conv"))[None]
    p_C = np.stack(g("p_C"))[None]
    p_n = np.stack(g("p_n"))[None]
    p_m = np.stack([a.reshape(4) for a in g("p_m")])[None]
    p_mconv = np.stack(g("p_mconv"))[None]
    s_S = np.concatenate(g("s_S"), 0)[None]
    s_conv = np.concatenate([a.reshape(16, 3, 1536) for a in g("s_conv")], 0)[None]
    s_C = np.concatenate(g("s_C"), 0)[None]
    s_n = np.concatenate([a.reshape(16, 4, 128) for a in g("s_n")], 0)[None]
    s_m = np.concatenate(g("s_m"), 0)[None]
    s_mconv = np.concatenate([a.reshape(16, 3, 1024) for a in g("s_mconv")], 0)[None]
    return (y_p, y_s, p_S, p_conv, p_C, p_n, p_m, p_mconv, s_S, s_conv, s_C, s_n, s_m, s_mconv)
```
